# Optimizing a Trainium2 kernel written in Bass

```python
import jax, jax.numpy as jnp
from jax import lax
import numpy as np

D_MODEL = 2048
BATCH = 8
SEQ = 2048
DEPTH = 1

MIX_W = D_MODEL
DN_HEADS = 8
DN_HEAD_DIM = MIX_W // (2 * DN_HEADS)
DN_W = DN_HEADS * DN_HEAD_DIM
SB_HEADS = 8
SB_HEAD_DIM = MIX_W // (2 * SB_HEADS)
SB_W = SB_HEADS * SB_HEAD_DIM
CONV_K = 4
DN_CHUNK = 64
SB_BLOCK = 128
D_FF = ((8 * D_MODEL // 3 + 255) // 256) * 256
PLE_DIM = 256
RMS_EPS = 1e-6
L2_EPS = 1e-6
IN_SIZES = (DN_W, DN_W, DN_W, DN_W, DN_HEADS, DN_HEADS, SB_W, SB_W, SB_W)
IN_COLS = sum(IN_SIZES)
IN_SPLITS = tuple(int(s) for s in np.cumsum(IN_SIZES)[:-1])

kernel_name = "hybrid_deltanet_stickbreaking_macaron_block"


def rms_norm(x, w):
    xf = x.astype(jnp.float32)
    y = xf * lax.rsqrt(jnp.mean(xf * xf, axis=-1, keepdims=True) + RMS_EPS)
    return (y * w.astype(jnp.float32)).astype(x.dtype)


def l2_norm(x):
    xf = x.astype(jnp.float32)
    return xf * lax.rsqrt(jnp.sum(xf * xf, axis=-1, keepdims=True) + L2_EPS)


def swiglu(x, w_gu, w_down):
    gate, up = jnp.split(x @ w_gu, 2, axis=-1)
    return (jax.nn.silu(gate) * up) @ w_down


def causal_depthwise_conv(x, w):
    k_w, c = w.shape
    return lax.conv_general_dilated(
        x, w[:, None, :].astype(x.dtype), window_strides=(1,),
        padding=((k_w - 1, 0),), dimension_numbers=("NWC", "WIO", "NWC"),
        feature_group_count=c)


def chunk_gated_delta_rule(q, k, v, g, beta):
    b, t, h, dk = q.shape
    dv = v.shape[-1]
    c = DN_CHUNK
    n = t // c
    to_chunks = lambda a: a.reshape(b, n, c, h, a.shape[-1]).transpose(0, 1, 3, 2, 4)
    q, k, v = to_chunks(q), to_chunks(k), to_chunks(v.astype(jnp.float32))
    g = g.astype(jnp.float32).reshape(b, n, c, h).transpose(0, 1, 3, 2)
    beta = beta.astype(jnp.float32).reshape(b, n, c, h).transpose(0, 1, 3, 2)
    g = jnp.cumsum(g, axis=-1)
    idx = jnp.arange(c)
    lower_incl = idx[:, None] >= idx[None, :]
    strict = idx[:, None] > idx[None, :]
    decay = jnp.exp(jnp.where(lower_incl, g[..., :, None] - g[..., None, :], -jnp.inf))
    k_beta = k * beta[..., None]
    v_beta = v * beta[..., None]
    lmat = jnp.where(strict, jnp.einsum("bnhid,bnhjd->bnhij", k_beta, k) * decay, 0.0)
    mmat = lmat + jnp.eye(c, dtype=jnp.float32)
    rhs = jnp.concatenate([v_beta, k_beta * jnp.exp(g)[..., None]], axis=-1)
    sol = lax.linalg.triangular_solve(mmat, rhs, left_side=True, lower=True,
                                      unit_diagonal=True)
    u, w = sol[..., :dv], sol[..., dv:]
    attn_intra = jnp.where(lower_incl, jnp.einsum("bnhid,bnhjd->bnhij", q, k) * decay, 0.0)

    def step(state, xs):
        qi, ki, ui, wi, gi, ai = xs
        v_new = ui - jnp.einsum("bhcd,bhde->bhce", wi, state)
        o = (jnp.einsum("bhcd,bhde->bhce", qi * jnp.exp(gi)[..., None], state)
             + jnp.einsum("bhij,bhje->bhie", ai, v_new))
        g_last = gi[..., -1]
        state = (state * jnp.exp(g_last)[..., None, None]
                 + jnp.einsum("bhcd,bhce->bhde", ki * jnp.exp(g_last[..., None] - gi)[..., None], v_new))
        return state, o

    xs = tuple(jnp.moveaxis(a, 1, 0) for a in (q, k, u, w, g, attn_intra))
    state0 = jnp.zeros((b, h, dk, dv), jnp.float32)
    _, o = lax.scan(step, state0, xs)
    return o.transpose(1, 0, 3, 2, 4).reshape(b, t, h, dv)


def stick_breaking_attention(q, k, v):
    b, t, h, d = q.shape
    scale = d ** -0.5
    outs = []
    for blk in range(t // SB_BLOCK):
        s0, s1 = blk * SB_BLOCK, (blk + 1) * SB_BLOCK
        z = jnp.einsum("bqhd,bkhd->bhqk", q[:, s0:s1], k[:, :s1]).astype(jnp.float32) * scale
        qpos = s0 + jnp.arange(SB_BLOCK)
        kpos = jnp.arange(s1)
        causal = kpos[None, :] < qpos[:, None]
        log_beta = jax.nn.log_sigmoid(z)
        log_1m = jnp.where(causal, jax.nn.log_sigmoid(-z), 0.0)
        rev = lax.cumsum(log_1m, axis=3, reverse=True)
        log_a = log_beta + rev - log_1m
        a = jnp.where(causal, jnp.exp(log_a), 0.0)
        outs.append(jnp.einsum("bhqk,bkhd->bqhd", a.astype(v.dtype), v[:, :s1]))
    return jnp.concatenate(outs, axis=1)


def hybrid_mixer(n, w_in, conv_w, a_log, dt_bias, dn_out_norm, w_out):
    b, t, _ = n.shape
    proj = n @ w_in
    q_dn, k_dn, v_dn, z_dn, a_dn, b_dn, q_sb, k_sb, v_sb = jnp.split(proj, IN_SPLITS, axis=-1)
    qkv = jax.nn.silu(causal_depthwise_conv(proj[..., :3 * DN_W], conv_w))
    q_c, k_c, v_c = jnp.split(qkv, 3, axis=-1)
    q_c = l2_norm(q_c.reshape(b, t, DN_HEADS, DN_HEAD_DIM)) * (DN_HEAD_DIM ** -0.5)
    k_c = l2_norm(k_c.reshape(b, t, DN_HEADS, DN_HEAD_DIM))
    v_c = v_c.reshape(b, t, DN_HEADS, DN_HEAD_DIM)
    g = -jnp.exp(a_log.astype(jnp.float32)) * jax.nn.softplus(
        a_dn.astype(jnp.float32) + dt_bias.astype(jnp.float32))
    beta = jax.nn.sigmoid(b_dn.astype(jnp.float32))
    o_dn = chunk_gated_delta_rule(q_c, k_c, v_c, g, beta).astype(n.dtype)
    o_dn = rms_norm(o_dn, dn_out_norm) * jax.nn.silu(z_dn.reshape(b, t, DN_HEADS, DN_HEAD_DIM))
    o_sb = stick_breaking_attention(q_sb.reshape(b, t, SB_HEADS, SB_HEAD_DIM),
                                    k_sb.reshape(b, t, SB_HEADS, SB_HEAD_DIM),
                                    v_sb.reshape(b, t, SB_HEADS, SB_HEAD_DIM))
    o = jnp.concatenate([o_dn.reshape(b, t, DN_W), o_sb.reshape(b, t, SB_W)], axis=-1)
    return o @ w_out


def setup_inputs(seed: int = 0) -> dict:
    key = jax.random.key(seed)
    ks = jax.random.split(key, 24)
    f32 = jnp.float32
    nrm = lambda k, shape, fan_in: jax.random.normal(k, shape, f32) * (fan_in ** -0.5)
    gain = lambda k, shape: 1.0 + 0.02 * jax.random.normal(k, shape, f32)
    dt = jnp.exp(jax.random.uniform(ks[20], (DEPTH, DN_HEADS), f32, np.log(1e-3), np.log(1e-1)))
    return {
        "x": jax.random.normal(ks[0], (BATCH, SEQ, D_MODEL), f32),
        "p": jax.random.normal(ks[1], (DEPTH, BATCH, SEQ, PLE_DIM), f32),
        "ffn1_norm": gain(ks[2], (DEPTH, D_MODEL)),
        "ffn1_w_gu": nrm(ks[3], (DEPTH, D_MODEL, 2 * D_FF), D_MODEL),
        "ffn1_w_down": nrm(ks[4], (DEPTH, D_FF, D_MODEL), D_FF),
        "mix_norm": gain(ks[5], (DEPTH, D_MODEL)),
        "w_in": nrm(ks[6], (DEPTH, D_MODEL, IN_COLS), D_MODEL),
        "dn_conv": nrm(ks[7], (DEPTH, CONV_K, 3 * DN_W), CONV_K),
        "dn_a_log": jnp.log(jax.random.uniform(ks[8], (DEPTH, DN_HEADS), f32, 1.0, 16.0)),
        "dn_dt_bias": dt + jnp.log(-jnp.expm1(-dt)),
        "dn_out_norm": gain(ks[9], (DEPTH, DN_HEAD_DIM)),
        "w_out": nrm(ks[10], (DEPTH, MIX_W, D_MODEL), MIX_W),
        "ffn2_norm": gain(ks[11], (DEPTH, D_MODEL)),
        "ffn2_w_gu": nrm(ks[12], (DEPTH, D_MODEL, 2 * D_FF), D_MODEL),
        "ffn2_w_down": nrm(ks[13], (DEPTH, D_FF, D_MODEL), D_FF),
        "ple_norm": gain(ks[14], (DEPTH, D_MODEL)),
        "ple_w_gate": nrm(ks[15], (DEPTH, D_MODEL, D_MODEL), D_MODEL),
        "ple_w_proj": nrm(ks[16], (DEPTH, PLE_DIM, D_MODEL), PLE_DIM),
        "final_norm": gain(ks[17], (D_MODEL,)),
    }


def reference(x, p, ffn1_norm, ffn1_w_gu, ffn1_w_down, mix_norm, w_in, dn_conv, dn_a_log,
              dn_dt_bias, dn_out_norm, w_out, ffn2_norm, ffn2_w_gu, ffn2_w_down,
              ple_norm, ple_w_gate, ple_w_proj, final_norm):
    h = x
    for i in range(DEPTH):
        h = h + 0.5 * swiglu(rms_norm(h, ffn1_norm[i]), ffn1_w_gu[i], ffn1_w_down[i])
        h = h + hybrid_mixer(rms_norm(h, mix_norm[i]), w_in[i], dn_conv[i], dn_a_log[i],
                             dn_dt_bias[i], dn_out_norm[i], w_out[i])
        h = h + 0.5 * swiglu(rms_norm(h, ffn2_norm[i]), ffn2_w_gu[i], ffn2_w_down[i])
        gate = jax.nn.sigmoid(rms_norm(h, ple_norm[i]) @ ple_w_gate[i])
        h = h + gate * (p[i] @ ple_w_proj[i])
    return rms_norm(h, final_norm)
```

```python
import numpy as np
from contextlib import ExitStack
import concourse.bass as bass
import concourse.mybir as mybir
from concourse.bass_utils import run_bass_kernel_spmd

F32 = mybir.dt.float32
BF16 = mybir.dt.bfloat16
AF = mybir.ActivationFunctionType
ALU = mybir.AluOpType
AX = mybir.AxisListType

P = 128
D = 2048
F = 5632
KD = D // P
KF = F // P
TT = 512
NS = TT // P
NH = 8
HD = 128
CH = 64
IN_COLS = 7184
PLE = 256
RMS_EPS = 1e-6
L2_EPS = 1e-6
NEG = -30000.0

ENGS = ("pe", "act", "dve", "pool", "sp")


class Buf:
    __slots__ = ("name", "w", "r")

    def __init__(self, name):
        self.name = name
        self.w = []
        self.r = []


class Op:
    __slots__ = ("eng", "fn", "dma", "deps", "needed", "val")

    def __init__(self, eng, fn, dma):
        self.eng = eng
        self.fn = fn
        self.dma = dma
        self.deps = []
        self.needed = False
        self.val = 0

    def key(self):
        return ("d", self.dma[0]) if self.dma else ("e", self.eng)


def _push(lst, o):
    k = o.key()
    lst[:] = [x for x in lst if x.key() != k]
    lst.append(o)


class Sched:
    def __init__(self):
        self.q = {e: [] for e in ENGS}
        self.dcnt = {}

    def add(self, eng, fn, reads=(), writes=(), dma=None):
        if dma is not None:
            c = self.dcnt.get(dma, 0) + 16
            self.dcnt[dma] = c
            o = Op(eng, fn, (dma, c))
        else:
            o = Op(eng, fn, None)
        deps = []
        for b in reads:
            for x in b.w:
                if x.dma or o.dma or x.eng != o.eng or o.eng != "pe":
                    deps.append(x)
        for b in writes:
            for x in b.r + b.w:
                if x.dma or o.dma or x.eng != o.eng:
                    deps.append(x)
        for b in reads:
            _push(b.r, o)
        for b in writes:
            if b.r and not (len(b.r) == 1 and b.r[0] is o):
                b.w = [o]
                b.r = [x for x in b.r if x is o]
            else:
                _push(b.w, o)
        for x in deps:
            if x is not o:
                x.needed = True
                o.deps.append(x)
        self.q[eng].append(o)
        return o

    def emit(self, nc, es, block):
        sems = {}
        for e in ENGS:
            sems[("e", e)] = es.enter_context(nc.semaphore("s_" + e))
            c = 0
            for o in self.q[e]:
                if o.dma is None and o.needed:
                    c += 1
                    o.val = c
        for d in self.dcnt:
            sems[("d", d)] = es.enter_context(nc.semaphore("d_" + d))

        def body(ename):
            def run(eng):
                seen = {}
                for o in self.q[ename]:
                    need = {}
                    for x in o.deps:
                        k = x.key()
                        v = x.dma[1] if x.dma else x.val
                        if seen.get(k, 0) >= v:
                            continue
                        if need.get(k, 0) < v:
                            need[k] = v
                    for k, v in need.items():
                        seen[k] = v
                        eng.wait_ge(sems[k], v)
                    ins = o.fn(eng)
                    if o.dma:
                        ins.then_inc(sems[o.key()], 16)
                    elif o.needed:
                        ins.then_inc(sems[o.key()], 1)
            return run

        block.tensor(body("pe"))
        block.scalar(body("act"))
        block.vector(body("dve"))
        block.gpsimd(body("pool"))
        block.sync(body("sp"))


SM_GAMMA = 0
SM_CONV = 64
SM_ALOG = 160
SM_DTB = 168
SM_ONORM = 176
SMALL_COLS = 304

C_IDENT = 0
C_MASKS = 128
C_ONES = 256
C_TRIKI = 384
C_UPS = 448
C_NEGTRI = 512
C_NMLS = 576
C_PMUI = 640
C_ZERO = 704
C_I2 = 705
C_NEGM = 769
CONST_COLS = 897
NEGB = -30000.0 * (128.0 ** 0.5)

KB = 1024
import os
DSUB = int(os.environ.get("DSUB", "0"))


def build(T, stages="all"):
    NT = T // TT
    nc = bass.Bass("TRN2", target_bir_lowering=False)

    def din(name, shape):
        return nc.dram_tensor(name, list(shape), F32, kind="ExternalInput").ap()

    x_d = din("x", [T, D])
    p_d = din("p", [T, PLE])
    w1gu = din("ffn1_w_gu", [D, 2 * F])
    w1d = din("ffn1_w_down", [F, D])
    win = din("w_in", [D, IN_COLS])
    winf = din("w_in_feat", [20, P, KD * 256])
    winab = din("w_in_ab", [P, KD * 16])
    wout = din("w_out", [D, D])
    w2gu = din("ffn2_w_gu", [D, 2 * F])
    w2d = din("ffn2_w_down", [F, D])
    wpg = din("ple_w_gate", [D, D])
    wpp = din("ple_w_proj", [PLE, D])
    small_d = din("small", [P, SMALL_COLS])
    const_d = din("consts", [P, CONST_COLS])
    fin_d = din("final_bc", [P, D])
    out_d = nc.dram_tensor("out", [T, D], F32, kind="ExternalOutput").ap()
    ksb_d = nc.dram_tensor("ksb_scr", [NH, P, T], BF16, kind="Internal").ap()
    vsb_d = nc.dram_tensor("vsb_scr", [T, NH * HD], BF16, kind="Internal").ap()
    ksbB = [Buf(f"ksbd{hh}") for hh in range(NH)]
    vsbB = Buf("vsbd")

    S = Sched()
    es = ExitStack()

    def sb(name, shape, dt=F32):
        return es.enter_context(nc.sbuf_tensor("sb_" + name, list(shape), dt))

    h = [sb(f"h{s}", [P, D]) for s in range(NS)]
    hB = [Buf(f"h{s}") for s in range(NS)]
    nT = sb("nT", [P, KD, TT], BF16)
    nTB = Buf("nT")
    small = sb("small", [P, SMALL_COLS])
    smallB = Buf("small")
    consts = sb("consts", [P, CONST_COLS])
    constsB = Buf("consts")
    ident_bf = sb("ident_bf", [P, P], BF16)
    maskS_bf = sb("maskS_bf", [P, P], BF16)
    ones_bf = sb("ones_bf", [P, P], BF16)
    negm_bf = sb("negm_bf", [P, P], BF16)
    cbfB = Buf("cbf")
    NW = 4
    WSZ = 4096
    wslots = [sb(f"w{i}", [P, WSZ], BF16) for i in range(NW)]
    wB = [Buf(f"w{i}") for i in range(NW)]
    wctr = [0]
    ssq = sb("ssq", [P, 8])
    ssqB = Buf("ssq")
    rstd = sb("rstd", [P, 8])
    rstdB = Buf("rstd")
    NSG = 4
    sg = [sb(f"sg{i}", [P, TT]) for i in range(NSG)]
    sgB = [Buf(f"sg{i}") for i in range(NSG)]
    sgctr = [0]
    Sf = sb("Sf", [P, NH, HD])
    SfB = Buf("Sf")
    Sb = sb("Sb", [P, NH, HD], BF16)
    SbB = Buf("Sb")
    halo = sb("halo", [P, 24, 3])
    haloB = Buf("halo")
    negA = sb("negA", [P, 8])
    negAB = Buf("negA")
    graw = sb("graw", [P, NS, 8])
    grawB = Buf("graw")
    gtmp = sb("gtmp", [P, NS, 8])
    gtmpB = Buf("gtmp")
    beta = sb("beta", [P, NS, 8])
    betaB = Buf("beta")
    ek = sb("ek", [P, 16])
    ekB = [Buf("ek0"), Buf("ek1")]
    bk = sb("bk", [P, 8])
    bkB = [Buf("bk0"), Buf("bk1")]
    sdecp = sb("sdecp", [P, 16])
    sdecB = [Buf("sdec0"), Buf("sdec1")]
    oss = sb("oss", [P, 8])
    ossB = Buf("oss")
    ntot = sb("ntot", [P, 4])
    ntotB = Buf("ntot")

    ARENA_KB = 100
    arena = sb("arena", [P, ARENA_KB * KB // 4])

    def av(off_b, nbytes, dt, shape=None):
        a = arena[:, off_b // 4:(off_b + nbytes) // 4]
        if dt is BF16:
            a = a.bitcast(BF16)
        return a

    hid = av(0, 44 * KB, BF16).rearrange("p (f t) -> p f t", f=KF)
    hidB = [Buf(f"hid{f}") for f in range(KF)]
    hs = [av((88 + 4 * i) * KB, 4 * KB, BF16) for i in range(2)]
    hsB = [Buf(f"hs{i}") for i in range(2)]
    hsctr = [0]
    junk = av(96 * KB, 4 * KB, BF16)
    junkB = Buf("junk")
    omixT = av(0, 16 * KB, BF16).rearrange("p (c t) -> p c t", c=16)
    omixB = [Buf(f"omix{c}") for c in range(16)]
    zs = av(16 * KB, 8 * KB, BF16).rearrange("p (s c) -> p s c", s=NS)
    zsB = Buf("zs")
    finbc = av(0, 8 * KB, F32)
    finbcB = Buf("finbc")
    pT = av(8 * KB, 1 * KB * 2, BF16).rearrange("p (k t) -> p k t", k=2)
    pTB = Buf("pT")
    ptok = av(10 * KB, 4 * KB, F32).rearrange("p (s c) -> p s c", s=NS)
    ptokB = Buf("ptok")
    ptokb = av(14 * KB, 2 * KB, BF16).rearrange("p (s c) -> p s c", s=NS)
    ptokbB = Buf("ptokb")
    qTs = av(24 * KB, 8 * KB, BF16).rearrange("p (h t) -> p h t", h=NH)
    qTsB = [Buf(f"qTs{i}") for i in range(NH)]
    kstage = [av((32 + i) * KB, 1 * KB, BF16) for i in range(2)]
    kstageB = [Buf(f"kst{i}") for i in range(2)]
    vstage = [av((34 + 2 * i) * KB, 2 * KB, BF16) for i in range(2)]
    vstageB = [Buf(f"vst{i}") for i in range(2)]
    NPAR = 4
    et = [av((38 + 2 * i) * KB, 2 * KB, F32) for i in range(NPAR)]
    etB = [Buf(f"et{i}") for i in range(NPAR)]
    spt = [av((46 + 2 * i) * KB, 2 * KB, F32) for i in range(NPAR)]
    sptB = [Buf(f"spt{i}") for i in range(NPAR)]
    Pb = [av((54 + 3 * i) * KB, 3 * KB, F32) for i in range(NPAR)]
    PbB = [Buf(f"Pb{i}") for i in range(NPAR)]
    lb = [av((66 + 2 * i) * KB, 2 * KB, F32) for i in range(NPAR)]
    lbB = [Buf(f"lb{i}") for i in range(NPAR)]
    aa = [av((74 + i) * KB, 1 * KB, BF16) for i in range(NPAR)]
    aaB = [Buf(f"aa{i}") for i in range(NPAR)]
    aT = [av((78 + i) * KB, 1 * KB, BF16).rearrange("p (k q) -> p k q", k=4) for i in range(NPAR)]
    aTB = [Buf(f"aT{i}") for i in range(NPAR)]
    ktl = [av((82 + 4 * i) * KB, 4 * KB, BF16) for i in range(2)]
    ktlB = [Buf(f"ktl{i}") for i in range(2)]
    vtl = [av((90 + 4 * i) * KB, 4 * KB, BF16).rearrange("p (k d) -> p k d", k=16) for i in range(2)]
    vtlB = [Buf(f"vtl{i}") for i in range(2)]
    ntotB2 = [Buf(f"ntot{i}") for i in range(NPAR)]
    qTd = av(24 * KB, 8 * KB, BF16).rearrange("p (h t) -> p h t", h=NH)
    qTdB = [Buf(f"qTd{i}") for i in range(NH)]
    kTd = av(32 * KB, 8 * KB, BF16).rearrange("p (h t) -> p h t", h=NH)
    kTdB = [Buf(f"kTd{i}") for i in range(NH)]
    ktok = av(40 * KB, 8 * KB, BF16).rearrange("p (s h d) -> p s h d", s=NS, h=NH)
    ktokB = Buf("ktok")
    vtok = av(48 * KB, 8 * KB, BF16).rearrange("p (s h d) -> p s h d", s=NS, h=NH)
    vtokB = Buf("vtok")
    NSET = 6
    SETB = 7 * KB
    raw = [av(56 * KB + SETB * i, 2560, F32) for i in range(NSET)]
    rawB = [Buf(f"raw{i}") for i in range(NSET)]
    cacc = [av(56 * KB + SETB * i + 2560, 2 * KB, F32) for i in range(NSET)]
    caccB = [Buf(f"cacc{i}") for i in range(NSET)]
    csil = cacc
    csilB = caccB
    sqb = [av(56 * KB + SETB * i + 2560 + 2 * KB, 1 * KB, BF16) for i in range(NSET)]
    sqbB = [Buf(f"sqb{i}") for i in range(NSET)]
    rinv = [raw[i][:, 0:TT] for i in range(NSET)]
    rinvB = rawB
    G1 = av(56 * KB, 2 * KB, F32)
    G1B = [Buf("G10"), Buf("G11")]
    G2 = av(58 * KB, 2 * KB, F32)
    G2B = [Buf("G20"), Buf("G21")]
    E1 = av(60 * KB, 2 * KB, F32)
    E1B = [Buf("E10"), Buf("E11")]
    E2 = av(62 * KB, 2 * KB, F32)
    E2B = [Buf("E20"), Buf("E21")]
    Lc = av(64 * KB, 2 * KB, F32)
    LcB = [Buf("Lc0"), Buf("Lc1")]
    Am = av(66 * KB, 2 * KB, F32)
    Pm = av(68 * KB, 2 * KB, F32)
    AB_ = [Buf("A0"), Buf("A1")]
    PB_ = [Buf("Pm0"), Buf("Pm1")]
    ATi = av(70 * KB, 1 * KB, BF16)
    ATiB = [Buf("ATi0"), Buf("ATi1")]
    MT = av(71 * KB, 1 * KB, BF16)
    MTB = [Buf("MT0"), Buf("MT1")]
    Vb = av(72 * KB, 2 * KB, BF16)
    VbB = [Buf("Vb0"), Buf("Vb1")]
    Kbg = av(74 * KB, 2 * KB, BF16)
    KbgB = [Buf("Kbg0"), Buf("Kbg1")]
    Kd = av(76 * KB, 2 * KB, BF16)
    KdB = [Buf("Kd0"), Buf("Kd1")]
    Ut = av(78 * KB, 4 * KB, F32)
    UtB = [Buf("U0"), Buf("U1")]
    WT2 = [av(82 * KB, 1 * KB, BF16), av(99 * KB, 1 * KB, BF16)]
    WTB = [Buf("WT0"), Buf("WT1")]
    vn = av(83 * KB, 2 * KB, BF16)
    vnB = [Buf("vn0"), Buf("vn1")]
    otmp = av(85 * KB, 4 * KB, F32)
    otmpB = [Buf("otmp0"), Buf("otmp1")]
    osub = av(89 * KB, 4 * KB, F32)
    osubB = [Buf("osub0"), Buf("osub1")]
    ob = av(93 * KB, 2 * KB, BF16)
    obB = Buf("ob")
    sqo = av(95 * KB, 4 * KB, F32)
    sqoB = Buf("sqo")

    ATi_b = sb("ATi_b", [P, 512], BF16)
    ek_b = sb("ek_b", [P, 16])
    sdecp_b = sb("sdecp_b", [P, 16])
    ATi_h = [ATi, ATi_b[:]]
    Kd_h = [Kd, sg[2][:].bitcast(BF16)]
    ek_h = [ek, ek_b]
    sdecp_h = [sdecp, sdecp_b]
    Ut_h = [(Ut[:, 0:512], Ut[:, 512:1024]), (sg[0][:], sg[1][:])]
    _sg3 = sg[3][:].bitcast(BF16)
    WT_h = [[WT2[0], WT2[1]], [_sg3[:, 0:512], _sg3[:, 512:1024]]]
    WTBh = [[Buf("WT00"), Buf("WT01")], [Buf("WT10"), Buf("WT11")]]
    psum = [es.enter_context(nc.psum_tensor(f"ps{i}", [P, 512], F32)) for i in range(8)]
    psB = [Buf(f"ps{i}") for i in range(8)]
    psctr = [0]

    class Ring:
        def __init__(self, banks):
            self.banks = list(banks)
            self.c = 0

        def next(self):
            i = self.banks[self.c % len(self.banks)]
            self.c += 1
            return psum[i], psB[i]

    ring_all = Ring(range(8))
    cur_ring = [ring_all]

    def ps_next():
        return cur_ring[0].next()

    def run_interleaved(gens):
        gens = list(gens)
        while gens:
            for g in list(gens):
                try:
                    cur_ring[0] = g[1]
                    next(g[0])
                except StopIteration:
                    gens.remove(g)
        cur_ring[0] = ring_all

    def mm(out, lhsT, rhs, start, stop, reads, writes):
        S.add("pe", lambda e: e.matmul(out, lhsT, rhs, start=start, stop=stop), reads, writes)

    def tr(out, in_, ident, reads, writes):
        S.add("pe", lambda e: e.transpose(out, in_, ident), reads, writes)

    def act(out, in_, func, reads, writes, bias=None, scale=None, accum_out=None):
        kw = {}
        if bias is not None:
            kw["bias"] = bias
        if scale is not None:
            kw["scale"] = scale
        if accum_out is not None:
            kw["accum_out"] = accum_out
        S.add("act", lambda e: e.activation(out, in_, func, **kw), reads, writes)

    def tt(eng, out, in0, in1, op, reads, writes):
        S.add(eng, lambda e: e.tensor_tensor(out, in0, in1, op), reads, writes)

    def ts(eng, out, in0, s1, s2, op0, op1, reads, writes):
        if op1 is None:
            S.add(eng, lambda e: e.tensor_scalar(out, in0, s1, None, op0), reads, writes)
        else:
            S.add(eng, lambda e: e.tensor_scalar(out, in0, s1, s2, op0, op1), reads, writes)

    def stt(out, in0, scalar, in1, op0, op1, reads, writes):
        S.add("dve", lambda e: e.scalar_tensor_tensor(out, in0, scalar, in1, op0, op1), reads, writes)

    def cp(eng, out, in_, reads, writes):
        if eng == "act":
            S.add("act", lambda e: e.copy(out, in_), reads, writes)
        else:
            S.add(eng, lambda e: e.tensor_copy(out, in_), reads, writes)

    def barrier():
        lasts = []
        for e in ENGS:
            for o in reversed(S.q[e]):
                if o.dma is None:
                    lasts.append(o)
                    break
        seen = set()
        for e in ENGS:
            for o in reversed(S.q[e]):
                if o.dma and o.dma[0] not in seen:
                    seen.add(o.dma[0])
                    lasts.append(o)
        for e in ENGS:
            b = Op(e, lambda eng: eng.nop(), None)
            for x in lasts:
                if x.dma or x.eng != e:
                    x.needed = True
                    b.deps.append(x)
            S.q[e].append(b)

    def wload(src, kc, ncols):
        i = wctr[0] % NW
        wctr[0] += 1
        t = wslots[i][:, 0:kc * ncols].rearrange("p (k c) -> p k c", k=kc)
        b = wB[i]
        step = 8
        for k0 in range(0, kc, step):
            k1 = min(kc, k0 + step)
            S.add("pool", lambda e, k0=k0, k1=k1: e.dma_start(out=t[:, k0:k1, :], in_=src[:, k0:k1, :]),
                  reads=(), writes=(b,), dma=f"w{i}")
        return t, b

    def bc(ap2, shape):
        return ap2.broadcast_to(list(shape))

    S.add("sp", lambda e: e.dma_start(out=small[:], in_=small_d), (), (smallB,), dma="small")
    S.add("sp", lambda e: e.dma_start(out=consts[:], in_=const_d), (), (constsB,), dma="consts")
    cp("dve", ident_bf[:], consts[:, C_IDENT:C_IDENT + P], (constsB,), (cbfB,))
    cp("dve", maskS_bf[:], consts[:, C_MASKS:C_MASKS + P], (constsB,), (cbfB,))
    cp("dve", ones_bf[:], consts[:, C_ONES:C_ONES + P], (constsB,), (cbfB,))
    cp("dve", negm_bf[:], consts[:, C_NEGM:C_NEGM + P], (constsB,), (cbfB,))
    S.add("pool", lambda e: e.memset(Sf[:], 0.0), (), (SfB,))
    S.add("pool", lambda e: e.memset(Sb[:], 0.0), (), (SbB,))
    S.add("pool", lambda e: e.memset(halo[:], 0.0), (), (haloB,))
    act(negA[:], small[:, SM_ALOG:SM_ALOG + 8], AF.Exp, (smallB,), (negAB,))
    ts("dve", negA[:], negA[:], -1.0, None, ALU.mult, None, (negAB,), (negAB,))
    ident_f = consts[:, C_IDENT:C_IDENT + P]
    ones_f = consts[:, C_ONES:C_ONES + P]
    maskS_f = consts[:, C_MASKS:C_MASKS + P]
    zcol = consts[:, C_ZERO:C_ZERO + 1]

    def gamma(idx):
        return small[:, SM_GAMMA + idx * KD: SM_GAMMA + (idx + 1) * KD]

    def row_rstd(n):
        ts("dve", rstd[:, 0:n], ssq[:, 0:n], 1.0 / D, RMS_EPS, ALU.mult, ALU.add, (ssqB,), (rstdB,))
        act(rstd[:, 0:n], rstd[:, 0:n], AF.Sqrt, (rstdB,), (rstdB,))
        S.add("dve", lambda e: e.reciprocal(rstd[:, 0:n], rstd[:, 0:n]), (rstdB,), (rstdB,))

    def sum_squares():
        for s in range(NS):
            if s % 2 == 0:
                act(junk, h[s][:], AF.Square, (hB[s],), (junkB, ssqB), accum_out=ssq[:, s:s + 1])
            else:
                S.add("dve", lambda e, s=s: e.scalar_tensor_tensor(hs[1], h[s][:], 1.0, h[s][:], ALU.mult, ALU.mult,
                                                                  accum_out=ssq[:, s:s + 1]),
                      (hB[s],), (hsB[1], ssqB))

    def rmsnorm_to_nT(gidx):
        g = gamma(gidx)
        sum_squares()
        row_rstd(NS)
        for s in range(NS):
            i = hsctr[0] % 2
            hsctr[0] += 1
            if s % 2 == 0:
                ts("dve", hs[i], h[s][:], rstd[:, s:s + 1], None, ALU.mult, None, (hB[s], rstdB), (hsB[i],))
            else:
                act(hs[i], h[s][:], AF.Copy, (hB[s], rstdB), (hsB[i],), scale=rstd[:, s:s + 1])
            for jg in range(KD // 4):
                pt, pb = ps_next()
                pv = pt[:].bitcast(BF16)
                for jj in range(4):
                    j = jg * 4 + jj
                    tr(pv[:, jj * P:(jj + 1) * P], hs[i][:, j * P:(j + 1) * P], ident_bf[:], (hsB[i], cbfB), (pb,))
                gb = bc(g[:, jg * 4:(jg + 1) * 4].unsqueeze(2), [P, 4, P])
                tt("dve", nT[:, jg * 4:(jg + 1) * 4, s * P:(s + 1) * P],
                   pv[:, 0:4 * P].rearrange("p (a b) -> p a b", a=4), gb, ALU.mult, (pb, smallB), (nTB,))

    def ffn(wgu, wd):
        wgu_v = wgu.rearrange("(k p) c -> p k c", p=P)
        wd_v = wd.rearrange("(k p) c -> p k c", p=P)
        for fg in range(F // 512):
            sgi = []
            for part, coff in ((0, 0), (1, F)):
                pss_ = [ps_next() for _ in range(4)]
                for kh in range(2):
                    wt, wtb = wload(wgu_v[:, kh * 8:(kh + 1) * 8, coff + fg * 512:coff + (fg + 1) * 512], 8, 512)
                    for kk in range(8):
                        k = kh * 8 + kk
                        for fl in range(4):
                            mm(pss_[fl][0][:], wt[:, kk, fl * P:(fl + 1) * P], nT[:, k, :], k == 0, k == KD - 1,
                               (wtb, nTB), (pss_[fl][1],))
                for fl in range(4):
                    if part == 0:
                        i = sgctr[0] % NSG
                        sgctr[0] += 1
                        sgi.append(i)
                        act(sg[i][:], pss_[fl][0][:], AF.Silu, (pss_[fl][1],), (sgB[i],))
                    else:
                        i = sgi[fl]
                        tt("dve", hid[:, fg * 4 + fl, :], sg[i][:], pss_[fl][0][:], ALU.mult,
                           (sgB[i], pss_[fl][1]), (hidB[fg * 4 + fl],))
        for cb in range(D // 512):
            pss = [ps_next() for _ in range(NS)]
            f0 = 0
            while f0 < KF:
                kq = min(8, KF - f0)
                wt, wtb = wload(wd_v[:, f0:f0 + kq, cb * 512:(cb + 1) * 512], kq, 512)
                for fl in range(kq):
                    f = f0 + fl
                    for s in range(NS):
                        mm(pss[s][0][:], hid[:, f, s * P:(s + 1) * P], wt[:, fl, :], f == 0, f == KF - 1,
                           (hidB[f], wtb), (pss[s][1],))
                f0 += kq
            for s in range(NS):
                hv = h[s][:, cb * 512:(cb + 1) * 512]
                stt(hv, pss[s][0][:], 0.5, hv, ALU.mult, ALU.add, (pss[s][1], hB[s]), (hB[s],))

    def proj_tok(wv, nk, c0, lhs_fn, lhs_bufs):
        pss = [ps_next() for _ in range(NS)]
        k0 = 0
        while k0 < nk:
            kq = min(8, nk - k0)
            wt, wtb = wload(wv[:, k0:k0 + kq, c0:c0 + 512], kq, 512)
            for kk in range(kq):
                k = k0 + kk
                for s in range(NS):
                    mm(pss[s][0][:], lhs_fn(k, s), wt[:, kk, :], k == 0, k == nk - 1,
                       tuple(lhs_bufs(k)) + (wtb,), (pss[s][1],))
            k0 += kq
        return pss

    win_v = win.rearrange("(k p) c -> p k c", p=P)

    def feat_blk(c0):
        if c0 < 3072:
            return c0 // 256
        return 12 + (c0 - 4112) // 256

    def proj_feat(c0, emit):
        wt, wtb = wload(winf[feat_blk(c0)].rearrange("p (k c) -> p k c", k=KD), KD, 256)
        for cl in range(2):
            pt, pb = ps_next()
            for k in range(KD):
                mm(pt[:], wt[:, k, cl * P:(cl + 1) * P], nT[:, k, :], k == 0, k == KD - 1, (wtb, nTB), (pb,))
            emit(cl, pt, pb)

    SB_SCALE = float(HD) ** -0.5
    Q_OFF, K_OFF, V_OFF, Z_OFF, A_OFF = 0, 1024, 2048, 3072, 4096
    QS_OFF, KS_OFF, VS_OFF = 4112, 5136, 6160

    def sb_phase(it):
        t0 = it * TT
        for blk in range(4):
            def emit_q(cl, pt, pb, blk=blk):
                hh = blk * 2 + cl
                cp("act", qTs[:, hh, :], pt[:], (pb,), (qTsB[hh],))
            proj_feat(QS_OFF + blk * 256, emit_q)
        for blk in range(4):
            def emit_k(cl, pt, pb, blk=blk):
                hh = blk * 2 + cl
                i = hh % 2
                cp("act", kstage[i], pt[:], (pb,), (kstageB[i],))
                S.add("sp", lambda e: e.dma_start(out=ksb_d[hh, :, t0:t0 + TT], in_=kstage[i]),
                      (kstageB[i],), (ksbB[hh],), dma=f"kst{i}")
            proj_feat(KS_OFF + blk * 256, emit_k)
        for cb in range(2):
            pss = proj_tok(win_v, KD, VS_OFF + cb * 512, lambda k, s: nT[:, k, s * P:(s + 1) * P], lambda k: (nTB,))
            for s in range(NS):
                i = s % 2
                cp("act", vstage[i][:, 0:512], pss[s][0][:], (pss[s][1],), (vstageB[i],))
                S.add("sp", lambda e, s=s, i=i, cb=cb: e.dma_start(
                    out=vsb_d[t0 + s * P:t0 + (s + 1) * P, cb * 512:(cb + 1) * 512], in_=vstage[i][:, 0:512]),
                    (vstageB[i],), (vsbB,), dma=f"vst{i}")
        nkb_tot = 4 * (it + 1)
        if stages == "b2":
            return
        for par in range(NPAR):
            S.add("pool", lambda e, par=par: e.memset(Pb[par][:, 0:1], 0.0), (), (PbB[par],))
        rings = [Ring([par]) for par in range(NPAR)]

        def sb_iter(hh, qs, par, kv):
            qb = 4 * it + qs
            nk = qb + 1
            nkeys = nk * P
            po, pob = psum[4 + par], psB[4 + par]
            tiles = []
            kt0 = 0
            while kt0 < nkeys:
                nkt = min(512, nkeys - kt0)
                tiles.append((kt0, nkt))
                kt0 += nkt
            ntl = len(tiles)
            for ti, (kt0, nkt) in enumerate(reversed(tiles)):
                diag = (kt0 + nkt == nkeys)
                pz, pzb = ps_next()
                mm(pz[:, 0:nkt], qTs[:, hh, qs * P:(qs + 1) * P], ktl[kv][:, kt0:kt0 + nkt], True, not diag,
                   (qTsB[hh], ktlB[kv]), (pzb,))
                if diag:
                    mm(pz[:, nkt - P:nkt], ident_bf[:], negm_bf[:], False, True, (cbfB,), (pzb,))
                act(et[par][:, 0:nkt], pz[:, 0:nkt], AF.Exp, (pzb,), (etB[par],), scale=SB_SCALE)
                act(spt[par][:, 0:nkt], et[par][:, 0:nkt], AF.Ln, (etB[par],), (sptB[par],), bias=1.0)
                yield
                S.add("dve", lambda e, par=par, nkt=nkt: e.tensor_tensor_scan(
                    Pb[par][:, 1:1 + nkt], spt[par][:, 0:nkt], bc(zcol, [P, nkt]),
                    Pb[par][:, 0:1], ALU.add, ALU.add), (sptB[par], PbB[par], constsB), (PbB[par],))
                if ti == 0:
                    ts("dve", ntot[:, par:par + 1], Pb[par][:, nkt:nkt + 1], -1.0, None, ALU.mult, None,
                       (PbB[par],), (ntotB2[par],))
                else:
                    stt(ntot[:, par:par + 1], Pb[par][:, nkt:nkt + 1], -1.0, ntot[:, par:par + 1], ALU.mult, ALU.add,
                        (PbB[par], ntotB2[par]), (ntotB2[par],))
                stt(lb[par][:, 0:nkt], pz[:, 0:nkt], SB_SCALE, Pb[par][:, 0:nkt], ALU.mult, ALU.add,
                    (pzb, PbB[par]), (lbB[par],))
                yield
                act(aa[par][:, 0:nkt], lb[par][:, 0:nkt], AF.Exp, (lbB[par], ntotB2[par]), (aaB[par],),
                    bias=ntot[:, par:par + 1])
                yield
                n = nkt // P
                pt, pb = ps_next()
                pv = pt[:].bitcast(BF16)
                for j in range(n):
                    tr(pv[:, j * P:(j + 1) * P], aa[par][:, j * P:(j + 1) * P], ident_bf[:], (aaB[par], cbfB), (pb,))
                cp("act", aT[par][:, 0:n, :], pv[:, 0:n * P].rearrange("p (a b) -> p a b", a=n), (pb,), (aTB[par],))
                yield
                for j in range(n):
                    kb = kt0 // P + j
                    mm(po[:, 0:P], vtl[kv][:, kb, :], aT[par][:, j, :], ti == 0 and j == 0,
                       ti == ntl - 1 and j == n - 1, (vtlB[kv], aTB[par]), (pob,))
                yield
            cp("act", omixT[:, 8 + hh, qs * P:(qs + 1) * P], po[:, 0:P], (pob,), (omixB[8 + hh],))

        def load_kv(hh):
            kv = hh % 2
            S.add("sp", lambda e: e.dma_start(out=ktl[kv][:, 0:nkb_tot * P], in_=ksb_d[hh, :, 0:nkb_tot * P]),
                  (ksbB[hh],), (ktlB[kv],), dma=f"ktl{kv}")
            S.add("sp", lambda e: e.dma_start(
                out=vtl[kv][:, 0:nkb_tot, :],
                in_=vsb_d[0:nkb_tot * P, hh * HD:(hh + 1) * HD].rearrange("(kb p) d -> p kb d", p=P)),
                (vsbB,), (vtlB[kv],), dma=f"vtl{kv}")

        if stages == "b3":
            for hh in range(NH):
                load_kv(hh)
            return
        todo = [(hh, qs) for hh in range(NH) for qs in range(NS)]
        active = {}
        free_slots = list(range(NPAR))
        loaded = set()
        remaining = {hh: NS for hh in range(NH)}
        while todo or active:
            while todo and free_slots:
                hh, qs = todo[0]
                if hh >= 2 and remaining[hh - 2] > 0:
                    break
                todo.pop(0)
                if hh not in loaded:
                    load_kv(hh)
                    loaded.add(hh)
                par = free_slots.pop(0)
                active[par] = (sb_iter(hh, qs, par, hh % 2), hh)
            for par in sorted(active):
                cur_ring[0] = rings[par]
                try:
                    next(active[par][0])
                except StopIteration:
                    remaining[active[par][1]] -= 1
                    del active[par]
                    free_slots.append(par)
        cur_ring[0] = ring_all

    def dn_inproj():
        wcache = {}

        def qkv_chunk(grp, goff, hh, r):
            blk, cl = hh // 2, hh % 2
            if (grp, blk) not in wcache:
                wcache[(grp, blk)] = wload(winf[feat_blk(goff + blk * 256)].rearrange("p (k c) -> p k c", k=KD), KD, 256)
            wt, wtb = wcache[(grp, blk)]
            cidx = grp * 8 + hh
            cw = small[:, SM_CONV + cidx * 4: SM_CONV + cidx * 4 + 4]
            pt, pb = ps_next()
            for k in range(KD):
                mm(pt[:], wt[:, k, cl * P:(cl + 1) * P], nT[:, k, :], k == 0, k == KD - 1, (wtb, nTB), (pb,))
            cp("pool", raw[r][:, 0:3], halo[:, cidx, :], (haloB,), (rawB[r],))
            cp("act", raw[r][:, 3:3 + TT], pt[:], (pb,), (rawB[r],))
            cp("pool", halo[:, cidx, :], raw[r][:, TT:TT + 3], (rawB[r],), (haloB,))
            yield
            ts("dve", cacc[r], raw[r][:, 0:TT], cw[:, 0:1], None, ALU.mult, None, (rawB[r], smallB), (caccB[r],))
            for k in range(1, 4):
                stt(cacc[r], raw[r][:, k:k + TT], cw[:, k:k + 1], cacc[r], ALU.mult, ALU.add,
                    (rawB[r], smallB, caccB[r]), (caccB[r],))
            yield
            if grp == 2:
                act(sqb[r], cacc[r], AF.Silu, (caccB[r],), (sqbB[r],))
                yield
                pt2, pb2 = ps_next()
                pv = pt2[:].bitcast(BF16)
                for s in range(NS):
                    tr(pv[:, s * P:(s + 1) * P], sqb[r][:, s * P:(s + 1) * P], ident_bf[:], (sqbB[r], cbfB), (pb2,))
                cp("dve", vtok[:, :, hh, :], pv[:, 0:TT].rearrange("p (s d) -> p s d", s=NS), (pb2,), (vtokB,))
                return
            act(csil[r], cacc[r], AF.Silu, (caccB[r],), (csilB[r],))
            act(sqb[r], csil[r], AF.Square, (csilB[r],), (sqbB[r],))
            yield
            pt2, pb2 = ps_next()
            mm(pt2[:], ones_bf[:], sqb[r], True, True, (cbfB, sqbB[r]), (pb2,))
            act(rinv[r], pt2[:], AF.Sqrt, (pb2,), (rinvB[r],), bias=L2_EPS)
            yield
            S.add("dve", lambda e: e.reciprocal(rinv[r], rinv[r]), (rinvB[r],), (rinvB[r],))
            if grp == 0:
                stt(qTd[:, hh, :], csil[r], float(HD) ** -0.5, rinv[r], ALU.mult, ALU.mult,
                    (csilB[r], rinvB[r]), (qTdB[hh],))
            else:
                tt("dve", kTd[:, hh, :], csil[r], rinv[r], ALU.mult, (csilB[r], rinvB[r]), (kTdB[hh],))
                yield
                pt3, pb3 = ps_next()
                pv = pt3[:].bitcast(BF16)
                for s in range(NS):
                    tr(pv[:, s * P:(s + 1) * P], kTd[:, hh, s * P:(s + 1) * P], ident_bf[:], (kTdB[hh], cbfB), (pb3,))
                cp("dve", ktok[:, :, hh, :], pv[:, 0:TT].rearrange("p (s d) -> p s d", s=NS), (pb3,), (ktokB,))

        todo = [(grp, goff, hh) for grp, goff in ((0, Q_OFF), (1, K_OFF), (2, V_OFF)) for hh in range(NH)]
        ringsQ = [Ring([r]) for r in range(NSET)]
        active = {}
        free_sets = list(range(NSET))
        while todo or active:
            while todo and free_sets:
                r = free_sets.pop(0)
                grp, goff, hh = todo.pop(0)
                active[r] = qkv_chunk(grp, goff, hh, r)
            for r in sorted(active):
                cur_ring[0] = ringsQ[r]
                try:
                    next(active[r])
                except StopIteration:
                    del active[r]
                    free_sets.append(r)
        cur_ring[0] = ring_all
        for cb in range(2):
            pss = proj_tok(win_v, KD, Z_OFF + cb * 512, lambda k, s: nT[:, k, s * P:(s + 1) * P], lambda k: (nTB,))
            for s in range(NS):
                act(zs[:, s, cb * 512:(cb + 1) * 512], pss[s][0][:], AF.Silu, (pss[s][1],), (zsB,))
        wt, wtb = wload(winab.rearrange("p (k c) -> p k c", k=KD), KD, 16)
        pab, pabb = ps_next()
        for s in range(NS):
            for k in range(KD):
                mm(pab[:, s * 16:(s + 1) * 16], nT[:, k, s * P:(s + 1) * P], wt[:, k, :], k == 0, k == KD - 1,
                   (nTB, wtb), (pabb,))
        pab3 = pab[:, 0:NS * 16].rearrange("p (s c) -> p s c", s=NS)
        tt("dve", gtmp[:], pab3[:, :, 0:8], bc(small[:, SM_DTB:SM_DTB + 8].unsqueeze(1), [P, NS, 8]), ALU.add,
           (pabb, smallB), (gtmpB,))
        act(gtmp[:], gtmp[:], AF.Exp, (gtmpB,), (gtmpB,))
        act(gtmp[:], gtmp[:], AF.Ln, (gtmpB,), (gtmpB,), bias=1.0)
        tt("dve", graw[:], gtmp[:], bc(negA[:].unsqueeze(1), [P, NS, 8]), ALU.mult, (gtmpB, negAB), (grawB,))
        act(beta[:], pab3[:, :, 8:16], AF.Sigmoid, (pabb,), (betaB,))

    dlev = int(stages[2:]) if stages.startswith("dc") else 99

    def dn_chunks():
        TRIKI = consts[:, C_TRIKI:C_TRIKI + CH]
        UPS = consts[:, C_UPS:C_UPS + CH]
        NEGTRI = consts[:, C_NEGTRI:C_NEGTRI + CH]
        NMLS = consts[:, C_NMLS:C_NMLS + CH]
        PMUI = consts[:, C_PMUI:C_PMUI + CH]
        I2 = consts[:, C_I2:C_I2 + CH]
        HALVES = ((0, slice(0, CH)), (1, slice(CH, P)))

        def f3(ap):
            return ap.rearrange("p (h j) -> p h j", h=NH)

        def hc_(hh):
            return slice(hh * CH, (hh + 1) * CH)

        def wy(pr):
            s = pr
            hb = pr % 2
            ATi, Kd, ek, sdecp = ATi_h[hb], Kd_h[hb], ek_h[hb], sdecp_h[hb]
            Ut0, Ut1 = Ut_h[hb]
            WTs = WT_h[hb]
            g_s = graw[:, s, :]
            b_s = beta[:, s, :]
            yield
            cp("dve", f3(G1), bc(g_s.unsqueeze(2), [P, NH, CH]), (grawB,), (G1B[0],))
            tt("dve", f3(G2), bc(g_s.unsqueeze(2), [P, NH, CH]), bc(NEGTRI.unsqueeze(1), [P, NH, CH]),
               ALU.mult, (grawB, constsB), (G2B[0],))
            pd, pdb = ps_next()
            pm, pmb = ps_next()
            for hf, sl in HALVES:
                mm(pd[sl, :], TRIKI[sl, :], G1[sl, :], True, False, (constsB, G1B[0]), (pdb,))
                mm(pd[sl, :], ones_f[sl, 0:CH], G2[sl, :], False, True, (constsB, G2B[0]), (pdb,))
            for hf, sl in HALVES:
                mm(pm[sl, 0:8], TRIKI[sl, :], graw[sl, s, :], True, True, (constsB, grawB), (pmb,))
                mm(pm[sl, 8:16], UPS[sl, :], graw[sl, s, :], True, True, (constsB, grawB), (pmb,))
                mm(pm[:, 16 + 8 * hf:24 + 8 * hf], ones_f[sl, :], graw[sl, s, :], True, True, (constsB, grawB), (pmb,))
            act(ek[:, :], pm[:, 0:16], AF.Exp, (pmb,), (ekB[hb],))
            act(sdecp[:, :], pm[:, 16:32], AF.Exp, (pmb,), (sdecB[hb],))
            stt(f3(E1), f3(pd[:]), 0.0, bc(NMLS.unsqueeze(1), [P, NH, CH]), ALU.min, ALU.add, (pdb, constsB), (E1B[0],))
            act(E1, E1, AF.Exp, (E1B[0],), (E1B[0],))
            stt(f3(E2), f3(pd[:]), 0.0, bc(PMUI.unsqueeze(1), [P, NH, CH]), ALU.max, ALU.add, (pdb, constsB), (E2B[0],))
            act(E2, E2, AF.Exp, (E2B[0],), (E2B[0],), scale=-1.0)
            yield
            pk, pkb = ps_next()
            pq, pqb = ps_next()
            for hh in range(NH):
                for hf, sl in HALVES:
                    cs_ = slice((2 * pr + hf) * CH, (2 * pr + hf + 1) * CH)
                    mm(pk[sl, hc_(hh)], kTd[:, hh, cs_], kTd[:, hh, cs_], True, True, (kTdB[hh],), (pkb,))
            for hh in range(NH):
                for hf, sl in HALVES:
                    cs_ = slice((2 * pr + hf) * CH, (2 * pr + hf + 1) * CH)
                    mm(pq[sl, hc_(hh)], kTd[:, hh, cs_], qTd[:, hh, cs_], True, True, (kTdB[hh], qTdB[hh]), (pqb,))
            tt("dve", Lc, pk[:], E1, ALU.mult, (pkb, E1B[0]), (LcB[0],))
            tt("dve", f3(Lc), f3(Lc), bc(b_s.unsqueeze(2), [P, NH, CH]), ALU.mult, (LcB[0], betaB), (LcB[0],))
            tt("dve", ATi, pq[:], E2, ALU.mult, (pqb, E2B[0]), (ATiB[hb],))
            yield
            pa, pab_ = ps_next()
            for hh in range(NH):
                for hf, sl in HALVES:
                    mm(pa[sl, hc_(hh)], Lc[sl, hc_(hh)], I2[sl, :], True, True, (LcB[0], constsB), (pab_,))
            cp("dve", Am, pa[:], (pab_,), (AB_[0],))
            stt(f3(Pm), f3(Am), -1.0, bc(I2.unsqueeze(1), [P, NH, CH]), ALU.mult, ALU.add, (AB_[0], constsB), (PB_[0],))
            yield
            p1, p1b = ps_next()
            p2, p2b = ps_next()
            for hh in range(NH):
                for hf, sl in HALVES:
                    mm(p1[sl, hc_(hh)], Lc[sl, hc_(hh)], Am[sl, hc_(hh)], True, True, (LcB[0], AB_[0]), (p1b,))
            for hh in range(NH):
                for hf, sl in HALVES:
                    mm(p2[sl, hc_(hh)], Am[sl, hc_(hh)], Lc[sl, hc_(hh)], True, True, (LcB[0], AB_[0]), (p2b,))
            cp("dve", Am, p1[:], (p1b,), (AB_[0],))
            cp("dve", Lc, p2[:], (p2b,), (LcB[0],))
            yield
            for lvl in range(1, 5):
                pA, pAb = ps_next()
                pP, pPb = ps_next()
                pl, plb = ps_next()
                for hh in range(NH):
                    for hf, sl in HALVES:
                        mm(pA[sl, hc_(hh)], Lc[sl, hc_(hh)], Am[sl, hc_(hh)], True, True, (LcB[0], AB_[0]), (pAb,))
                for hh in range(NH):
                    for hf, sl in HALVES:
                        mm(pP[sl, hc_(hh)], Lc[sl, hc_(hh)], Pm[sl, hc_(hh)], True, True, (LcB[0], PB_[0]), (pPb,))
                for hh in range(NH):
                    for hf, sl in HALVES:
                        mm(pl[sl, hc_(hh)], Am[sl, hc_(hh)], Lc[sl, hc_(hh)], True, True, (LcB[0], AB_[0]), (plb,))
                tt("dve", Pm, pP[:], Pm, ALU.add, (pPb, PB_[0]), (PB_[0],))
                cp("dve", Am, pA[:], (pAb,), (AB_[0],))
                cp("dve", Lc, pl[:], (plb,), (LcB[0],))
                yield
            yield
            p5, p5b = ps_next()
            for hh in range(NH):
                for hf, sl in HALVES:
                    mm(p5[sl, hc_(hh)], Lc[sl, hc_(hh)], Pm[sl, hc_(hh)], True, True, (LcB[0], PB_[0]), (p5b,))
            tt("dve", MT, p5[:], Pm, ALU.add, (p5b, PB_[0]), (MTB[0],))
            yield
            h3 = lambda ap: ap.rearrange("p (h d) -> p h d", h=NH)
            tt("dve", h3(Vb), vtok[:, s, :, :], bc(b_s.unsqueeze(2), [P, NH, HD]), ALU.mult, (vtokB, betaB), (VbB[0],))
            tt("dve", bk[:, :], b_s, ek[:, 0:8], ALU.mult, (betaB, ekB[hb]), (bkB[0],))
            tt("dve", h3(Kbg), ktok[:, s, :, :], bc(bk[:, :].unsqueeze(2), [P, NH, HD]), ALU.mult, (ktokB, bkB[0]), (KbgB[0],))
            tt("dve", h3(Kd), ktok[:, s, :, :], bc(ek[:, 8:16].unsqueeze(2), [P, NH, HD]), ALU.mult, (ktokB, ekB[hb]), (KdB[hb],))
            yield
            pua, puab = ps_next()
            pub_, pubb = ps_next()
            pws = [ps_next(), ps_next()]
            for hh in range(NH):
                px, pxB_ = (pua, puab) if hh < 4 else (pub_, pubb)
                o0 = (hh % 4) * HD
                for hf, sl in HALVES:
                    mm(px[sl, o0:o0 + HD], MT[sl, hc_(hh)], Vb[sl, hh * HD:(hh + 1) * HD], True, True,
                       (MTB[0], VbB[0]), (pxB_,))
            for hh in range(NH):
                for hf, sl in HALVES:
                    mm(pws[hf][0][:, hc_(hh)], Kbg[sl, hh * HD:(hh + 1) * HD], MT[sl, hc_(hh)], True, True,
                       (KbgB[0], MTB[0]), (pws[hf][1],))
            cp("dve", Ut0, pua[:], (puab,), (UtB[hb],))
            cp("dve", Ut1, pub_[:], (pubb,), (UtB[hb],))
            for hf, sl in HALVES:
                cp("act", WTs[hf], pws[hf][0][:], (pws[hf][1],), (WTBh[hb][hf],))
        def seq(pr):
            s = pr
            hb = pr % 2
            ATi, Kd, ek, sdecp = ATi_h[hb], Kd_h[hb], ek_h[hb], sdecp_h[hb]
            Ut0, Ut1 = Ut_h[hb]
            WTs = WT_h[hb]
            for hf, sl in HALVES:
                c = 2 * pr + hf
                cs_ = slice(c * CH, (c + 1) * CH)
                WT = WTs[hf]
                pwa, pwab = ps_next()
                pwb2, pwbb2 = ps_next()
                for hh in range(NH):
                    px, pxB_ = (pwa, pwab) if hh < 4 else (pwb2, pwbb2)
                    o0 = (hh % 4) * HD
                    mm(px[sl, o0:o0 + HD], WT[:, hc_(hh)], Sb[:, hh, :], True, True, (WTBh[hb][hf], SbB), (pxB_,))
                tt("dve", vn[sl, 0:512], Ut0[sl, :], pwa[sl, :], ALU.subtract, (UtB[hb], pwab), (vnB[hf],))
                tt("dve", vn[sl, 512:1024], Ut1[sl, :], pwb2[sl, :], ALU.subtract, (UtB[hb], pwbb2), (vnB[hf],))
                yield
                pqa, pqab = ps_next()
                pqb2, pqbb2 = ps_next()
                for hh in range(NH):
                    px, pxB_ = (pqa, pqab) if hh < 4 else (pqb2, pqbb2)
                    o0 = (hh % 4) * HD
                    mm(px[sl, o0:o0 + HD], qTd[:, hh, cs_], Sb[:, hh, :], True, True, (qTdB[hh], SbB), (pxB_,))
                pva, pvab = ps_next()
                pvb2, pvbb2 = ps_next()
                for hh in range(NH):
                    px, pxB_ = (pva, pvab) if hh < 4 else (pvb2, pvbb2)
                    o0 = (hh % 4) * HD
                    mm(px[sl, o0:o0 + HD], ATi[sl, hc_(hh)], vn[sl, hh * HD:(hh + 1) * HD], True, True,
                       (ATiB[hb], vnB[hf]), (pxB_,))
                for half, (pqx, pqxb, pvx, pvxb) in enumerate(((pqa, pqab, pva, pvab), (pqb2, pqbb2, pvb2, pvbb2))):
                    cols = slice(half * 512, half * 512 + 512)
                    eg = bc(ek[sl, half * 4:half * 4 + 4].unsqueeze(2), [CH, 4, HD])
                    tt("dve", otmp[sl, cols].rearrange("p (h d) -> p h d", h=4),
                       pqx[sl, :].rearrange("p (h d) -> p h d", h=4), eg, ALU.mult, (pqxb, ekB[hb]), (otmpB[hf],))
                    tt("dve", osub[sl, cols], otmp[sl, cols], pvx[sl, :], ALU.add, (otmpB[hf], pvxb), (osubB[hf],))
                yield
                psa, psab = ps_next()
                psb2, psbb2 = ps_next()
                for hh in range(NH):
                    px, pxB_ = (psa, psab) if hh < 4 else (psb2, psbb2)
                    o0 = (hh % 4) * HD
                    mm(px[:, o0:o0 + HD], Kd[sl, hh * HD:(hh + 1) * HD], vn[sl, hh * HD:(hh + 1) * HD], True, True,
                       (KdB[hb], vnB[hf]), (pxB_,))
                tt("pool", Sf[:], Sf[:], bc(sdecp[:, 8 * hf:8 * hf + 8].unsqueeze(2), [P, NH, HD]), ALU.mult,
                   (SfB, sdecB[hb]), (SfB,))
                Sf2 = Sf[:].rearrange("p h d -> p (h d)")
                tt("dve", Sf2[:, 0:512], Sf2[:, 0:512], psa[:], ALU.add, (SfB, psab), (SfB,))
                tt("dve", Sf2[:, 512:1024], Sf2[:, 512:1024], psb2[:], ALU.add, (SfB, psbb2), (SfB,))
                cp("act", Sb[:], Sf[:], (SfB,), (SbB,))
                yield
            o3 = osub.rearrange("p (h d) -> p h d", h=NH)
            tt("dve", sqo, osub, osub, ALU.mult, (osubB[0], osubB[1]), (sqoB,))
            S.add("dve", lambda e: e.tensor_reduce(oss[:], sqo.rearrange("p (h d) -> p h d", h=NH), AX.X, ALU.add),
                  (sqoB,), (ossB,))
            ts("dve", oss[:], oss[:], 1.0 / HD, RMS_EPS, ALU.mult, ALU.add, (ossB,), (ossB,))
            act(oss[:], oss[:], AF.Sqrt, (ossB,), (ossB,))
            S.add("dve", lambda e: e.reciprocal(oss[:], oss[:]), (ossB,), (ossB,))
            tt("dve", o3, o3, bc(oss[:].unsqueeze(2), [P, NH, HD]), ALU.mult, (osubB[0], osubB[1], ossB), (osubB[0], osubB[1]))
            tt("dve", o3, o3, bc(small[:, SM_ONORM:SM_ONORM + HD].unsqueeze(1), [P, NH, HD]), ALU.mult,
               (osubB[0], osubB[1], smallB), (osubB[0], osubB[1]))
            tt("dve", ob, osub, zs[:, s, :], ALU.mult, (osubB[0], osubB[1], zsB), (obB,))
            for g4 in range(2):
                pt, pb = ps_next()
                pv = pt[:].bitcast(BF16)
                for j in range(4):
                    hh = g4 * 4 + j
                    tr(pv[:, j * P:(j + 1) * P], ob[:, hh * HD:(hh + 1) * HD], ident_bf[:], (obB, cbfB), (pb,))
                for j in range(4):
                    hh = g4 * 4 + j
                    cp("act", omixT[:, hh, s * P:(s + 1) * P], pv[:, j * P:(j + 1) * P], (pb,), (omixB[hh],))

        ringW, ringS = Ring([0, 1, 2, 3]), Ring([4, 5, 6, 7])
        NPR = TT // CH // 2
        cur_ring[0] = ringW
        for _ in wy(0):
            pass
        for pr in range(NPR):
            gens = [(seq(pr), ringS)]
            if pr + 1 < NPR:
                gens.append((wy(pr + 1), ringW))
            run_interleaved(gens)
        cur_ring[0] = ring_all

    def out_proj():
        wv = wout.rearrange("(k p) c -> p k c", p=P)
        for cb in range(D // 512):
            pss = proj_tok(wv, KD, cb * 512, lambda k, s: omixT[:, k, s * P:(s + 1) * P], lambda k: (omixB[k],))
            for s in range(NS):
                hv = h[s][:, cb * 512:(cb + 1) * 512]
                tt("dve", hv, pss[s][0][:], hv, ALU.add, (pss[s][1], hB[s]), (hB[s],))

    def ple(it):
        t0 = it * TT
        S.add("sp", lambda e: e.dma_start(out=ptok[:], in_=p_d[t0:t0 + TT, :].rearrange("(s p) c -> p s c", p=P)),
              (), (ptokB,), dma="ptok")
        cp("dve", ptokb[:], ptok[:], (ptokB,), (ptokbB,))
        for s in range(NS):
            pt, pb = ps_next()
            pv = pt[:].bitcast(BF16)
            for k in range(2):
                tr(pv[:, k * P:(k + 1) * P], ptokb[:, s, k * P:(k + 1) * P], ident_bf[:], (ptokbB, cbfB), (pb,))
            cp("dve", pT[:, :, s * P:(s + 1) * P], pv[:, 0:2 * P].rearrange("p (k t) -> p k t", k=2), (pb,), (pTB,))
        wgv = wpg.rearrange("(k p) c -> p k c", p=P)
        wpv = wpp.rearrange("(k p) c -> p k c", p=P)
        for cb in range(D // 512):
            psg = proj_tok(wgv, KD, cb * 512, lambda k, s: nT[:, k, s * P:(s + 1) * P], lambda k: (nTB,))
            psp = proj_tok(wpv, 2, cb * 512, lambda k, s: pT[:, k, s * P:(s + 1) * P], lambda k: (pTB,))
            for s in range(NS):
                i = sgctr[0] % NSG
                sgctr[0] += 1
                act(sg[i][:], psg[s][0][:], AF.Sigmoid, (psg[s][1],), (sgB[i],))
                tt("dve", sg[i][:], sg[i][:], psp[s][0][:], ALU.mult, (sgB[i], psp[s][1]), (sgB[i],))
                hv = h[s][:, cb * 512:(cb + 1) * 512]
                tt("dve", hv, hv, sg[i][:], ALU.add, (hB[s], sgB[i]), (hB[s],))

    def final_norm_store(it):
        t0 = it * TT
        S.add("sp", lambda e: e.dma_start(out=finbc, in_=fin_d), (), (finbcB,), dma="finbc")
        sum_squares()
        row_rstd(NS)
        for s in range(NS):
            stt(h[s][:], h[s][:], rstd[:, s:s + 1], finbc, ALU.mult, ALU.mult, (hB[s], rstdB, finbcB), (hB[s],))
            S.add("sp", lambda e, s=s: e.dma_start(out=out_d[t0 + s * P:t0 + (s + 1) * P, :], in_=h[s][:]),
                  (hB[s],), (), dma=f"o{s}")

    for it in range(NT):
        t0 = it * TT
        for s in range(NS):
            S.add("sp", lambda e, s=s, t0=t0: e.dma_start(out=h[s][:], in_=x_d[t0 + s * P:t0 + (s + 1) * P, :]),
                  (), (hB[s],), dma=f"x{s}")
        rmsnorm_to_nT(0)
        ffn(w1gu, w1d)
        if stages == "ffn1":
            for s in range(NS):
                S.add("sp", lambda e, s=s, t0=t0: e.dma_start(out=out_d[t0 + s * P:t0 + (s + 1) * P, :], in_=h[s][:]),
                      (hB[s],), (), dma=f"o{s}")
            continue
        barrier()
        rmsnorm_to_nT(1)
        if stages != "b1":
            sb_phase(it)
        barrier()
        if stages in ("b1", "b2", "b3", "b4"):
            for s in range(NS):
                S.add("sp", lambda e, s=s, t0=t0: e.dma_start(out=out_d[t0 + s * P:t0 + (s + 1) * P, :], in_=h[s][:]),
                      (hB[s],), (), dma=f"o{s}")
            continue
        if stages in ("nodn", "nodnmix"):
            for c in range(8):
                S.add("pool", lambda e, c=c: e.memset(omixT[:, c, :], 0.0), (), (omixB[c],))
        else:
            dn_inproj()
            barrier()
            if dlev > 0:
                dn_chunks()
            if dlev < 99:
                barrier()
                for c in range(8):
                    S.add("pool", lambda e, c=c: e.memset(omixT[:, c, :], 0.0), (), (omixB[c],))
        out_proj()
        barrier()
        if stages in ("mix", "nodnmix") or dlev < 99:
            for s in range(NS):
                S.add("sp", lambda e, s=s, t0=t0: e.dma_start(out=out_d[t0 + s * P:t0 + (s + 1) * P, :], in_=h[s][:]),
                      (hB[s],), (), dma=f"o{s}")
            continue
        rmsnorm_to_nT(2)
        ffn(w2gu, w2d)
        barrier()
        rmsnorm_to_nT(3)
        ple(it)
        final_norm_store(it)
        barrier()

    last = {}
    for o in S.q["sp"]:
        if o.dma and o.dma[0].startswith("o"):
            last[o.dma[0]] = o
    endop = Op("sp", lambda e: e.nop(), None)
    for o in last.values():
        endop.deps.append(o)
    S.q["sp"].append(endop)

    with nc.Block() as block:
        S.emit(nc, es, block)
    es.close()
    return nc


def make_small(inp):
    sm = np.zeros((P, SMALL_COLS), np.float32)
    for i, k in enumerate(("ffn1_norm", "mix_norm", "ffn2_norm", "ple_norm")):
        sm[:, SM_GAMMA + i * KD: SM_GAMMA + (i + 1) * KD] = np.asarray(inp[k], np.float32).reshape(KD, P).T
    cw = np.asarray(inp["dn_conv"], np.float32).reshape(4, 24, P)
    sm[:, SM_CONV:SM_CONV + 96] = cw.transpose(2, 1, 0).reshape(P, 96)
    sm[:, SM_ALOG:SM_ALOG + 8] = np.asarray(inp["dn_a_log"], np.float32).reshape(1, 8)
    sm[:, SM_DTB:SM_DTB + 8] = np.asarray(inp["dn_dt_bias"], np.float32).reshape(1, 8)
    sm[:, SM_ONORM:SM_ONORM + 128] = np.asarray(inp["dn_out_norm"], np.float32).reshape(1, 128)
    return sm


def make_consts():
    c = np.zeros((P, CONST_COLS), np.float32)
    c[:, C_IDENT:C_IDENT + P] = np.eye(P, dtype=np.float32)
    q = np.arange(P)[:, None]
    k = np.arange(P)[None, :]
    c[:, C_MASKS:C_MASKS + P] = (k < q)
    c[:, C_ONES:C_ONES + P] = 1.0
    pm = (np.arange(P) % CH)[:, None]
    j = np.arange(CH)[None, :]
    c[:, C_TRIKI:C_TRIKI + CH] = (pm <= j)
    c[:, C_UPS:C_UPS + CH] = (pm > j)
    c[:, C_NEGTRI:C_NEGTRI + CH] = -(pm <= j).astype(np.float32)
    c[:, C_NMLS:C_NMLS + CH] = np.where(pm > j, 0.0, NEG)
    c[:, C_PMUI:C_PMUI + CH] = np.where(j >= pm, 0.0, -NEG)
    c[:, C_I2:C_I2 + CH] = (pm == j)
    c[:, C_NEGM:C_NEGM + P] = np.where(k < q, 0.0, NEGB)
    return c


_NC_CACHE = {}


def run(inputs, T=2048, ncores=8, stages="all"):
    import os
    key = (T, stages)
    if key not in _NC_CACHE:
        _NC_CACHE[key] = build(T, stages)
    nc = _NC_CACHE[key]
    sm = make_small(inputs)
    cs = make_consts()
    fin_bc = np.ascontiguousarray(np.broadcast_to(np.asarray(inputs["final_norm"], np.float32).reshape(1, D), (P, D)))
    shared = {
        "ffn1_w_gu": np.ascontiguousarray(inputs["ffn1_w_gu"][0]),
        "ffn1_w_down": np.ascontiguousarray(inputs["ffn1_w_down"][0]),
        "w_in": np.ascontiguousarray(inputs["w_in"][0]),
        "w_out": np.ascontiguousarray(inputs["w_out"][0]),
        "ffn2_w_gu": np.ascontiguousarray(inputs["ffn2_w_gu"][0]),
        "ffn2_w_down": np.ascontiguousarray(inputs["ffn2_w_down"][0]),
        "ple_w_gate": np.ascontiguousarray(inputs["ple_w_gate"][0]),
        "ple_w_proj": np.ascontiguousarray(inputs["ple_w_proj"][0]),
        "small": sm, "consts": cs, "final_bc": fin_bc,
    }
    w_in0 = np.asarray(inputs["w_in"][0], np.float32)
    cols = np.concatenate([np.arange(0, 3072), np.arange(4112, 6160)])
    wf = w_in0[:, cols].reshape(KD, P, 20, 256).transpose(2, 1, 0, 3)
    shared["w_in_feat"] = np.ascontiguousarray(wf).reshape(20, P, KD * 256)
    wab = w_in0[:, 4096:4112].reshape(KD, P, 16).transpose(1, 0, 2)
    shared["w_in_ab"] = np.ascontiguousarray(wab).reshape(P, KD * 16)
    in_maps = []
    for c in range(ncores):
        m = dict(shared)
        m["x"] = np.ascontiguousarray(inputs["x"][c, :T])
        m["p"] = np.ascontiguousarray(inputs["p"][0, c, :T])
        in_maps.append(m)
    trace = bool(os.environ.get("K_TRACE"))
    res = run_bass_kernel_spmd(nc, in_maps, core_ids=list(range(ncores)), trace=trace)
    if trace:
        print("exec_time_ns", res.exec_time_ns)
    return np.stack([np.asarray(r["out"]) for r in res.results], axis=0)


def kernel(**inputs):
    inputs = {k: np.asarray(v) for k, v in inputs.items()}
    return run(inputs, T=2048, ncores=8).astype(np.float32)
```

```python
import numpy as np
from contextlib import ExitStack
import concourse.bass as bass
import concourse.mybir as mybir
from concourse.bass_utils import run_bass_kernel_spmd

F32 = mybir.dt.float32
BF16 = mybir.dt.bfloat16
AF = mybir.ActivationFunctionType
ALU = mybir.AluOpType
AX = mybir.AxisListType

P = 128
D = 2048
F = 5632
KD = D // P
KF = F // P
TT = 512
NS = TT // P
NH = 8
HD = 128
CH = 64
IN_COLS = 7184
PLE = 256
RMS_EPS = 1e-6
L2_EPS = 1e-6
NEG = -30000.0

ENGS = ("pe", "act", "dve", "pool", "sp")


class Buf:
    __slots__ = ("name", "w", "r")

    def __init__(self, name):
        self.name = name
        self.w = []
        self.r = []


class Op:
    __slots__ = ("eng", "fn", "dma", "deps", "needed", "val")

    def __init__(self, eng, fn, dma):
        self.eng = eng
        self.fn = fn
        self.dma = dma
        self.deps = []
        self.needed = False
        self.val = 0

    def key(self):
        return ("d", self.dma[0]) if self.dma else ("e", self.eng)


def _push(lst, o):
    k = o.key()
    lst[:] = [x for x in lst if x.key() != k]
    lst.append(o)


class Sched:
    def __init__(self):
        self.q = {e: [] for e in ENGS}
        self.dcnt = {}

    def add(self, eng, fn, reads=(), writes=(), dma=None):
        if dma is not None:
            c = self.dcnt.get(dma, 0) + 16
            self.dcnt[dma] = c
            o = Op(eng, fn, (dma, c))
        else:
            o = Op(eng, fn, None)
        deps = []
        for b in reads:
            for x in b.w:
                if x.dma or o.dma or x.eng != o.eng or o.eng != "pe":
                    deps.append(x)
        for b in writes:
            for x in b.r + b.w:
                if x.dma or o.dma or x.eng != o.eng:
                    deps.append(x)
        for b in reads:
            _push(b.r, o)
        for b in writes:
            if b.r and not (len(b.r) == 1 and b.r[0] is o):
                b.w = [o]
                b.r = [x for x in b.r if x is o]
            else:
                _push(b.w, o)
        for x in deps:
            if x is not o:
                x.needed = True
                o.deps.append(x)
        self.q[eng].append(o)
        return o

    def emit(self, nc, es, block):
        sems = {}
        for e in ENGS:
            sems[("e", e)] = es.enter_context(nc.semaphore("s_" + e))
            c = 0
            for o in self.q[e]:
                if o.dma is None and o.needed:
                    c += 1
                    o.val = c
        for d in self.dcnt:
            sems[("d", d)] = es.enter_context(nc.semaphore("d_" + d))

        def body(ename):
            def run(eng):
                seen = {}
                for o in self.q[ename]:
                    need = {}
                    for x in o.deps:
                        k = x.key()
                        v = x.dma[1] if x.dma else x.val
                        if seen.get(k, 0) >= v:
                            continue
                        if need.get(k, 0) < v:
                            need[k] = v
                    for k, v in need.items():
                        seen[k] = v
                        eng.wait_ge(sems[k], v)
                    ins = o.fn(eng)
                    if o.dma:
                        ins.then_inc(sems[o.key()], 16)
                    elif o.needed:
                        ins.then_inc(sems[o.key()], 1)
            return run

        block.tensor(body("pe"))
        block.scalar(body("act"))
        block.vector(body("dve"))
        block.gpsimd(body("pool"))
        block.sync(body("sp"))


SM_GAMMA = 0
SM_CONV = 64
SM_ALOG = 160
SM_DTB = 168
SM_ONORM = 176
SMALL_COLS = 304

C_IDENT = 0
C_MASKS = 128
C_ONES = 256
C_TRIKI = 384
C_UPS = 448
C_NEGTRI = 512
C_NMLS = 576
C_PMUI = 640
C_ZERO = 704
C_I2 = 705
C_NEGM = 769
CONST_COLS = 897
NEGB = -30000.0 * (128.0 ** 0.5)

KB = 1024
import os
DSUB = int(os.environ.get("DSUB", "0"))


def build(T, stages="all"):
    NT = T // TT
    nc = bass.Bass("TRN2", target_bir_lowering=False)

    def din(name, shape):
        return nc.dram_tensor(name, list(shape), F32, kind="ExternalInput").ap()

    x_d = din("x", [T, D])
    p_d = din("p", [T, PLE])
    w1gu = din("ffn1_w_gu", [D, 2 * F])
    w1d = din("ffn1_w_down", [F, D])
    win = din("w_in", [D, IN_COLS])
    winf = din("w_in_feat", [20, P, KD * 256])
    winab = din("w_in_ab", [P, KD * 16])
    wout = din("w_out", [D, D])
    w2gu = din("ffn2_w_gu", [D, 2 * F])
    w2d = din("ffn2_w_down", [F, D])
    wpg = din("ple_w_gate", [D, D])
    wpp = din("ple_w_proj", [PLE, D])
    small_d = din("small", [P, SMALL_COLS])
    const_d = din("consts", [P, CONST_COLS])
    fin_d = din("final_bc", [P, D])
    out_d = nc.dram_tensor("out", [T, D], F32, kind="ExternalOutput").ap()
    ksb_d = nc.dram_tensor("ksb_scr", [NH, P, T], BF16, kind="Internal").ap()
    vsb_d = nc.dram_tensor("vsb_scr", [T, NH * HD], BF16, kind="Internal").ap()
    ksbB = [Buf(f"ksbd{hh}") for hh in range(NH)]
    vsbB = Buf("vsbd")

    S = Sched()
    es = ExitStack()

    def sb(name, shape, dt=F32):
        return es.enter_context(nc.sbuf_tensor("sb_" + name, list(shape), dt))

    h = [sb(f"h{s}", [P, D]) for s in range(NS)]
    hB = [Buf(f"h{s}") for s in range(NS)]
    nT = sb("nT", [P, KD, TT], BF16)
    nTB = Buf("nT")
    small = sb("small", [P, SMALL_COLS])
    smallB = Buf("small")
    consts = sb("consts", [P, CONST_COLS])
    constsB = Buf("consts")
    ident_bf = sb("ident_bf", [P, P], BF16)
    maskS_bf = sb("maskS_bf", [P, P], BF16)
    ones_bf = sb("ones_bf", [P, P], BF16)
    negm_bf = sb("negm_bf", [P, P], BF16)
    cbfB = Buf("cbf")
    NW = 4
    WSZ = 4096
    wslots = [sb(f"w{i}", [P, WSZ], BF16) for i in range(NW)]
    wB = [Buf(f"w{i}") for i in range(NW)]
    wctr = [0]
    ssq = sb("ssq", [P, 8])
    ssqB = Buf("ssq")
    rstd = sb("rstd", [P, 8])
    rstdB = Buf("rstd")
    NSG = 4
    sg = [sb(f"sg{i}", [P, TT]) for i in range(NSG)]
    sgB = [Buf(f"sg{i}") for i in range(NSG)]
    sgctr = [0]
    Sf = sb("Sf", [P, NH, HD])
    SfB = Buf("Sf")
    Sb = sb("Sb", [P, NH, HD], BF16)
    SbB = Buf("Sb")
    halo = sb("halo", [P, 24, 3])
    haloB = Buf("halo")
    negA = sb("negA", [P, 8])
    negAB = Buf("negA")
    graw = sb("graw", [P, NS, 8])
    grawB = Buf("graw")
    gtmp = sb("gtmp", [P, NS, 8])
    gtmpB = Buf("gtmp")
    beta = sb("beta", [P, NS, 8])
    betaB = Buf("beta")
    ek = sb("ek", [P, 16])
    ekB = [Buf("ek0"), Buf("ek1")]
    bk = sb("bk", [P, 8])
    bkB = [Buf("bk0"), Buf("bk1")]
    sdecp = sb("sdecp", [P, 16])
    sdecB = [Buf("sdec0"), Buf("sdec1")]
    oss = sb("oss", [P, 8])
    ossB = Buf("oss")
    ntot = sb("ntot", [P, 4])
    ntotB = Buf("ntot")

    ARENA_KB = 100
    arena = sb("arena", [P, ARENA_KB * KB // 4])

    def av(off_b, nbytes, dt, shape=None):
        a = arena[:, off_b // 4:(off_b + nbytes) // 4]
        if dt is BF16:
            a = a.bitcast(BF16)
        return a

    hid = av(0, 44 * KB, BF16).rearrange("p (f t) -> p f t", f=KF)
    hidB = [Buf(f"hid{f}") for f in range(KF)]
    hs = [av((88 + 4 * i) * KB, 4 * KB, BF16) for i in range(2)]
    hsB = [Buf(f"hs{i}") for i in range(2)]
    hsctr = [0]
    junk = av(96 * KB, 4 * KB, BF16)
    junkB = Buf("junk")
    junkA = [av((44 + 4 * i) * KB, 4 * KB, BF16) for i in range(NS)]
    junkAB = [Buf(f"junkA{i}") for i in range(NS)]
    omixT = av(0, 16 * KB, BF16).rearrange("p (c t) -> p c t", c=16)
    omixB = [Buf(f"omix{c}") for c in range(16)]
    zs = av(16 * KB, 8 * KB, BF16).rearrange("p (s c) -> p s c", s=NS)
    zsB = Buf("zs")
    finbc = av(0, 8 * KB, F32)
    finbcB = Buf("finbc")
    pT = av(8 * KB, 1 * KB * 2, BF16).rearrange("p (k t) -> p k t", k=2)
    pTB = Buf("pT")
    ptok = av(10 * KB, 4 * KB, F32).rearrange("p (s c) -> p s c", s=NS)
    ptokB = Buf("ptok")
    ptokb = av(14 * KB, 2 * KB, BF16).rearrange("p (s c) -> p s c", s=NS)
    ptokbB = Buf("ptokb")
    qTs = av(24 * KB, 8 * KB, BF16).rearrange("p (h t) -> p h t", h=NH)
    qTsB = [Buf(f"qTs{i}") for i in range(NH)]
    kstage = [av((32 + i) * KB, 1 * KB, BF16) for i in range(2)]
    kstageB = [Buf(f"kst{i}") for i in range(2)]
    vstage = [av((34 + 2 * i) * KB, 2 * KB, BF16) for i in range(2)]
    vstageB = [Buf(f"vst{i}") for i in range(2)]
    NPAR = 4
    et = [av((38 + 2 * i) * KB, 2 * KB, F32) for i in range(NPAR)]
    etB = [Buf(f"et{i}") for i in range(NPAR)]
    spt = [av((46 + 2 * i) * KB, 2 * KB, F32) for i in range(NPAR)]
    sptB = [Buf(f"spt{i}") for i in range(NPAR)]
    Pb = [av((54 + 3 * i) * KB, 3 * KB, F32) for i in range(NPAR)]
    PbB = [Buf(f"Pb{i}") for i in range(NPAR)]
    lb = [av((66 + 2 * i) * KB, 2 * KB, F32) for i in range(NPAR)]
    lbB = [Buf(f"lb{i}") for i in range(NPAR)]
    aa = [av((74 + i) * KB, 1 * KB, BF16) for i in range(NPAR)]
    aaB = [Buf(f"aa{i}") for i in range(NPAR)]
    aT = [av((78 + i) * KB, 1 * KB, BF16).rearrange("p (k q) -> p k q", k=4) for i in range(NPAR)]
    aTB = [Buf(f"aT{i}") for i in range(NPAR)]
    ktl = [av((82 + 4 * i) * KB, 4 * KB, BF16) for i in range(2)]
    ktlB = [Buf(f"ktl{i}") for i in range(2)]
    vtl = [av((90 + 4 * i) * KB, 4 * KB, BF16).rearrange("p (k d) -> p k d", k=16) for i in range(2)]
    vtlB = [Buf(f"vtl{i}") for i in range(2)]
    ntotB2 = [Buf(f"ntot{i}") for i in range(NPAR)]
    qTd = av(24 * KB, 8 * KB, BF16).rearrange("p (h t) -> p h t", h=NH)
    qTdB = [Buf(f"qTd{i}") for i in range(NH)]
    kTd = av(32 * KB, 8 * KB, BF16).rearrange("p (h t) -> p h t", h=NH)
    kTdB = [Buf(f"kTd{i}") for i in range(NH)]
    ktok = av(40 * KB, 8 * KB, BF16).rearrange("p (s h d) -> p s h d", s=NS, h=NH)
    ktokB = Buf("ktok")
    vtok = av(48 * KB, 8 * KB, BF16).rearrange("p (s h d) -> p s h d", s=NS, h=NH)
    vtokB = Buf("vtok")
    NSET = 6
    SETB = 7 * KB
    raw = [av(56 * KB + SETB * i, 2560, F32) for i in range(NSET)]
    rawB = [Buf(f"raw{i}") for i in range(NSET)]
    cacc = [av(56 * KB + SETB * i + 2560, 2 * KB, F32) for i in range(NSET)]
    caccB = [Buf(f"cacc{i}") for i in range(NSET)]
    csil = cacc
    csilB = caccB
    sqb = [av(56 * KB + SETB * i + 2560 + 2 * KB, 1 * KB, BF16) for i in range(NSET)]
    sqbB = [Buf(f"sqb{i}") for i in range(NSET)]
    rinv = [raw[i][:, 0:TT] for i in range(NSET)]
    rinvB = rawB
    G1 = av(56 * KB, 2 * KB, F32)
    G1B = [Buf("G10"), Buf("G11")]
    G2 = av(58 * KB, 2 * KB, F32)
    G2B = [Buf("G20"), Buf("G21")]
    E1 = av(60 * KB, 2 * KB, F32)
    E1B = [Buf("E10"), Buf("E11")]
    E2 = av(62 * KB, 2 * KB, F32)
    E2B = [Buf("E20"), Buf("E21")]
    Lc = av(64 * KB, 2 * KB, F32)
    LcB = [Buf("Lc0"), Buf("Lc1")]
    Am = av(66 * KB, 2 * KB, F32)
    Pm = av(68 * KB, 2 * KB, F32)
    AB_ = [Buf("A0"), Buf("A1")]
    PB_ = [Buf("Pm0"), Buf("Pm1")]
    ATi = av(70 * KB, 1 * KB, BF16)
    ATiB = [Buf("ATi0"), Buf("ATi1")]
    MT = av(71 * KB, 1 * KB, BF16)
    MTB = [Buf("MT0"), Buf("MT1")]
    Vb = av(72 * KB, 2 * KB, BF16)
    VbB = [Buf("Vb0"), Buf("Vb1")]
    Kbg = av(74 * KB, 2 * KB, BF16)
    KbgB = [Buf("Kbg0"), Buf("Kbg1")]
    Kd = av(76 * KB, 2 * KB, BF16)
    KdB = [Buf("Kd0"), Buf("Kd1")]
    Ut = av(78 * KB, 4 * KB, F32)
    UtB = [Buf("U0"), Buf("U1")]
    WT2 = [av(82 * KB, 1 * KB, BF16), av(99 * KB, 1 * KB, BF16)]
    WTB = [Buf("WT0"), Buf("WT1")]
    vn = av(83 * KB, 2 * KB, BF16)
    vnB = [Buf("vn0"), Buf("vn1")]
    otmp = av(85 * KB, 4 * KB, F32)
    otmpB = [Buf("otmp0"), Buf("otmp1")]
    osub = av(89 * KB, 4 * KB, F32)
    osubB = [Buf("osub0"), Buf("osub1")]
    ob = av(93 * KB, 2 * KB, BF16)
    obB = Buf("ob")
    sqo = av(95 * KB, 4 * KB, F32)
    sqoB = Buf("sqo")

    ATi_b = sb("ATi_b", [P, 512], BF16)
    ek_b = sb("ek_b", [P, 16])
    sdecp_b = sb("sdecp_b", [P, 16])
    ATi_h = [ATi, ATi_b[:]]
    Kd_h = [Kd, sg[2][:].bitcast(BF16)]
    ek_h = [ek, ek_b]
    sdecp_h = [sdecp, sdecp_b]
    Ut_h = [(Ut[:, 0:512], Ut[:, 512:1024]), (sg[0][:], sg[1][:])]
    _sg3 = sg[3][:].bitcast(BF16)
    WT_h = [[WT2[0], WT2[1]], [_sg3[:, 0:512], _sg3[:, 512:1024]]]
    WTBh = [[Buf("WT00"), Buf("WT01")], [Buf("WT10"), Buf("WT11")]]
    psum = [es.enter_context(nc.psum_tensor(f"ps{i}", [P, 512], F32)) for i in range(8)]
    psB = [Buf(f"ps{i}") for i in range(8)]
    psctr = [0]

    class Ring:
        def __init__(self, banks):
            self.banks = list(banks)
            self.c = 0

        def next(self):
            i = self.banks[self.c % len(self.banks)]
            self.c += 1
            return psum[i], psB[i]

    ring_all = Ring(range(8))
    cur_ring = [ring_all]

    def ps_next():
        return cur_ring[0].next()

    def run_interleaved(gens):
        gens = list(gens)
        while gens:
            for g in list(gens):
                try:
                    cur_ring[0] = g[1]
                    next(g[0])
                except StopIteration:
                    gens.remove(g)
        cur_ring[0] = ring_all

    def mm(out, lhsT, rhs, start, stop, reads, writes):
        S.add("pe", lambda e: e.matmul(out, lhsT, rhs, start=start, stop=stop), reads, writes)

    def tr(out, in_, ident, reads, writes):
        S.add("pe", lambda e: e.transpose(out, in_, ident), reads, writes)

    def act(out, in_, func, reads, writes, bias=None, scale=None, accum_out=None):
        kw = {}
        if bias is not None:
            kw["bias"] = bias
        if scale is not None:
            kw["scale"] = scale
        if accum_out is not None:
            kw["accum_out"] = accum_out
        S.add("act", lambda e: e.activation(out, in_, func, **kw), reads, writes)

    def tt(eng, out, in0, in1, op, reads, writes):
        S.add(eng, lambda e: e.tensor_tensor(out, in0, in1, op), reads, writes)

    def ts(eng, out, in0, s1, s2, op0, op1, reads, writes):
        if op1 is None:
            S.add(eng, lambda e: e.tensor_scalar(out, in0, s1, None, op0), reads, writes)
        else:
            S.add(eng, lambda e: e.tensor_scalar(out, in0, s1, s2, op0, op1), reads, writes)

    def stt(out, in0, scalar, in1, op0, op1, reads, writes):
        S.add("dve", lambda e: e.scalar_tensor_tensor(out, in0, scalar, in1, op0, op1), reads, writes)

    def cp(eng, out, in_, reads, writes):
        if eng == "act":
            S.add("act", lambda e: e.copy(out, in_), reads, writes)
        else:
            S.add(eng, lambda e: e.tensor_copy(out, in_), reads, writes)

    def barrier():
        lasts = []
        for e in ENGS:
            for o in reversed(S.q[e]):
                if o.dma is None:
                    lasts.append(o)
                    break
        seen = set()
        for e in ENGS:
            for o in reversed(S.q[e]):
                if o.dma and o.dma[0] not in seen:
                    seen.add(o.dma[0])
                    lasts.append(o)
        for e in ENGS:
            b = Op(e, lambda eng: eng.nop(), None)
            for x in lasts:
                if x.dma or x.eng != e:
                    x.needed = True
                    b.deps.append(x)
            S.q[e].append(b)

    def wload(src, kc, ncols):
        i = wctr[0] % NW
        wctr[0] += 1
        t = wslots[i][:, 0:kc * ncols].rearrange("p (k c) -> p k c", k=kc)
        b = wB[i]
        step = 8
        for k0 in range(0, kc, step):
            k1 = min(kc, k0 + step)
            S.add("pool", lambda e, k0=k0, k1=k1: e.dma_start(out=t[:, k0:k1, :], in_=src[:, k0:k1, :]),
                  reads=(), writes=(b,), dma=f"w{i}")
        return t, b

    def bc(ap2, shape):
        return ap2.broadcast_to(list(shape))

    S.add("sp", lambda e: e.dma_start(out=small[:], in_=small_d), (), (smallB,), dma="small")
    S.add("sp", lambda e: e.dma_start(out=consts[:], in_=const_d), (), (constsB,), dma="consts")
    cp("dve", ident_bf[:], consts[:, C_IDENT:C_IDENT + P], (constsB,), (cbfB,))
    cp("dve", maskS_bf[:], consts[:, C_MASKS:C_MASKS + P], (constsB,), (cbfB,))
    cp("dve", ones_bf[:], consts[:, C_ONES:C_ONES + P], (constsB,), (cbfB,))
    cp("dve", negm_bf[:], consts[:, C_NEGM:C_NEGM + P], (constsB,), (cbfB,))
    S.add("pool", lambda e: e.memset(Sf[:], 0.0), (), (SfB,))
    S.add("pool", lambda e: e.memset(Sb[:], 0.0), (), (SbB,))
    S.add("pool", lambda e: e.memset(halo[:], 0.0), (), (haloB,))
    act(negA[:], small[:, SM_ALOG:SM_ALOG + 8], AF.Exp, (smallB,), (negAB,))
    ts("dve", negA[:], negA[:], -1.0, None, ALU.mult, None, (negAB,), (negAB,))
    ident_f = consts[:, C_IDENT:C_IDENT + P]
    ones_f = consts[:, C_ONES:C_ONES + P]
    maskS_f = consts[:, C_MASKS:C_MASKS + P]
    zcol = consts[:, C_ZERO:C_ZERO + 1]

    def gamma(idx):
        return small[:, SM_GAMMA + idx * KD: SM_GAMMA + (idx + 1) * KD]

    def row_rstd(n):
        ts("dve", rstd[:, 0:n], ssq[:, 0:n], 1.0 / D, RMS_EPS, ALU.mult, ALU.add, (ssqB,) + tuple(junkAB), (rstdB,))
        act(rstd[:, 0:n], rstd[:, 0:n], AF.Sqrt, (rstdB,), (rstdB,))
        S.add("dve", lambda e: e.reciprocal(rstd[:, 0:n], rstd[:, 0:n]), (rstdB,), (rstdB,))

    def sum_squares():
        for s in range(NS):
            if s % 2 == 0:
                act(junkA[s], h[s][:], AF.Square, (hB[s],), (junkAB[s], ssqB), accum_out=ssq[:, s:s + 1])
            else:
                S.add("dve", lambda e, s=s: e.scalar_tensor_tensor(junkA[s], h[s][:], 1.0, h[s][:], ALU.mult, ALU.mult,
                                                                  accum_out=ssq[:, s:s + 1]),
                      (hB[s],), (junkAB[s], ssqB))

    def rmsnorm_to_nT(gidx):
        g = gamma(gidx)
        sum_squares()
        row_rstd(NS)
        for s in range(NS):
            i = hsctr[0] % 2
            hsctr[0] += 1
            if s % 2 == 0:
                ts("dve", hs[i], h[s][:], rstd[:, s:s + 1], None, ALU.mult, None, (hB[s], rstdB), (hsB[i],))
            else:
                act(hs[i], h[s][:], AF.Copy, (hB[s], rstdB), (hsB[i],), scale=rstd[:, s:s + 1])
            for jg in range(KD // 4):
                pt, pb = ps_next()
                pv = pt[:].bitcast(BF16)
                for jj in range(4):
                    j = jg * 4 + jj
                    tr(pv[:, jj * P:(jj + 1) * P], hs[i][:, j * P:(j + 1) * P], ident_bf[:], (hsB[i], cbfB), (pb,))
                gb = bc(g[:, jg * 4:(jg + 1) * 4].unsqueeze(2), [P, 4, P])
                tt("dve", nT[:, jg * 4:(jg + 1) * 4, s * P:(s + 1) * P],
                   pv[:, 0:4 * P].rearrange("p (a b) -> p a b", a=4), gb, ALU.mult, (pb, smallB), (nTB,))

    def ffn(wgu, wd):
        wgu_v = wgu.rearrange("(k p) c -> p k c", p=P)
        wd_v = wd.rearrange("(k p) c -> p k c", p=P)
        for fg in range(F // 512):
            sgi = []
            for part, coff in ((0, 0), (1, F)):
                pss_ = [ps_next() for _ in range(4)]
                for kh in range(2):
                    wt, wtb = wload(wgu_v[:, kh * 8:(kh + 1) * 8, coff + fg * 512:coff + (fg + 1) * 512], 8, 512)
                    for kk in range(8):
                        k = kh * 8 + kk
                        for fl in range(4):
                            mm(pss_[fl][0][:], wt[:, kk, fl * P:(fl + 1) * P], nT[:, k, :], k == 0, k == KD - 1,
                               (wtb, nTB), (pss_[fl][1],))
                for fl in range(4):
                    if part == 0:
                        i = sgctr[0] % NSG
                        sgctr[0] += 1
                        sgi.append(i)
                        act(sg[i][:], pss_[fl][0][:], AF.Silu, (pss_[fl][1],), (sgB[i],))
                    else:
                        i = sgi[fl]
                        tt("dve", hid[:, fg * 4 + fl, :], sg[i][:], pss_[fl][0][:], ALU.mult,
                           (sgB[i], pss_[fl][1]), (hidB[fg * 4 + fl],))
        for cb in range(D // 512):
            pss = [ps_next() for _ in range(NS)]
            f0 = 0
            while f0 < KF:
                kq = min(8, KF - f0)
                wt, wtb = wload(wd_v[:, f0:f0 + kq, cb * 512:(cb + 1) * 512], kq, 512)
                for fl in range(kq):
                    f = f0 + fl
                    for s in range(NS):
                        mm(pss[s][0][:], hid[:, f, s * P:(s + 1) * P], wt[:, fl, :], f == 0, f == KF - 1,
                           (hidB[f], wtb), (pss[s][1],))
                f0 += kq
            for s in range(NS):
                hv = h[s][:, cb * 512:(cb + 1) * 512]
                stt(hv, pss[s][0][:], 0.5, hv, ALU.mult, ALU.add, (pss[s][1], hB[s]), (hB[s],))

    def proj_tok(wv, nk, c0, lhs_fn, lhs_bufs):
        pss = [ps_next() for _ in range(NS)]
        k0 = 0
        while k0 < nk:
            kq = min(8, nk - k0)
            wt, wtb = wload(wv[:, k0:k0 + kq, c0:c0 + 512], kq, 512)
            for kk in range(kq):
                k = k0 + kk
                for s in range(NS):
                    mm(pss[s][0][:], lhs_fn(k, s), wt[:, kk, :], k == 0, k == nk - 1,
                       tuple(lhs_bufs(k)) + (wtb,), (pss[s][1],))
            k0 += kq
        return pss

    win_v = win.rearrange("(k p) c -> p k c", p=P)

    def feat_blk(c0):
        if c0 < 3072:
            return c0 // 256
        return 12 + (c0 - 4112) // 256

    def proj_feat(c0, emit):
        wt, wtb = wload(winf[feat_blk(c0)].rearrange("p (k c) -> p k c", k=KD), KD, 256)
        for cl in range(2):
            pt, pb = ps_next()
            for k in range(KD):
                mm(pt[:], wt[:, k, cl * P:(cl + 1) * P], nT[:, k, :], k == 0, k == KD - 1, (wtb, nTB), (pb,))
            emit(cl, pt, pb)

    SB_SCALE = float(HD) ** -0.5
    Q_OFF, K_OFF, V_OFF, Z_OFF, A_OFF = 0, 1024, 2048, 3072, 4096
    QS_OFF, KS_OFF, VS_OFF = 4112, 5136, 6160

    def sb_phase(it):
        t0 = it * TT
        for blk in range(4):
            def emit_q(cl, pt, pb, blk=blk):
                hh = blk * 2 + cl
                cp("act", qTs[:, hh, :], pt[:], (pb,), (qTsB[hh],))
            proj_feat(QS_OFF + blk * 256, emit_q)
        for blk in range(4):
            def emit_k(cl, pt, pb, blk=blk):
                hh = blk * 2 + cl
                i = hh % 2
                cp("act", kstage[i], pt[:], (pb,), (kstageB[i],))
                S.add("sp", lambda e: e.dma_start(out=ksb_d[hh, :, t0:t0 + TT], in_=kstage[i]),
                      (kstageB[i],), (ksbB[hh],), dma=f"kst{i}")
            proj_feat(KS_OFF + blk * 256, emit_k)
        for cb in range(2):
            pss = proj_tok(win_v, KD, VS_OFF + cb * 512, lambda k, s: nT[:, k, s * P:(s + 1) * P], lambda k: (nTB,))
            for s in range(NS):
                i = s % 2
                cp("act", vstage[i][:, 0:512], pss[s][0][:], (pss[s][1],), (vstageB[i],))
                S.add("sp", lambda e, s=s, i=i, cb=cb: e.dma_start(
                    out=vsb_d[t0 + s * P:t0 + (s + 1) * P, cb * 512:(cb + 1) * 512], in_=vstage[i][:, 0:512]),
                    (vstageB[i],), (vsbB,), dma=f"vst{i}")
        nkb_tot = 4 * (it + 1)
        if stages == "b2":
            return
        for par in range(NPAR):
            S.add("pool", lambda e, par=par: e.memset(Pb[par][:, 0:1], 0.0), (), (PbB[par],))
        rings = [Ring([par]) for par in range(NPAR)]

        def sb_iter(hh, qs, par, kv):
            qb = 4 * it + qs
            nk = qb + 1
            nkeys = nk * P
            po, pob = psum[4 + par], psB[4 + par]
            tiles = []
            kt0 = 0
            while kt0 < nkeys:
                nkt = min(512, nkeys - kt0)
                tiles.append((kt0, nkt))
                kt0 += nkt
            ntl = len(tiles)
            for ti, (kt0, nkt) in enumerate(reversed(tiles)):
                diag = (kt0 + nkt == nkeys)
                pz, pzb = ps_next()
                mm(pz[:, 0:nkt], qTs[:, hh, qs * P:(qs + 1) * P], ktl[kv][:, kt0:kt0 + nkt], True, not diag,
                   (qTsB[hh], ktlB[kv]), (pzb,))
                if diag:
                    mm(pz[:, nkt - P:nkt], ident_bf[:], negm_bf[:], False, True, (cbfB,), (pzb,))
                act(et[par][:, 0:nkt], pz[:, 0:nkt], AF.Exp, (pzb,), (etB[par],), scale=SB_SCALE)
                act(spt[par][:, 0:nkt], et[par][:, 0:nkt], AF.Ln, (etB[par],), (sptB[par],), bias=1.0)
                yield
                S.add("dve", lambda e, par=par, nkt=nkt: e.tensor_tensor_scan(
                    Pb[par][:, 1:1 + nkt], spt[par][:, 0:nkt], bc(zcol, [P, nkt]),
                    Pb[par][:, 0:1], ALU.add, ALU.add), (sptB[par], PbB[par], constsB), (PbB[par],))
                if ti == 0:
                    ts("dve", ntot[:, par:par + 1], Pb[par][:, nkt:nkt + 1], -1.0, None, ALU.mult, None,
                       (PbB[par],), (ntotB2[par],))
                else:
                    stt(ntot[:, par:par + 1], Pb[par][:, nkt:nkt + 1], -1.0, ntot[:, par:par + 1], ALU.mult, ALU.add,
                        (PbB[par], ntotB2[par]), (ntotB2[par],))
                stt(lb[par][:, 0:nkt], pz[:, 0:nkt], SB_SCALE, Pb[par][:, 0:nkt], ALU.mult, ALU.add,
                    (pzb, PbB[par]), (lbB[par],))
                yield
                act(aa[par][:, 0:nkt], lb[par][:, 0:nkt], AF.Exp, (lbB[par], ntotB2[par]), (aaB[par],),
                    bias=ntot[:, par:par + 1])
                yield
                n = nkt // P
                pt, pb = ps_next()
                pv = pt[:].bitcast(BF16)
                for j in range(n):
                    tr(pv[:, j * P:(j + 1) * P], aa[par][:, j * P:(j + 1) * P], ident_bf[:], (aaB[par], cbfB), (pb,))
                cp("act", aT[par][:, 0:n, :], pv[:, 0:n * P].rearrange("p (a b) -> p a b", a=n), (pb,), (aTB[par],))
                yield
                for j in range(n):
                    kb = kt0 // P + j
                    mm(po[:, 0:P], vtl[kv][:, kb, :], aT[par][:, j, :], ti == 0 and j == 0,
                       ti == ntl - 1 and j == n - 1, (vtlB[kv], aTB[par]), (pob,))
                yield
            cp("act", omixT[:, 8 + hh, qs * P:(qs + 1) * P], po[:, 0:P], (pob,), (omixB[8 + hh],))

        def load_kv(hh):
            kv = hh % 2
            S.add("sp", lambda e: e.dma_start(out=ktl[kv][:, 0:nkb_tot * P], in_=ksb_d[hh, :, 0:nkb_tot * P]),
                  (ksbB[hh],), (ktlB[kv],), dma=f"ktl{kv}")
            S.add("sp", lambda e: e.dma_start(
                out=vtl[kv][:, 0:nkb_tot, :],
                in_=vsb_d[0:nkb_tot * P, hh * HD:(hh + 1) * HD].rearrange("(kb p) d -> p kb d", p=P)),
                (vsbB,), (vtlB[kv],), dma=f"vtl{kv}")

        if stages == "b3":
            for hh in range(NH):
                load_kv(hh)
            return
        todo = [(hh, qs) for hh in range(NH) for qs in range(NS)]
        active = {}
        free_slots = list(range(NPAR))
        loaded = set()
        remaining = {hh: NS for hh in range(NH)}
        while todo or active:
            while todo and free_slots:
                hh, qs = todo[0]
                if hh >= 2 and remaining[hh - 2] > 0:
                    break
                todo.pop(0)
                if hh not in loaded:
                    load_kv(hh)
                    loaded.add(hh)
                par = free_slots.pop(0)
                active[par] = (sb_iter(hh, qs, par, hh % 2), hh)
            for par in sorted(active):
                cur_ring[0] = rings[par]
                try:
                    next(active[par][0])
                except StopIteration:
                    remaining[active[par][1]] -= 1
                    del active[par]
                    free_slots.append(par)
        cur_ring[0] = ring_all

    def dn_inproj():
        wcache = {}

        def qkv_chunk(grp, goff, hh, r):
            blk, cl = hh // 2, hh % 2
            if (grp, blk) not in wcache:
                wcache[(grp, blk)] = wload(winf[feat_blk(goff + blk * 256)].rearrange("p (k c) -> p k c", k=KD), KD, 256)
            wt, wtb = wcache[(grp, blk)]
            cidx = grp * 8 + hh
            cw = small[:, SM_CONV + cidx * 4: SM_CONV + cidx * 4 + 4]
            pt, pb = ps_next()
            for k in range(KD):
                mm(pt[:], wt[:, k, cl * P:(cl + 1) * P], nT[:, k, :], k == 0, k == KD - 1, (wtb, nTB), (pb,))
            cp("pool", raw[r][:, 0:3], halo[:, cidx, :], (haloB,), (rawB[r],))
            cp("act", raw[r][:, 3:3 + TT], pt[:], (pb,), (rawB[r],))
            cp("pool", halo[:, cidx, :], raw[r][:, TT:TT + 3], (rawB[r],), (haloB,))
            yield
            ts("dve", cacc[r], raw[r][:, 0:TT], cw[:, 0:1], None, ALU.mult, None, (rawB[r], smallB), (caccB[r],))
            for k in range(1, 4):
                stt(cacc[r], raw[r][:, k:k + TT], cw[:, k:k + 1], cacc[r], ALU.mult, ALU.add,
                    (rawB[r], smallB, caccB[r]), (caccB[r],))
            yield
            if grp == 2:
                act(sqb[r], cacc[r], AF.Silu, (caccB[r],), (sqbB[r],))
                yield
                pt2, pb2 = ps_next()
                pv = pt2[:].bitcast(BF16)
                for s in range(NS):
                    tr(pv[:, s * P:(s + 1) * P], sqb[r][:, s * P:(s + 1) * P], ident_bf[:], (sqbB[r], cbfB), (pb2,))
                cp("dve", vtok[:, :, hh, :], pv[:, 0:TT].rearrange("p (s d) -> p s d", s=NS), (pb2,), (vtokB,))
                return
            act(csil[r], cacc[r], AF.Silu, (caccB[r],), (csilB[r],))
            act(sqb[r], csil[r], AF.Square, (csilB[r],), (sqbB[r],))
            yield
            pt2, pb2 = ps_next()
            mm(pt2[:], ones_bf[:], sqb[r], True, True, (cbfB, sqbB[r]), (pb2,))
            act(rinv[r], pt2[:], AF.Sqrt, (pb2,), (rinvB[r],), bias=L2_EPS)
            yield
            S.add("dve", lambda e: e.reciprocal(rinv[r], rinv[r]), (rinvB[r],), (rinvB[r],))
            if grp == 0:
                stt(qTd[:, hh, :], csil[r], float(HD) ** -0.5, rinv[r], ALU.mult, ALU.mult,
                    (csilB[r], rinvB[r]), (qTdB[hh],))
            else:
                tt("dve", kTd[:, hh, :], csil[r], rinv[r], ALU.mult, (csilB[r], rinvB[r]), (kTdB[hh],))
                yield
                pt3, pb3 = ps_next()
                pv = pt3[:].bitcast(BF16)
                for s in range(NS):
                    tr(pv[:, s * P:(s + 1) * P], kTd[:, hh, s * P:(s + 1) * P], ident_bf[:], (kTdB[hh], cbfB), (pb3,))
                cp("dve", ktok[:, :, hh, :], pv[:, 0:TT].rearrange("p (s d) -> p s d", s=NS), (pb3,), (ktokB,))

        todo = [(grp, goff, hh) for grp, goff in ((0, Q_OFF), (1, K_OFF), (2, V_OFF)) for hh in range(NH)]
        ringsQ = [Ring([r]) for r in range(NSET)]
        active = {}
        free_sets = list(range(NSET))
        while todo or active:
            while todo and free_sets:
                r = free_sets.pop(0)
                grp, goff, hh = todo.pop(0)
                active[r] = qkv_chunk(grp, goff, hh, r)
            for r in sorted(active):
                cur_ring[0] = ringsQ[r]
                try:
                    next(active[r])
                except StopIteration:
                    del active[r]
                    free_sets.append(r)
        cur_ring[0] = ring_all
        for cb in range(2):
            pss = proj_tok(win_v, KD, Z_OFF + cb * 512, lambda k, s: nT[:, k, s * P:(s + 1) * P], lambda k: (nTB,))
            for s in range(NS):
                act(zs[:, s, cb * 512:(cb + 1) * 512], pss[s][0][:], AF.Silu, (pss[s][1],), (zsB,))
        wt, wtb = wload(winab.rearrange("p (k c) -> p k c", k=KD), KD, 16)
        pab, pabb = ps_next()
        for s in range(NS):
            for k in range(KD):
                mm(pab[:, s * 16:(s + 1) * 16], nT[:, k, s * P:(s + 1) * P], wt[:, k, :], k == 0, k == KD - 1,
                   (nTB, wtb), (pabb,))
        pab3 = pab[:, 0:NS * 16].rearrange("p (s c) -> p s c", s=NS)
        tt("dve", gtmp[:], pab3[:, :, 0:8], bc(small[:, SM_DTB:SM_DTB + 8].unsqueeze(1), [P, NS, 8]), ALU.add,
           (pabb, smallB), (gtmpB,))
        act(gtmp[:], gtmp[:], AF.Exp, (gtmpB,), (gtmpB,))
        act(gtmp[:], gtmp[:], AF.Ln, (gtmpB,), (gtmpB,), bias=1.0)
        tt("dve", graw[:], gtmp[:], bc(negA[:].unsqueeze(1), [P, NS, 8]), ALU.mult, (gtmpB, negAB), (grawB,))
        act(beta[:], pab3[:, :, 8:16], AF.Sigmoid, (pabb,), (betaB,))

    dlev = int(stages[2:]) if stages.startswith("dc") else 99

    def dn_chunks():
        TRIKI = consts[:, C_TRIKI:C_TRIKI + CH]
        UPS = consts[:, C_UPS:C_UPS + CH]
        NEGTRI = consts[:, C_NEGTRI:C_NEGTRI + CH]
        NMLS = consts[:, C_NMLS:C_NMLS + CH]
        PMUI = consts[:, C_PMUI:C_PMUI + CH]
        I2 = consts[:, C_I2:C_I2 + CH]
        HALVES = ((0, slice(0, CH)), (1, slice(CH, P)))

        def f3(ap):
            return ap.rearrange("p (h j) -> p h j", h=NH)

        def hc_(hh):
            return slice(hh * CH, (hh + 1) * CH)

        def wy(pr):
            s = pr
            hb = pr % 2
            ATi, Kd, ek, sdecp = ATi_h[hb], Kd_h[hb], ek_h[hb], sdecp_h[hb]
            Ut0, Ut1 = Ut_h[hb]
            WTs = WT_h[hb]
            g_s = graw[:, s, :]
            b_s = beta[:, s, :]
            yield
            cp("dve", f3(G1), bc(g_s.unsqueeze(2), [P, NH, CH]), (grawB,), (G1B[0],))
            tt("dve", f3(G2), bc(g_s.unsqueeze(2), [P, NH, CH]), bc(NEGTRI.unsqueeze(1), [P, NH, CH]),
               ALU.mult, (grawB, constsB), (G2B[0],))
            pd, pdb = ps_next()
            pm, pmb = ps_next()
            for hf, sl in HALVES:
                mm(pd[sl, :], TRIKI[sl, :], G1[sl, :], True, False, (constsB, G1B[0]), (pdb,))
                mm(pd[sl, :], ones_f[sl, 0:CH], G2[sl, :], False, True, (constsB, G2B[0]), (pdb,))
            for hf, sl in HALVES:
                mm(pm[sl, 0:8], TRIKI[sl, :], graw[sl, s, :], True, True, (constsB, grawB), (pmb,))
                mm(pm[sl, 8:16], UPS[sl, :], graw[sl, s, :], True, True, (constsB, grawB), (pmb,))
                mm(pm[:, 16 + 8 * hf:24 + 8 * hf], ones_f[sl, :], graw[sl, s, :], True, True, (constsB, grawB), (pmb,))
            act(ek[:, :], pm[:, 0:16], AF.Exp, (pmb,), (ekB[hb],))
            act(sdecp[:, :], pm[:, 16:32], AF.Exp, (pmb,), (sdecB[hb],))
            stt(f3(E1), f3(pd[:]), 0.0, bc(NMLS.unsqueeze(1), [P, NH, CH]), ALU.min, ALU.add, (pdb, constsB), (E1B[0],))
            act(E1, E1, AF.Exp, (E1B[0],), (E1B[0],))
            stt(f3(E2), f3(pd[:]), 0.0, bc(PMUI.unsqueeze(1), [P, NH, CH]), ALU.max, ALU.add, (pdb, constsB), (E2B[0],))
            act(E2, E2, AF.Exp, (E2B[0],), (E2B[0],), scale=-1.0)
            yield
            pk, pkb = ps_next()
            pq, pqb = ps_next()
            for hh in range(NH):
                for hf, sl in HALVES:
                    cs_ = slice((2 * pr + hf) * CH, (2 * pr + hf + 1) * CH)
                    mm(pk[sl, hc_(hh)], kTd[:, hh, cs_], kTd[:, hh, cs_], True, True, (kTdB[hh],), (pkb,))
            for hh in range(NH):
                for hf, sl in HALVES:
                    cs_ = slice((2 * pr + hf) * CH, (2 * pr + hf + 1) * CH)
                    mm(pq[sl, hc_(hh)], kTd[:, hh, cs_], qTd[:, hh, cs_], True, True, (kTdB[hh], qTdB[hh]), (pqb,))
            tt("dve", Lc, pk[:], E1, ALU.mult, (pkb, E1B[0]), (LcB[0],))
            tt("dve", f3(Lc), f3(Lc), bc(b_s.unsqueeze(2), [P, NH, CH]), ALU.mult, (LcB[0], betaB), (LcB[0],))
            tt("dve", ATi, pq[:], E2, ALU.mult, (pqb, E2B[0]), (ATiB[hb],))
            yield
            pa, pab_ = ps_next()
            for hh in range(NH):
                for hf, sl in HALVES:
                    mm(pa[sl, hc_(hh)], Lc[sl, hc_(hh)], I2[sl, :], True, True, (LcB[0], constsB), (pab_,))
            cp("dve", Am, pa[:], (pab_,), (AB_[0],))
            stt(f3(Pm), f3(Am), -1.0, bc(I2.unsqueeze(1), [P, NH, CH]), ALU.mult, ALU.add, (AB_[0], constsB), (PB_[0],))
            yield
            p1, p1b = ps_next()
            p2, p2b = ps_next()
            for hh in range(NH):
                for hf, sl in HALVES:
                    mm(p1[sl, hc_(hh)], Lc[sl, hc_(hh)], Am[sl, hc_(hh)], True, True, (LcB[0], AB_[0]), (p1b,))
            for hh in range(NH):
                for hf, sl in HALVES:
                    mm(p2[sl, hc_(hh)], Am[sl, hc_(hh)], Lc[sl, hc_(hh)], True, True, (LcB[0], AB_[0]), (p2b,))
            cp("dve", Am, p1[:], (p1b,), (AB_[0],))
            cp("dve", Lc, p2[:], (p2b,), (LcB[0],))
            yield
            for lvl in range(1, 5):
                pA, pAb = ps_next()
                pP, pPb = ps_next()
                pl, plb = ps_next()
                for hh in range(NH):
                    for hf, sl in HALVES:
                        mm(pA[sl, hc_(hh)], Lc[sl, hc_(hh)], Am[sl, hc_(hh)], True, True, (LcB[0], AB_[0]), (pAb,))
                for hh in range(NH):
                    for hf, sl in HALVES:
                        mm(pP[sl, hc_(hh)], Lc[sl, hc_(hh)], Pm[sl, hc_(hh)], True, True, (LcB[0], PB_[0]), (pPb,))
                for hh in range(NH):
                    for hf, sl in HALVES:
                        mm(pl[sl, hc_(hh)], Am[sl, hc_(hh)], Lc[sl, hc_(hh)], True, True, (LcB[0], AB_[0]), (plb,))
                tt("dve", Pm, pP[:], Pm, ALU.add, (pPb, PB_[0]), (PB_[0],))
                cp("dve", Am, pA[:], (pAb,), (AB_[0],))
                cp("dve", Lc, pl[:], (plb,), (LcB[0],))
                yield
            yield
            p5, p5b = ps_next()
            for hh in range(NH):
                for hf, sl in HALVES:
                    mm(p5[sl, hc_(hh)], Lc[sl, hc_(hh)], Pm[sl, hc_(hh)], True, True, (LcB[0], PB_[0]), (p5b,))
            tt("dve", MT, p5[:], Pm, ALU.add, (p5b, PB_[0]), (MTB[0],))
            yield
            h3 = lambda ap: ap.rearrange("p (h d) -> p h d", h=NH)
            tt("dve", h3(Vb), vtok[:, s, :, :], bc(b_s.unsqueeze(2), [P, NH, HD]), ALU.mult, (vtokB, betaB), (VbB[0],))
            tt("dve", bk[:, :], b_s, ek[:, 0:8], ALU.mult, (betaB, ekB[hb]), (bkB[0],))
            tt("dve", h3(Kbg), ktok[:, s, :, :], bc(bk[:, :].unsqueeze(2), [P, NH, HD]), ALU.mult, (ktokB, bkB[0]), (KbgB[0],))
            tt("dve", h3(Kd), ktok[:, s, :, :], bc(ek[:, 8:16].unsqueeze(2), [P, NH, HD]), ALU.mult, (ktokB, ekB[hb]), (KdB[hb],))
            yield
            pua, puab = ps_next()
            pub_, pubb = ps_next()
            pws = [ps_next(), ps_next()]
            for hh in range(NH):
                px, pxB_ = (pua, puab) if hh < 4 else (pub_, pubb)
                o0 = (hh % 4) * HD
                for hf, sl in HALVES:
                    mm(px[sl, o0:o0 + HD], MT[sl, hc_(hh)], Vb[sl, hh * HD:(hh + 1) * HD], True, True,
                       (MTB[0], VbB[0]), (pxB_,))
            for hh in range(NH):
                for hf, sl in HALVES:
                    mm(pws[hf][0][:, hc_(hh)], Kbg[sl, hh * HD:(hh + 1) * HD], MT[sl, hc_(hh)], True, True,
                       (KbgB[0], MTB[0]), (pws[hf][1],))
            cp("dve", Ut0, pua[:], (puab,), (UtB[hb],))
            cp("dve", Ut1, pub_[:], (pubb,), (UtB[hb],))
            for hf, sl in HALVES:
                cp("act", WTs[hf], pws[hf][0][:], (pws[hf][1],), (WTBh[hb][hf],))
        def seq(pr):
            s = pr
            hb = pr % 2
            ATi, Kd, ek, sdecp = ATi_h[hb], Kd_h[hb], ek_h[hb], sdecp_h[hb]
            Ut0, Ut1 = Ut_h[hb]
            WTs = WT_h[hb]
            for hf, sl in HALVES:
                c = 2 * pr + hf
                cs_ = slice(c * CH, (c + 1) * CH)
                WT = WTs[hf]
                pwa, pwab = ps_next()
                pwb2, pwbb2 = ps_next()
                for hh in range(NH):
                    px, pxB_ = (pwa, pwab) if hh < 4 else (pwb2, pwbb2)
                    o0 = (hh % 4) * HD
                    mm(px[sl, o0:o0 + HD], WT[:, hc_(hh)], Sb[:, hh, :], True, True, (WTBh[hb][hf], SbB), (pxB_,))
                tt("dve", vn[sl, 0:512], Ut0[sl, :], pwa[sl, :], ALU.subtract, (UtB[hb], pwab), (vnB[hf],))
                tt("dve", vn[sl, 512:1024], Ut1[sl, :], pwb2[sl, :], ALU.subtract, (UtB[hb], pwbb2), (vnB[hf],))
                yield
                pqa, pqab = ps_next()
                pqb2, pqbb2 = ps_next()
                for hh in range(NH):
                    px, pxB_ = (pqa, pqab) if hh < 4 else (pqb2, pqbb2)
                    o0 = (hh % 4) * HD
                    mm(px[sl, o0:o0 + HD], qTd[:, hh, cs_], Sb[:, hh, :], True, True, (qTdB[hh], SbB), (pxB_,))
                pva, pvab = ps_next()
                pvb2, pvbb2 = ps_next()
                for hh in range(NH):
                    px, pxB_ = (pva, pvab) if hh < 4 else (pvb2, pvbb2)
                    o0 = (hh % 4) * HD
                    mm(px[sl, o0:o0 + HD], ATi[sl, hc_(hh)], vn[sl, hh * HD:(hh + 1) * HD], True, True,
                       (ATiB[hb], vnB[hf]), (pxB_,))
                for half, (pqx, pqxb, pvx, pvxb) in enumerate(((pqa, pqab, pva, pvab), (pqb2, pqbb2, pvb2, pvbb2))):
                    cols = slice(half * 512, half * 512 + 512)
                    eg = bc(ek[sl, half * 4:half * 4 + 4].unsqueeze(2), [CH, 4, HD])
                    tt("dve", otmp[sl, cols].rearrange("p (h d) -> p h d", h=4),
                       pqx[sl, :].rearrange("p (h d) -> p h d", h=4), eg, ALU.mult, (pqxb, ekB[hb]), (otmpB[hf],))
                    tt("dve", osub[sl, cols], otmp[sl, cols], pvx[sl, :], ALU.add, (otmpB[hf], pvxb), (osubB[hf],))
                yield
                psa, psab = ps_next()
                psb2, psbb2 = ps_next()
                for hh in range(NH):
                    px, pxB_ = (psa, psab) if hh < 4 else (psb2, psbb2)
                    o0 = (hh % 4) * HD
                    mm(px[:, o0:o0 + HD], Kd[sl, hh * HD:(hh + 1) * HD], vn[sl, hh * HD:(hh + 1) * HD], True, True,
                       (KdB[hb], vnB[hf]), (pxB_,))
                tt("pool", Sf[:], Sf[:], bc(sdecp[:, 8 * hf:8 * hf + 8].unsqueeze(2), [P, NH, HD]), ALU.mult,
                   (SfB, sdecB[hb]), (SfB,))
                Sf2 = Sf[:].rearrange("p h d -> p (h d)")
                tt("dve", Sf2[:, 0:512], Sf2[:, 0:512], psa[:], ALU.add, (SfB, psab), (SfB,))
                tt("dve", Sf2[:, 512:1024], Sf2[:, 512:1024], psb2[:], ALU.add, (SfB, psbb2), (SfB,))
                cp("act", Sb[:], Sf[:], (SfB,), (SbB,))
                yield
            o3 = osub.rearrange("p (h d) -> p h d", h=NH)
            tt("dve", sqo, osub, osub, ALU.mult, (osubB[0], osubB[1]), (sqoB,))
            S.add("dve", lambda e: e.tensor_reduce(oss[:], sqo.rearrange("p (h d) -> p h d", h=NH), AX.X, ALU.add),
                  (sqoB,), (ossB,))
            ts("dve", oss[:], oss[:], 1.0 / HD, RMS_EPS, ALU.mult, ALU.add, (ossB,), (ossB,))
            act(oss[:], oss[:], AF.Sqrt, (ossB,), (ossB,))
            S.add("dve", lambda e: e.reciprocal(oss[:], oss[:]), (ossB,), (ossB,))
            tt("dve", o3, o3, bc(oss[:].unsqueeze(2), [P, NH, HD]), ALU.mult, (osubB[0], osubB[1], ossB), (osubB[0], osubB[1]))
            tt("dve", o3, o3, bc(small[:, SM_ONORM:SM_ONORM + HD].unsqueeze(1), [P, NH, HD]), ALU.mult,
               (osubB[0], osubB[1], smallB), (osubB[0], osubB[1]))
            tt("dve", ob, osub, zs[:, s, :], ALU.mult, (osubB[0], osubB[1], zsB), (obB,))
            for g4 in range(2):
                pt, pb = ps_next()
                pv = pt[:].bitcast(BF16)
                for j in range(4):
                    hh = g4 * 4 + j
                    tr(pv[:, j * P:(j + 1) * P], ob[:, hh * HD:(hh + 1) * HD], ident_bf[:], (obB, cbfB), (pb,))
                for j in range(4):
                    hh = g4 * 4 + j
                    cp("act", omixT[:, hh, s * P:(s + 1) * P], pv[:, j * P:(j + 1) * P], (pb,), (omixB[hh],))

        ringW, ringS = Ring([0, 1, 2, 3]), Ring([4, 5, 6, 7])
        NPR = TT // CH // 2
        cur_ring[0] = ringW
        for _ in wy(0):
            pass
        for pr in range(NPR):
            gens = [(seq(pr), ringS)]
            if pr + 1 < NPR:
                gens.append((wy(pr + 1), ringW))
            run_interleaved(gens)
        cur_ring[0] = ring_all

    def out_proj():
        wv = wout.rearrange("(k p) c -> p k c", p=P)
        for cb in range(D // 512):
            pss = proj_tok(wv, KD, cb * 512, lambda k, s: omixT[:, k, s * P:(s + 1) * P], lambda k: (omixB[k],))
            for s in range(NS):
                hv = h[s][:, cb * 512:(cb + 1) * 512]
                tt("dve", hv, pss[s][0][:], hv, ALU.add, (pss[s][1], hB[s]), (hB[s],))

    def ple(it):
        t0 = it * TT
        S.add("sp", lambda e: e.dma_start(out=ptok[:], in_=p_d[t0:t0 + TT, :].rearrange("(s p) c -> p s c", p=P)),
              (), (ptokB,), dma="ptok")
        cp("dve", ptokb[:], ptok[:], (ptokB,), (ptokbB,))
        for s in range(NS):
            pt, pb = ps_next()
            pv = pt[:].bitcast(BF16)
            for k in range(2):
                tr(pv[:, k * P:(k + 1) * P], ptokb[:, s, k * P:(k + 1) * P], ident_bf[:], (ptokbB, cbfB), (pb,))
            cp("dve", pT[:, :, s * P:(s + 1) * P], pv[:, 0:2 * P].rearrange("p (k t) -> p k t", k=2), (pb,), (pTB,))
        wgv = wpg.rearrange("(k p) c -> p k c", p=P)
        wpv = wpp.rearrange("(k p) c -> p k c", p=P)
        for cb in range(D // 512):
            psg = proj_tok(wgv, KD, cb * 512, lambda k, s: nT[:, k, s * P:(s + 1) * P], lambda k: (nTB,))
            psp = proj_tok(wpv, 2, cb * 512, lambda k, s: pT[:, k, s * P:(s + 1) * P], lambda k: (pTB,))
            for s in range(NS):
                i = sgctr[0] % NSG
                sgctr[0] += 1
                act(sg[i][:], psg[s][0][:], AF.Sigmoid, (psg[s][1],), (sgB[i],))
                tt("dve", sg[i][:], sg[i][:], psp[s][0][:], ALU.mult, (sgB[i], psp[s][1]), (sgB[i],))
                hv = h[s][:, cb * 512:(cb + 1) * 512]
                tt("dve", hv, hv, sg[i][:], ALU.add, (hB[s], sgB[i]), (hB[s],))

    def final_norm_store(it):
        t0 = it * TT
        S.add("sp", lambda e: e.dma_start(out=finbc, in_=fin_d), (), (finbcB,), dma="finbc")
        sum_squares()
        row_rstd(NS)
        for s in range(NS):
            stt(h[s][:], h[s][:], rstd[:, s:s + 1], finbc, ALU.mult, ALU.mult, (hB[s], rstdB, finbcB), (hB[s],))
            S.add("sp", lambda e, s=s: e.dma_start(out=out_d[t0 + s * P:t0 + (s + 1) * P, :], in_=h[s][:]),
                  (hB[s],), (), dma=f"o{s}")

    for it in range(NT):
        t0 = it * TT
        for s in range(NS):
            S.add("sp", lambda e, s=s, t0=t0: e.dma_start(out=h[s][:], in_=x_d[t0 + s * P:t0 + (s + 1) * P, :]),
                  (), (hB[s],), dma=f"x{s}")
        rmsnorm_to_nT(0)
        ffn(w1gu, w1d)
        if stages == "ffn1":
            for s in range(NS):
                S.add("sp", lambda e, s=s, t0=t0: e.dma_start(out=out_d[t0 + s * P:t0 + (s + 1) * P, :], in_=h[s][:]),
                      (hB[s],), (), dma=f"o{s}")
            continue
        barrier()
        rmsnorm_to_nT(1)
        if stages != "b1":
            sb_phase(it)
        barrier()
        if stages in ("b1", "b2", "b3", "b4"):
            for s in range(NS):
                S.add("sp", lambda e, s=s, t0=t0: e.dma_start(out=out_d[t0 + s * P:t0 + (s + 1) * P, :], in_=h[s][:]),
                      (hB[s],), (), dma=f"o{s}")
            continue
        if stages in ("nodn", "nodnmix"):
            for c in range(8):
                S.add("pool", lambda e, c=c: e.memset(omixT[:, c, :], 0.0), (), (omixB[c],))
        else:
            dn_inproj()
            barrier()
            if dlev > 0:
                dn_chunks()
            if dlev < 99:
                barrier()
                for c in range(8):
                    S.add("pool", lambda e, c=c: e.memset(omixT[:, c, :], 0.0), (), (omixB[c],))
        out_proj()
        barrier()
        if stages in ("mix", "nodnmix") or dlev < 99:
            for s in range(NS):
                S.add("sp", lambda e, s=s, t0=t0: e.dma_start(out=out_d[t0 + s * P:t0 + (s + 1) * P, :], in_=h[s][:]),
                      (hB[s],), (), dma=f"o{s}")
            continue
        rmsnorm_to_nT(2)
        ffn(w2gu, w2d)
        barrier()
        rmsnorm_to_nT(3)
        ple(it)
        final_norm_store(it)
        barrier()

    last = {}
    for o in S.q["sp"]:
        if o.dma and o.dma[0].startswith("o"):
            last[o.dma[0]] = o
    endop = Op("sp", lambda e: e.nop(), None)
    for o in last.values():
        endop.deps.append(o)
    S.q["sp"].append(endop)

    with nc.Block() as block:
        S.emit(nc, es, block)
    es.close()
    return nc


def make_small(inp):
    sm = np.zeros((P, SMALL_COLS), np.float32)
    for i, k in enumerate(("ffn1_norm", "mix_norm", "ffn2_norm", "ple_norm")):
        sm[:, SM_GAMMA + i * KD: SM_GAMMA + (i + 1) * KD] = np.asarray(inp[k], np.float32).reshape(KD, P).T
    cw = np.asarray(inp["dn_conv"], np.float32).reshape(4, 24, P)
    sm[:, SM_CONV:SM_CONV + 96] = cw.transpose(2, 1, 0).reshape(P, 96)
    sm[:, SM_ALOG:SM_ALOG + 8] = np.asarray(inp["dn_a_log"], np.float32).reshape(1, 8)
    sm[:, SM_DTB:SM_DTB + 8] = np.asarray(inp["dn_dt_bias"], np.float32).reshape(1, 8)
    sm[:, SM_ONORM:SM_ONORM + 128] = np.asarray(inp["dn_out_norm"], np.float32).reshape(1, 128)
    return sm


def make_consts():
    c = np.zeros((P, CONST_COLS), np.float32)
    c[:, C_IDENT:C_IDENT + P] = np.eye(P, dtype=np.float32)
    q = np.arange(P)[:, None]
    k = np.arange(P)[None, :]
    c[:, C_MASKS:C_MASKS + P] = (k < q)
    c[:, C_ONES:C_ONES + P] = 1.0
    pm = (np.arange(P) % CH)[:, None]
    j = np.arange(CH)[None, :]
    c[:, C_TRIKI:C_TRIKI + CH] = (pm <= j)
    c[:, C_UPS:C_UPS + CH] = (pm > j)
    c[:, C_NEGTRI:C_NEGTRI + CH] = -(pm <= j).astype(np.float32)
    c[:, C_NMLS:C_NMLS + CH] = np.where(pm > j, 0.0, NEG)
    c[:, C_PMUI:C_PMUI + CH] = np.where(j >= pm, 0.0, -NEG)
    c[:, C_I2:C_I2 + CH] = (pm == j)
    c[:, C_NEGM:C_NEGM + P] = np.where(k < q, 0.0, NEGB)
    return c


_NC_CACHE = {}


def run(inputs, T=2048, ncores=8, stages="all"):
    import os
    key = (T, stages)
    if key not in _NC_CACHE:
        _NC_CACHE[key] = build(T, stages)
    nc = _NC_CACHE[key]
    sm = make_small(inputs)
    cs = make_consts()
    fin_bc = np.ascontiguousarray(np.broadcast_to(np.asarray(inputs["final_norm"], np.float32).reshape(1, D), (P, D)))
    shared = {
        "ffn1_w_gu": np.ascontiguousarray(inputs["ffn1_w_gu"][0]),
        "ffn1_w_down": np.ascontiguousarray(inputs["ffn1_w_down"][0]),
        "w_in": np.ascontiguousarray(inputs["w_in"][0]),
        "w_out": np.ascontiguousarray(inputs["w_out"][0]),
        "ffn2_w_gu": np.ascontiguousarray(inputs["ffn2_w_gu"][0]),
        "ffn2_w_down": np.ascontiguousarray(inputs["ffn2_w_down"][0]),
        "ple_w_gate": np.ascontiguousarray(inputs["ple_w_gate"][0]),
        "ple_w_proj": np.ascontiguousarray(inputs["ple_w_proj"][0]),
        "small": sm, "consts": cs, "final_bc": fin_bc,
    }
    w_in0 = np.asarray(inputs["w_in"][0], np.float32)
    cols = np.concatenate([np.arange(0, 3072), np.arange(4112, 6160)])
    wf = w_in0[:, cols].reshape(KD, P, 20, 256).transpose(2, 1, 0, 3)
    shared["w_in_feat"] = np.ascontiguousarray(wf).reshape(20, P, KD * 256)
    wab = w_in0[:, 4096:4112].reshape(KD, P, 16).transpose(1, 0, 2)
    shared["w_in_ab"] = np.ascontiguousarray(wab).reshape(P, KD * 16)
    in_maps = []
    for c in range(ncores):
        m = dict(shared)
        m["x"] = np.ascontiguousarray(inputs["x"][c, :T])
        m["p"] = np.ascontiguousarray(inputs["p"][0, c, :T])
        in_maps.append(m)
    trace = bool(os.environ.get("K_TRACE"))
    res = run_bass_kernel_spmd(nc, in_maps, core_ids=list(range(ncores)), trace=trace)
    if trace:
        print("exec_time_ns", res.exec_time_ns)
    return np.stack([np.asarray(r["out"]) for r in res.results], axis=0)


def kernel(**inputs):
    inputs = {k: np.asarray(v) for k, v in inputs.items()}
    return run(inputs, T=2048, ncores=8).astype(np.float32)
```

```python
import numpy as np
from contextlib import ExitStack
import concourse.bass as bass
import concourse.mybir as mybir
from concourse.bass_utils import run_bass_kernel_spmd

F32 = mybir.dt.float32
BF16 = mybir.dt.bfloat16
AF = mybir.ActivationFunctionType
ALU = mybir.AluOpType
AX = mybir.AxisListType

P = 128
D = 2048
F = 5632
KD = D // P
KF = F // P
TT = 512
NS = TT // P
NH = 8
HD = 128
CH = 64
IN_COLS = 7184
PLE = 256
RMS_EPS = 1e-6
L2_EPS = 1e-6
NEG = -30000.0

ENGS = ("pe", "act", "dve", "pool", "sp")


class Buf:
    __slots__ = ("name", "w", "r")

    def __init__(self, name):
        self.name = name
        self.w = []
        self.r = []


class Op:
    __slots__ = ("eng", "fn", "dma", "deps", "needed", "val")

    def __init__(self, eng, fn, dma):
        self.eng = eng
        self.fn = fn
        self.dma = dma
        self.deps = []
        self.needed = False
        self.val = 0

    def key(self):
        return ("d", self.dma[0]) if self.dma else ("e", self.eng)


def _push(lst, o):
    k = o.key()
    lst[:] = [x for x in lst if x.key() != k]
    lst.append(o)


class Sched:
    def __init__(self):
        self.q = {e: [] for e in ENGS}
        self.dcnt = {}

    def add(self, eng, fn, reads=(), writes=(), dma=None):
        if dma is not None:
            c = self.dcnt.get(dma, 0) + 16
            self.dcnt[dma] = c
            o = Op(eng, fn, (dma, c))
        else:
            o = Op(eng, fn, None)
        deps = []
        for b in reads:
            for x in b.w:
                if x.dma or o.dma or x.eng != o.eng or o.eng != "pe":
                    deps.append(x)
        for b in writes:
            for x in b.r + b.w:
                if x.dma or o.dma or x.eng != o.eng:
                    deps.append(x)
        for b in reads:
            _push(b.r, o)
        for b in writes:
            if b.r and not (len(b.r) == 1 and b.r[0] is o):
                b.w = [o]
                b.r = [x for x in b.r if x is o]
            else:
                _push(b.w, o)
        for x in deps:
            if x is not o:
                x.needed = True
                o.deps.append(x)
        self.q[eng].append(o)
        return o

    def emit(self, nc, es, block):
        sems = {}
        for e in ENGS:
            sems[("e", e)] = es.enter_context(nc.semaphore("s_" + e))
            c = 0
            for o in self.q[e]:
                if o.dma is None and o.needed:
                    c += 1
                    o.val = c
        for d in self.dcnt:
            sems[("d", d)] = es.enter_context(nc.semaphore("d_" + d))

        def body(ename):
            def run(eng):
                seen = {}
                for o in self.q[ename]:
                    need = {}
                    for x in o.deps:
                        k = x.key()
                        v = x.dma[1] if x.dma else x.val
                        if seen.get(k, 0) >= v:
                            continue
                        if need.get(k, 0) < v:
                            need[k] = v
                    for k, v in need.items():
                        seen[k] = v
                        eng.wait_ge(sems[k], v)
                    ins = o.fn(eng)
                    if o.dma:
                        ins.then_inc(sems[o.key()], 16)
                    elif o.needed:
                        ins.then_inc(sems[o.key()], 1)
            return run

        block.tensor(body("pe"))
        block.scalar(body("act"))
        block.vector(body("dve"))
        block.gpsimd(body("pool"))
        block.sync(body("sp"))


SM_GAMMA = 0
SM_CONV = 64
SM_ALOG = 160
SM_DTB = 168
SM_ONORM = 176
SMALL_COLS = 304

C_IDENT = 0
C_MASKS = 128
C_ONES = 256
C_TRIKI = 384
C_UPS = 448
C_NEGTRI = 512
C_NMLS = 576
C_PMUI = 640
C_ZERO = 704
C_I2 = 705
C_NEGM = 769
CONST_COLS = 897
NEGB = -30000.0 * (128.0 ** 0.5)

KB = 1024
import os
DSUB = int(os.environ.get("DSUB", "0"))


def build(T, stages="all"):
    NT = T // TT
    nc = bass.Bass("TRN2", target_bir_lowering=False)

    def din(name, shape):
        return nc.dram_tensor(name, list(shape), F32, kind="ExternalInput").ap()

    x_d = din("x", [T, D])
    p_d = din("p", [T, PLE])
    w1gu = din("ffn1_w_gu", [D, 2 * F])
    w1d = din("ffn1_w_down", [F, D])
    win = din("w_in", [D, IN_COLS])
    winf = din("w_in_feat", [20, P, KD * 256])
    winab = din("w_in_ab", [P, KD * 16])
    wout = din("w_out", [D, D])
    w2gu = din("ffn2_w_gu", [D, 2 * F])
    w2d = din("ffn2_w_down", [F, D])
    wpg = din("ple_w_gate", [D, D])
    wpp = din("ple_w_proj", [PLE, D])
    small_d = din("small", [P, SMALL_COLS])
    const_d = din("consts", [P, CONST_COLS])
    fin_d = din("final_bc", [P, D])
    out_d = nc.dram_tensor("out", [T, D], F32, kind="ExternalOutput").ap()
    ksb_d = nc.dram_tensor("ksb_scr", [NH, P, T], BF16, kind="Internal").ap()
    vsb_d = nc.dram_tensor("vsb_scr", [T, NH * HD], BF16, kind="Internal").ap()
    ksbB = [Buf(f"ksbd{hh}") for hh in range(NH)]
    vsbB = Buf("vsbd")

    S = Sched()
    es = ExitStack()

    def sb(name, shape, dt=F32):
        return es.enter_context(nc.sbuf_tensor("sb_" + name, list(shape), dt))

    h = [sb(f"h{s}", [P, D]) for s in range(NS)]
    hB = [Buf(f"h{s}") for s in range(NS)]
    nT = sb("nT", [P, KD, TT], BF16)
    nTB = Buf("nT")
    small = sb("small", [P, SMALL_COLS])
    smallB = Buf("small")
    consts = sb("consts", [P, CONST_COLS])
    constsB = Buf("consts")
    ident_bf = sb("ident_bf", [P, P], BF16)
    maskS_bf = sb("maskS_bf", [P, P], BF16)
    ones_bf = sb("ones_bf", [P, P], BF16)
    negm_bf = sb("negm_bf", [P, P], BF16)
    cbfB = Buf("cbf")
    NW = 4
    WSZ = 4096
    wslots = [sb(f"w{i}", [P, WSZ], BF16) for i in range(NW)]
    wB = [Buf(f"w{i}") for i in range(NW)]
    wctr = [0]
    ssq = sb("ssq", [P, 8])
    ssqB = Buf("ssq")
    rstd = sb("rstd", [P, 8])
    rstdB = Buf("rstd")
    NSG = 4
    sg = [sb(f"sg{i}", [P, TT]) for i in range(NSG)]
    sgB = [Buf(f"sg{i}") for i in range(NSG)]
    sgctr = [0]
    Sf = sb("Sf", [P, NH, HD])
    SfB = Buf("Sf")
    Sb = sb("Sb", [P, NH, HD], BF16)
    SbB = Buf("Sb")
    halo = sb("halo", [P, 24, 3])
    haloB = Buf("halo")
    negA = sb("negA", [P, 8])
    negAB = Buf("negA")
    graw = sb("graw", [P, NS, 8])
    grawB = Buf("graw")
    gtmp = sb("gtmp", [P, NS, 8])
    gtmpB = Buf("gtmp")
    beta = sb("beta", [P, NS, 8])
    betaB = Buf("beta")
    ek = sb("ek", [P, 16])
    ekB = [Buf("ek0"), Buf("ek1")]
    bk = sb("bk", [P, 8])
    bkB = [Buf("bk0"), Buf("bk1")]
    sdecp = sb("sdecp", [P, 16])
    sdecB = [Buf("sdec0"), Buf("sdec1")]
    oss = sb("oss", [P, 8])
    ossB = Buf("oss")
    ntot = sb("ntot", [P, 4])
    ntotB = Buf("ntot")

    ARENA_KB = 100
    arena = sb("arena", [P, ARENA_KB * KB // 4])

    def av(off_b, nbytes, dt, shape=None):
        a = arena[:, off_b // 4:(off_b + nbytes) // 4]
        if dt is BF16:
            a = a.bitcast(BF16)
        return a

    hid = av(0, 44 * KB, BF16).rearrange("p (f t) -> p f t", f=KF)
    hidB = [Buf(f"hid{f}") for f in range(KF)]
    hs = [av((88 + 4 * i) * KB, 4 * KB, BF16) for i in range(2)]
    hsB = [Buf(f"hs{i}") for i in range(2)]
    hsctr = [0]
    junk = av(96 * KB, 4 * KB, BF16)
    junkB = Buf("junk")
    omixT = av(0, 16 * KB, BF16).rearrange("p (c t) -> p c t", c=16)
    omixB = [Buf(f"omix{c}") for c in range(16)]
    zs = av(16 * KB, 8 * KB, BF16).rearrange("p (s c) -> p s c", s=NS)
    zsB = Buf("zs")
    finbc = av(0, 8 * KB, F32)
    finbcBs = tuple(hidB[0:8])
    pT = av(8 * KB, 1 * KB * 2, BF16).rearrange("p (k t) -> p k t", k=2)
    pTBs = tuple(hidB[8:10])
    ptok = av(10 * KB, 4 * KB, F32).rearrange("p (s c) -> p s c", s=NS)
    ptokBs = tuple(hidB[10:14])
    ptokb = av(14 * KB, 2 * KB, BF16).rearrange("p (s c) -> p s c", s=NS)
    ptokbBs = tuple(hidB[14:16])
    qTs = av(24 * KB, 8 * KB, BF16).rearrange("p (h t) -> p h t", h=NH)
    qTsB = [Buf(f"qTs{i}") for i in range(NH)]
    kstage = [av((32 + i) * KB, 1 * KB, BF16) for i in range(2)]
    kstageB = [Buf(f"kst{i}") for i in range(2)]
    vstage = [av((34 + 2 * i) * KB, 2 * KB, BF16) for i in range(2)]
    vstageB = [Buf(f"vst{i}") for i in range(2)]
    NPAR = 4
    et = [av((38 + 2 * i) * KB, 2 * KB, F32) for i in range(NPAR)]
    etB = [Buf(f"et{i}") for i in range(NPAR)]
    spt = [av((46 + 2 * i) * KB, 2 * KB, F32) for i in range(NPAR)]
    sptB = [Buf(f"spt{i}") for i in range(NPAR)]
    Pb = [av((54 + 3 * i) * KB, 3 * KB, F32) for i in range(NPAR)]
    PbB = [Buf(f"Pb{i}") for i in range(NPAR)]
    lb = [av((66 + 2 * i) * KB, 2 * KB, F32) for i in range(NPAR)]
    lbB = [Buf(f"lb{i}") for i in range(NPAR)]
    aa = [av((74 + i) * KB, 1 * KB, BF16) for i in range(NPAR)]
    aaB = [Buf(f"aa{i}") for i in range(NPAR)]
    aT = [av((78 + i) * KB, 1 * KB, BF16).rearrange("p (k q) -> p k q", k=4) for i in range(NPAR)]
    aTB = [Buf(f"aT{i}") for i in range(NPAR)]
    ktl = [av((82 + 4 * i) * KB, 4 * KB, BF16) for i in range(2)]
    ktlB = [Buf(f"ktl{i}") for i in range(2)]
    vtl = [av((90 + 4 * i) * KB, 4 * KB, BF16).rearrange("p (k d) -> p k d", k=16) for i in range(2)]
    vtlB = [Buf(f"vtl{i}") for i in range(2)]
    ntotB2 = [Buf(f"ntot{i}") for i in range(NPAR)]
    qTd = av(24 * KB, 8 * KB, BF16).rearrange("p (h t) -> p h t", h=NH)
    qTdB = [Buf(f"qTd{i}") for i in range(NH)]
    kTd = av(32 * KB, 8 * KB, BF16).rearrange("p (h t) -> p h t", h=NH)
    kTdB = [Buf(f"kTd{i}") for i in range(NH)]
    ktok = av(40 * KB, 8 * KB, BF16).rearrange("p (s h d) -> p s h d", s=NS, h=NH)
    ktokB = Buf("ktok")
    vtok = av(48 * KB, 8 * KB, BF16).rearrange("p (s h d) -> p s h d", s=NS, h=NH)
    vtokB = Buf("vtok")
    NSET = 6
    SETB = 7 * KB
    raw = [av(56 * KB + SETB * i, 2560, F32) for i in range(NSET)]
    rawB = [Buf(f"raw{i}") for i in range(NSET)]
    cacc = [av(56 * KB + SETB * i + 2560, 2 * KB, F32) for i in range(NSET)]
    caccB = [Buf(f"cacc{i}") for i in range(NSET)]
    csil = cacc
    csilB = caccB
    sqb = [av(56 * KB + SETB * i + 2560 + 2 * KB, 1 * KB, BF16) for i in range(NSET)]
    sqbB = [Buf(f"sqb{i}") for i in range(NSET)]
    rinv = [raw[i][:, 0:TT] for i in range(NSET)]
    rinvB = rawB
    G1 = av(56 * KB, 2 * KB, F32)
    G1B = [Buf("G10"), Buf("G11")]
    G2 = av(58 * KB, 2 * KB, F32)
    G2B = [Buf("G20"), Buf("G21")]
    E1 = av(60 * KB, 2 * KB, F32)
    E1B = [Buf("E10"), Buf("E11")]
    E2 = av(62 * KB, 2 * KB, F32)
    E2B = [Buf("E20"), Buf("E21")]
    Lc = av(64 * KB, 2 * KB, F32)
    LcB = [Buf("Lc0"), Buf("Lc1")]
    Am = av(66 * KB, 2 * KB, F32)
    Pm = av(68 * KB, 2 * KB, F32)
    AB_ = [Buf("A0"), Buf("A1")]
    PB_ = [Buf("Pm0"), Buf("Pm1")]
    ATi = av(70 * KB, 1 * KB, BF16)
    ATiB = [Buf("ATi0"), Buf("ATi1")]
    MT = av(71 * KB, 1 * KB, BF16)
    MTB = [Buf("MT0"), Buf("MT1")]
    Vb = av(72 * KB, 2 * KB, BF16)
    VbB = [Buf("Vb0"), Buf("Vb1")]
    Kbg = av(74 * KB, 2 * KB, BF16)
    KbgB = [Buf("Kbg0"), Buf("Kbg1")]
    Kd = av(76 * KB, 2 * KB, BF16)
    KdB = [Buf("Kd0"), Buf("Kd1")]
    Ut = av(78 * KB, 4 * KB, F32)
    UtB = [Buf("U0"), Buf("U1")]
    WT2 = [av(82 * KB, 1 * KB, BF16), av(99 * KB, 1 * KB, BF16)]
    WTB = [Buf("WT0"), Buf("WT1")]
    vn = av(83 * KB, 2 * KB, BF16)
    vnB = [Buf("vn0"), Buf("vn1")]
    otmp = av(85 * KB, 4 * KB, F32)
    otmpB = [Buf("otmp0"), Buf("otmp1")]
    osub = av(89 * KB, 4 * KB, F32)
    osubB = [Buf("osub0"), Buf("osub1")]
    ob = av(93 * KB, 2 * KB, BF16)
    obB = Buf("ob")
    sqo = av(95 * KB, 4 * KB, F32)
    sqoB = Buf("sqo")

    ATi_b = sb("ATi_b", [P, 512], BF16)
    ek_b = sb("ek_b", [P, 16])
    sdecp_b = sb("sdecp_b", [P, 16])
    ATi_h = [ATi, ATi_b[:]]
    Kd_h = [Kd, sg[2][:].bitcast(BF16)]
    ek_h = [ek, ek_b]
    sdecp_h = [sdecp, sdecp_b]
    Ut_h = [(Ut[:, 0:512], Ut[:, 512:1024]), (sg[0][:], sg[1][:])]
    _sg3 = sg[3][:].bitcast(BF16)
    WT_h = [[WT2[0], WT2[1]], [_sg3[:, 0:512], _sg3[:, 512:1024]]]
    WTBh = [[Buf("WT00"), Buf("WT01")], [Buf("WT10"), Buf("WT11")]]
    psum = [es.enter_context(nc.psum_tensor(f"ps{i}", [P, 512], F32)) for i in range(8)]
    psB = [Buf(f"ps{i}") for i in range(8)]
    psctr = [0]

    class Ring:
        def __init__(self, banks):
            self.banks = list(banks)
            self.c = 0

        def next(self):
            i = self.banks[self.c % len(self.banks)]
            self.c += 1
            return psum[i], psB[i]

    ring_all = Ring(range(8))
    cur_ring = [ring_all]

    def ps_next():
        return cur_ring[0].next()

    def run_interleaved(gens):
        gens = list(gens)
        while gens:
            for g in list(gens):
                try:
                    cur_ring[0] = g[1]
                    next(g[0])
                except StopIteration:
                    gens.remove(g)
        cur_ring[0] = ring_all

    def mm(out, lhsT, rhs, start, stop, reads, writes):
        S.add("pe", lambda e: e.matmul(out, lhsT, rhs, start=start, stop=stop), reads, writes)

    def tr(out, in_, ident, reads, writes):
        S.add("pe", lambda e: e.transpose(out, in_, ident), reads, writes)

    def act(out, in_, func, reads, writes, bias=None, scale=None, accum_out=None):
        kw = {}
        if bias is not None:
            kw["bias"] = bias
        if scale is not None:
            kw["scale"] = scale
        if accum_out is not None:
            kw["accum_out"] = accum_out
        S.add("act", lambda e: e.activation(out, in_, func, **kw), reads, writes)

    def tt(eng, out, in0, in1, op, reads, writes):
        S.add(eng, lambda e: e.tensor_tensor(out, in0, in1, op), reads, writes)

    def ts(eng, out, in0, s1, s2, op0, op1, reads, writes):
        if op1 is None:
            S.add(eng, lambda e: e.tensor_scalar(out, in0, s1, None, op0), reads, writes)
        else:
            S.add(eng, lambda e: e.tensor_scalar(out, in0, s1, s2, op0, op1), reads, writes)

    def stt(out, in0, scalar, in1, op0, op1, reads, writes):
        S.add("dve", lambda e: e.scalar_tensor_tensor(out, in0, scalar, in1, op0, op1), reads, writes)

    def cp(eng, out, in_, reads, writes):
        if eng == "act":
            S.add("act", lambda e: e.copy(out, in_), reads, writes)
        else:
            S.add(eng, lambda e: e.tensor_copy(out, in_), reads, writes)

    def barrier():
        lasts = []
        for e in ENGS:
            for o in reversed(S.q[e]):
                if o.dma is None:
                    lasts.append(o)
                    break
        seen = set()
        for e in ENGS:
            for o in reversed(S.q[e]):
                if o.dma and o.dma[0] not in seen:
                    seen.add(o.dma[0])
                    lasts.append(o)
        for e in ENGS:
            b = Op(e, lambda eng: eng.nop(), None)
            for x in lasts:
                if x.dma or x.eng != e:
                    x.needed = True
                    b.deps.append(x)
            S.q[e].append(b)

    def wload(src, kc, ncols):
        i = wctr[0] % NW
        wctr[0] += 1
        t = wslots[i][:, 0:kc * ncols].rearrange("p (k c) -> p k c", k=kc)
        b = wB[i]
        step = 8
        for k0 in range(0, kc, step):
            k1 = min(kc, k0 + step)
            S.add("pool", lambda e, k0=k0, k1=k1: e.dma_start(out=t[:, k0:k1, :], in_=src[:, k0:k1, :]),
                  reads=(), writes=(b,), dma=f"w{i}")
        return t, b

    def bc(ap2, shape):
        return ap2.broadcast_to(list(shape))

    S.add("sp", lambda e: e.dma_start(out=small[:], in_=small_d), (), (smallB,), dma="small")
    S.add("sp", lambda e: e.dma_start(out=consts[:], in_=const_d), (), (constsB,), dma="consts")
    cp("dve", ident_bf[:], consts[:, C_IDENT:C_IDENT + P], (constsB,), (cbfB,))
    cp("dve", maskS_bf[:], consts[:, C_MASKS:C_MASKS + P], (constsB,), (cbfB,))
    cp("dve", ones_bf[:], consts[:, C_ONES:C_ONES + P], (constsB,), (cbfB,))
    cp("dve", negm_bf[:], consts[:, C_NEGM:C_NEGM + P], (constsB,), (cbfB,))
    S.add("pool", lambda e: e.memset(Sf[:], 0.0), (), (SfB,))
    S.add("pool", lambda e: e.memset(Sb[:], 0.0), (), (SbB,))
    S.add("pool", lambda e: e.memset(halo[:], 0.0), (), (haloB,))
    act(negA[:], small[:, SM_ALOG:SM_ALOG + 8], AF.Exp, (smallB,), (negAB,))
    ts("dve", negA[:], negA[:], -1.0, None, ALU.mult, None, (negAB,), (negAB,))
    ident_f = consts[:, C_IDENT:C_IDENT + P]
    ones_f = consts[:, C_ONES:C_ONES + P]
    maskS_f = consts[:, C_MASKS:C_MASKS + P]
    zcol = consts[:, C_ZERO:C_ZERO + 1]

    def gamma(idx):
        return small[:, SM_GAMMA + idx * KD: SM_GAMMA + (idx + 1) * KD]

    def row_rstd(n):
        ts("dve", rstd[:, 0:n], ssq[:, 0:n], 1.0 / D, RMS_EPS, ALU.mult, ALU.add, (ssqB,), (rstdB,))
        act(rstd[:, 0:n], rstd[:, 0:n], AF.Sqrt, (rstdB,), (rstdB,))
        S.add("dve", lambda e: e.reciprocal(rstd[:, 0:n], rstd[:, 0:n]), (rstdB,), (rstdB,))

    def sum_squares():
        for s in range(NS):
            if s % 2 == 0:
                act(junk, h[s][:], AF.Square, (hB[s],), (junkB, ssqB), accum_out=ssq[:, s:s + 1])
            else:
                S.add("dve", lambda e, s=s: e.scalar_tensor_tensor(hs[1], h[s][:], 1.0, h[s][:], ALU.mult, ALU.mult,
                                                                  accum_out=ssq[:, s:s + 1]),
                      (hB[s],), (hsB[1], ssqB))

    def rmsnorm_to_nT(gidx):
        g = gamma(gidx)
        sum_squares()
        row_rstd(NS)
        for s in range(NS):
            i = hsctr[0] % 2
            hsctr[0] += 1
            if s % 2 == 0:
                ts("dve", hs[i], h[s][:], rstd[:, s:s + 1], None, ALU.mult, None, (hB[s], rstdB), (hsB[i],))
            else:
                act(hs[i], h[s][:], AF.Copy, (hB[s], rstdB), (hsB[i],), scale=rstd[:, s:s + 1])
            for jg in range(KD // 4):
                pt, pb = ps_next()
                pv = pt[:].bitcast(BF16)
                for jj in range(4):
                    j = jg * 4 + jj
                    tr(pv[:, jj * P:(jj + 1) * P], hs[i][:, j * P:(j + 1) * P], ident_bf[:], (hsB[i], cbfB), (pb,))
                gb = bc(g[:, jg * 4:(jg + 1) * 4].unsqueeze(2), [P, 4, P])
                tt("dve", nT[:, jg * 4:(jg + 1) * 4, s * P:(s + 1) * P],
                   pv[:, 0:4 * P].rearrange("p (a b) -> p a b", a=4), gb, ALU.mult, (pb, smallB), (nTB,))

    def ffn(wgu, wd):
        wgu_v = wgu.rearrange("(k p) c -> p k c", p=P)
        wd_v = wd.rearrange("(k p) c -> p k c", p=P)
        for fg in range(F // 512):
            sgi = []
            for part, coff in ((0, 0), (1, F)):
                pss_ = [ps_next() for _ in range(4)]
                for kh in range(2):
                    wt, wtb = wload(wgu_v[:, kh * 8:(kh + 1) * 8, coff + fg * 512:coff + (fg + 1) * 512], 8, 512)
                    for kk in range(8):
                        k = kh * 8 + kk
                        for fl in range(4):
                            mm(pss_[fl][0][:], wt[:, kk, fl * P:(fl + 1) * P], nT[:, k, :], k == 0, k == KD - 1,
                               (wtb, nTB), (pss_[fl][1],))
                for fl in range(4):
                    if part == 0:
                        i = sgctr[0] % NSG
                        sgctr[0] += 1
                        sgi.append(i)
                        act(sg[i][:], pss_[fl][0][:], AF.Silu, (pss_[fl][1],), (sgB[i],))
                    else:
                        i = sgi[fl]
                        tt("dve", hid[:, fg * 4 + fl, :], sg[i][:], pss_[fl][0][:], ALU.mult,
                           (sgB[i], pss_[fl][1]), (hidB[fg * 4 + fl],))
        for cb in range(D // 512):
            pss = [ps_next() for _ in range(NS)]
            f0 = 0
            while f0 < KF:
                kq = min(8, KF - f0)
                wt, wtb = wload(wd_v[:, f0:f0 + kq, cb * 512:(cb + 1) * 512], kq, 512)
                for fl in range(kq):
                    f = f0 + fl
                    for s in range(NS):
                        mm(pss[s][0][:], hid[:, f, s * P:(s + 1) * P], wt[:, fl, :], f == 0, f == KF - 1,
                           (hidB[f], wtb), (pss[s][1],))
                f0 += kq
            for s in range(NS):
                hv = h[s][:, cb * 512:(cb + 1) * 512]
                stt(hv, pss[s][0][:], 0.5, hv, ALU.mult, ALU.add, (pss[s][1], hB[s]), (hB[s],))

    def proj_tok(wv, nk, c0, lhs_fn, lhs_bufs):
        pss = [ps_next() for _ in range(NS)]
        k0 = 0
        while k0 < nk:
            kq = min(8, nk - k0)
            wt, wtb = wload(wv[:, k0:k0 + kq, c0:c0 + 512], kq, 512)
            for kk in range(kq):
                k = k0 + kk
                for s in range(NS):
                    mm(pss[s][0][:], lhs_fn(k, s), wt[:, kk, :], k == 0, k == nk - 1,
                       tuple(lhs_bufs(k)) + (wtb,), (pss[s][1],))
            k0 += kq
        return pss

    win_v = win.rearrange("(k p) c -> p k c", p=P)

    def feat_blk(c0):
        if c0 < 3072:
            return c0 // 256
        return 12 + (c0 - 4112) // 256

    def proj_feat(c0, emit):
        wt, wtb = wload(winf[feat_blk(c0)].rearrange("p (k c) -> p k c", k=KD), KD, 256)
        for cl in range(2):
            pt, pb = ps_next()
            for k in range(KD):
                mm(pt[:], wt[:, k, cl * P:(cl + 1) * P], nT[:, k, :], k == 0, k == KD - 1, (wtb, nTB), (pb,))
            emit(cl, pt, pb)

    SB_SCALE = float(HD) ** -0.5
    Q_OFF, K_OFF, V_OFF, Z_OFF, A_OFF = 0, 1024, 2048, 3072, 4096
    QS_OFF, KS_OFF, VS_OFF = 4112, 5136, 6160

    def sb_phase(it):
        t0 = it * TT
        for blk in range(4):
            def emit_q(cl, pt, pb, blk=blk):
                hh = blk * 2 + cl
                cp("act", qTs[:, hh, :], pt[:], (pb,), (qTsB[hh],))
            proj_feat(QS_OFF + blk * 256, emit_q)
        for blk in range(4):
            def emit_k(cl, pt, pb, blk=blk):
                hh = blk * 2 + cl
                i = hh % 2
                cp("act", kstage[i], pt[:], (pb,), (kstageB[i],))
                S.add("sp", lambda e: e.dma_start(out=ksb_d[hh, :, t0:t0 + TT], in_=kstage[i]),
                      (kstageB[i],), (ksbB[hh],), dma=f"kst{i}")
            proj_feat(KS_OFF + blk * 256, emit_k)
        for cb in range(2):
            pss = proj_tok(win_v, KD, VS_OFF + cb * 512, lambda k, s: nT[:, k, s * P:(s + 1) * P], lambda k: (nTB,))
            for s in range(NS):
                i = s % 2
                cp("act", vstage[i][:, 0:512], pss[s][0][:], (pss[s][1],), (vstageB[i],))
                S.add("sp", lambda e, s=s, i=i, cb=cb: e.dma_start(
                    out=vsb_d[t0 + s * P:t0 + (s + 1) * P, cb * 512:(cb + 1) * 512], in_=vstage[i][:, 0:512]),
                    (vstageB[i],), (vsbB,), dma=f"vst{i}")
        nkb_tot = 4 * (it + 1)
        if stages == "b2":
            return
        for par in range(NPAR):
            S.add("pool", lambda e, par=par: e.memset(Pb[par][:, 0:1], 0.0), (), (PbB[par],))
        rings = [Ring([par]) for par in range(NPAR)]

        def sb_iter(hh, qs, par, kv):
            qb = 4 * it + qs
            nk = qb + 1
            nkeys = nk * P
            po, pob = psum[4 + par], psB[4 + par]
            tiles = []
            kt0 = 0
            while kt0 < nkeys:
                nkt = min(512, nkeys - kt0)
                tiles.append((kt0, nkt))
                kt0 += nkt
            ntl = len(tiles)
            for ti, (kt0, nkt) in enumerate(reversed(tiles)):
                diag = (kt0 + nkt == nkeys)
                pz, pzb = ps_next()
                mm(pz[:, 0:nkt], qTs[:, hh, qs * P:(qs + 1) * P], ktl[kv][:, kt0:kt0 + nkt], True, not diag,
                   (qTsB[hh], ktlB[kv]), (pzb,))
                if diag:
                    mm(pz[:, nkt - P:nkt], ident_bf[:], negm_bf[:], False, True, (cbfB,), (pzb,))
                act(et[par][:, 0:nkt], pz[:, 0:nkt], AF.Exp, (pzb,), (etB[par],), scale=SB_SCALE)
                act(spt[par][:, 0:nkt], et[par][:, 0:nkt], AF.Ln, (etB[par],), (sptB[par],), bias=1.0)
                yield
                S.add("dve", lambda e, par=par, nkt=nkt: e.tensor_tensor_scan(
                    Pb[par][:, 1:1 + nkt], spt[par][:, 0:nkt], bc(zcol, [P, nkt]),
                    Pb[par][:, 0:1], ALU.add, ALU.add), (sptB[par], PbB[par], constsB), (PbB[par],))
                if ti == 0:
                    ts("dve", ntot[:, par:par + 1], Pb[par][:, nkt:nkt + 1], -1.0, None, ALU.mult, None,
                       (PbB[par],), (ntotB2[par],))
                else:
                    stt(ntot[:, par:par + 1], Pb[par][:, nkt:nkt + 1], -1.0, ntot[:, par:par + 1], ALU.mult, ALU.add,
                        (PbB[par], ntotB2[par]), (ntotB2[par],))
                stt(lb[par][:, 0:nkt], pz[:, 0:nkt], SB_SCALE, Pb[par][:, 0:nkt], ALU.mult, ALU.add,
                    (pzb, PbB[par]), (lbB[par],))
                yield
                act(aa[par][:, 0:nkt], lb[par][:, 0:nkt], AF.Exp, (lbB[par], ntotB2[par]), (aaB[par],),
                    bias=ntot[:, par:par + 1])
                yield
                n = nkt // P
                pt, pb = ps_next()
                pv = pt[:].bitcast(BF16)
                for j in range(n):
                    tr(pv[:, j * P:(j + 1) * P], aa[par][:, j * P:(j + 1) * P], ident_bf[:], (aaB[par], cbfB), (pb,))
                cp("act", aT[par][:, 0:n, :], pv[:, 0:n * P].rearrange("p (a b) -> p a b", a=n), (pb,), (aTB[par],))
                yield
                for j in range(n):
                    kb = kt0 // P + j
                    mm(po[:, 0:P], vtl[kv][:, kb, :], aT[par][:, j, :], ti == 0 and j == 0,
                       ti == ntl - 1 and j == n - 1, (vtlB[kv], aTB[par]), (pob,))
                yield
            cp("act", omixT[:, 8 + hh, qs * P:(qs + 1) * P], po[:, 0:P], (pob,), (omixB[8 + hh],))

        def load_kv(hh):
            kv = hh % 2
            S.add("sp", lambda e: e.dma_start(out=ktl[kv][:, 0:nkb_tot * P], in_=ksb_d[hh, :, 0:nkb_tot * P]),
                  (ksbB[hh],), (ktlB[kv],), dma=f"ktl{kv}")
            S.add("sp", lambda e: e.dma_start(
                out=vtl[kv][:, 0:nkb_tot, :],
                in_=vsb_d[0:nkb_tot * P, hh * HD:(hh + 1) * HD].rearrange("(kb p) d -> p kb d", p=P)),
                (vsbB,), (vtlB[kv],), dma=f"vtl{kv}")

        if stages == "b3":
            for hh in range(NH):
                load_kv(hh)
            return
        todo = [(hh, qs) for hh in range(NH) for qs in range(NS)]
        active = {}
        free_slots = list(range(NPAR))
        loaded = set()
        remaining = {hh: NS for hh in range(NH)}
        while todo or active:
            while todo and free_slots:
                hh, qs = todo[0]
                if hh >= 2 and remaining[hh - 2] > 0:
                    break
                todo.pop(0)
                if hh not in loaded:
                    load_kv(hh)
                    loaded.add(hh)
                par = free_slots.pop(0)
                active[par] = (sb_iter(hh, qs, par, hh % 2), hh)
            for par in sorted(active):
                cur_ring[0] = rings[par]
                try:
                    next(active[par][0])
                except StopIteration:
                    remaining[active[par][1]] -= 1
                    del active[par]
                    free_slots.append(par)
        cur_ring[0] = ring_all

    def dn_inproj():
        wcache = {}

        def qkv_chunk(grp, goff, hh, r):
            blk, cl = hh // 2, hh % 2
            if (grp, blk) not in wcache:
                wcache[(grp, blk)] = wload(winf[feat_blk(goff + blk * 256)].rearrange("p (k c) -> p k c", k=KD), KD, 256)
            wt, wtb = wcache[(grp, blk)]
            cidx = grp * 8 + hh
            cw = small[:, SM_CONV + cidx * 4: SM_CONV + cidx * 4 + 4]
            pt, pb = ps_next()
            for k in range(KD):
                mm(pt[:], wt[:, k, cl * P:(cl + 1) * P], nT[:, k, :], k == 0, k == KD - 1, (wtb, nTB), (pb,))
            cp("pool", raw[r][:, 0:3], halo[:, cidx, :], (haloB,), (rawB[r],))
            cp("act", raw[r][:, 3:3 + TT], pt[:], (pb,), (rawB[r],))
            cp("pool", halo[:, cidx, :], raw[r][:, TT:TT + 3], (rawB[r],), (haloB,))
            yield
            ts("dve", cacc[r], raw[r][:, 0:TT], cw[:, 0:1], None, ALU.mult, None, (rawB[r], smallB), (caccB[r],))
            for k in range(1, 4):
                stt(cacc[r], raw[r][:, k:k + TT], cw[:, k:k + 1], cacc[r], ALU.mult, ALU.add,
                    (rawB[r], smallB, caccB[r]), (caccB[r],))
            yield
            if grp == 2:
                act(sqb[r], cacc[r], AF.Silu, (caccB[r],), (sqbB[r],))
                yield
                pt2, pb2 = ps_next()
                pv = pt2[:].bitcast(BF16)
                for s in range(NS):
                    tr(pv[:, s * P:(s + 1) * P], sqb[r][:, s * P:(s + 1) * P], ident_bf[:], (sqbB[r], cbfB), (pb2,))
                cp("dve", vtok[:, :, hh, :], pv[:, 0:TT].rearrange("p (s d) -> p s d", s=NS), (pb2,), (vtokB,))
                return
            act(csil[r], cacc[r], AF.Silu, (caccB[r],), (csilB[r],))
            act(sqb[r], csil[r], AF.Square, (csilB[r],), (sqbB[r],))
            yield
            pt2, pb2 = ps_next()
            mm(pt2[:], ones_bf[:], sqb[r], True, True, (cbfB, sqbB[r]), (pb2,))
            act(rinv[r], pt2[:], AF.Sqrt, (pb2,), (rinvB[r],), bias=L2_EPS)
            yield
            S.add("dve", lambda e: e.reciprocal(rinv[r], rinv[r]), (rinvB[r],), (rinvB[r],))
            if grp == 0:
                stt(qTd[:, hh, :], csil[r], float(HD) ** -0.5, rinv[r], ALU.mult, ALU.mult,
                    (csilB[r], rinvB[r]), (qTdB[hh],))
            else:
                tt("dve", kTd[:, hh, :], csil[r], rinv[r], ALU.mult, (csilB[r], rinvB[r]), (kTdB[hh],))
                yield
                pt3, pb3 = ps_next()
                pv = pt3[:].bitcast(BF16)
                for s in range(NS):
                    tr(pv[:, s * P:(s + 1) * P], kTd[:, hh, s * P:(s + 1) * P], ident_bf[:], (kTdB[hh], cbfB), (pb3,))
                cp("dve", ktok[:, :, hh, :], pv[:, 0:TT].rearrange("p (s d) -> p s d", s=NS), (pb3,), (ktokB,))

        todo = [(grp, goff, hh) for grp, goff in ((0, Q_OFF), (1, K_OFF), (2, V_OFF)) for hh in range(NH)]
        ringsQ = [Ring([r]) for r in range(NSET)]
        active = {}
        free_sets = list(range(NSET))
        while todo or active:
            while todo and free_sets:
                r = free_sets.pop(0)
                grp, goff, hh = todo.pop(0)
                active[r] = qkv_chunk(grp, goff, hh, r)
            for r in sorted(active):
                cur_ring[0] = ringsQ[r]
                try:
                    next(active[r])
                except StopIteration:
                    del active[r]
                    free_sets.append(r)
        cur_ring[0] = ring_all
        for cb in range(2):
            pss = proj_tok(win_v, KD, Z_OFF + cb * 512, lambda k, s: nT[:, k, s * P:(s + 1) * P], lambda k: (nTB,))
            for s in range(NS):
                act(zs[:, s, cb * 512:(cb + 1) * 512], pss[s][0][:], AF.Silu, (pss[s][1],), (zsB,))
        wt, wtb = wload(winab.rearrange("p (k c) -> p k c", k=KD), KD, 16)
        pab, pabb = ps_next()
        for s in range(NS):
            for k in range(KD):
                mm(pab[:, s * 16:(s + 1) * 16], nT[:, k, s * P:(s + 1) * P], wt[:, k, :], k == 0, k == KD - 1,
                   (nTB, wtb), (pabb,))
        pab3 = pab[:, 0:NS * 16].rearrange("p (s c) -> p s c", s=NS)
        tt("dve", gtmp[:], pab3[:, :, 0:8], bc(small[:, SM_DTB:SM_DTB + 8].unsqueeze(1), [P, NS, 8]), ALU.add,
           (pabb, smallB), (gtmpB,))
        act(gtmp[:], gtmp[:], AF.Exp, (gtmpB,), (gtmpB,))
        act(gtmp[:], gtmp[:], AF.Ln, (gtmpB,), (gtmpB,), bias=1.0)
        tt("dve", graw[:], gtmp[:], bc(negA[:].unsqueeze(1), [P, NS, 8]), ALU.mult, (gtmpB, negAB), (grawB,))
        act(beta[:], pab3[:, :, 8:16], AF.Sigmoid, (pabb,), (betaB,))

    dlev = int(stages[2:]) if stages.startswith("dc") else 99

    def dn_chunks():
        TRIKI = consts[:, C_TRIKI:C_TRIKI + CH]
        UPS = consts[:, C_UPS:C_UPS + CH]
        NEGTRI = consts[:, C_NEGTRI:C_NEGTRI + CH]
        NMLS = consts[:, C_NMLS:C_NMLS + CH]
        PMUI = consts[:, C_PMUI:C_PMUI + CH]
        I2 = consts[:, C_I2:C_I2 + CH]
        HALVES = ((0, slice(0, CH)), (1, slice(CH, P)))

        def f3(ap):
            return ap.rearrange("p (h j) -> p h j", h=NH)

        def hc_(hh):
            return slice(hh * CH, (hh + 1) * CH)

        def wy(pr):
            s = pr
            hb = pr % 2
            ATi, Kd, ek, sdecp = ATi_h[hb], Kd_h[hb], ek_h[hb], sdecp_h[hb]
            Ut0, Ut1 = Ut_h[hb]
            WTs = WT_h[hb]
            g_s = graw[:, s, :]
            b_s = beta[:, s, :]
            yield
            cp("dve", f3(G1), bc(g_s.unsqueeze(2), [P, NH, CH]), (grawB,), (G1B[0],))
            tt("dve", f3(G2), bc(g_s.unsqueeze(2), [P, NH, CH]), bc(NEGTRI.unsqueeze(1), [P, NH, CH]),
               ALU.mult, (grawB, constsB), (G2B[0],))
            pd, pdb = ps_next()
            pm, pmb = ps_next()
            for hf, sl in HALVES:
                mm(pd[sl, :], TRIKI[sl, :], G1[sl, :], True, False, (constsB, G1B[0]), (pdb,))
                mm(pd[sl, :], ones_f[sl, 0:CH], G2[sl, :], False, True, (constsB, G2B[0]), (pdb,))
            for hf, sl in HALVES:
                mm(pm[sl, 0:8], TRIKI[sl, :], graw[sl, s, :], True, True, (constsB, grawB), (pmb,))
                mm(pm[sl, 8:16], UPS[sl, :], graw[sl, s, :], True, True, (constsB, grawB), (pmb,))
                mm(pm[:, 16 + 8 * hf:24 + 8 * hf], ones_f[sl, :], graw[sl, s, :], True, True, (constsB, grawB), (pmb,))
            act(ek[:, :], pm[:, 0:16], AF.Exp, (pmb,), (ekB[hb],))
            act(sdecp[:, :], pm[:, 16:32], AF.Exp, (pmb,), (sdecB[hb],))
            stt(f3(E1), f3(pd[:]), 0.0, bc(NMLS.unsqueeze(1), [P, NH, CH]), ALU.min, ALU.add, (pdb, constsB), (E1B[0],))
            act(E1, E1, AF.Exp, (E1B[0],), (E1B[0],))
            stt(f3(E2), f3(pd[:]), 0.0, bc(PMUI.unsqueeze(1), [P, NH, CH]), ALU.max, ALU.add, (pdb, constsB), (E2B[0],))
            act(E2, E2, AF.Exp, (E2B[0],), (E2B[0],), scale=-1.0)
            yield
            pk, pkb = ps_next()
            pq, pqb = ps_next()
            for hh in range(NH):
                for hf, sl in HALVES:
                    cs_ = slice((2 * pr + hf) * CH, (2 * pr + hf + 1) * CH)
                    mm(pk[sl, hc_(hh)], kTd[:, hh, cs_], kTd[:, hh, cs_], True, True, (kTdB[hh],), (pkb,))
            for hh in range(NH):
                for hf, sl in HALVES:
                    cs_ = slice((2 * pr + hf) * CH, (2 * pr + hf + 1) * CH)
                    mm(pq[sl, hc_(hh)], kTd[:, hh, cs_], qTd[:, hh, cs_], True, True, (kTdB[hh], qTdB[hh]), (pqb,))
            tt("dve", Lc, pk[:], E1, ALU.mult, (pkb, E1B[0]), (LcB[0],))
            tt("dve", f3(Lc), f3(Lc), bc(b_s.unsqueeze(2), [P, NH, CH]), ALU.mult, (LcB[0], betaB), (LcB[0],))
            tt("dve", ATi, pq[:], E2, ALU.mult, (pqb, E2B[0]), (ATiB[hb],))
            yield
            pa, pab_ = ps_next()
            for hh in range(NH):
                for hf, sl in HALVES:
                    mm(pa[sl, hc_(hh)], Lc[sl, hc_(hh)], I2[sl, :], True, True, (LcB[0], constsB), (pab_,))
            cp("dve", Am, pa[:], (pab_,), (AB_[0],))
            stt(f3(Pm), f3(Am), -1.0, bc(I2.unsqueeze(1), [P, NH, CH]), ALU.mult, ALU.add, (AB_[0], constsB), (PB_[0],))
            yield
            p1, p1b = ps_next()
            p2, p2b = ps_next()
            for hh in range(NH):
                for hf, sl in HALVES:
                    mm(p1[sl, hc_(hh)], Lc[sl, hc_(hh)], Am[sl, hc_(hh)], True, True, (LcB[0], AB_[0]), (p1b,))
            for hh in range(NH):
                for hf, sl in HALVES:
                    mm(p2[sl, hc_(hh)], Am[sl, hc_(hh)], Lc[sl, hc_(hh)], True, True, (LcB[0], AB_[0]), (p2b,))
            cp("dve", Am, p1[:], (p1b,), (AB_[0],))
            cp("dve", Lc, p2[:], (p2b,), (LcB[0],))
            yield
            for lvl in range(1, 5):
                pA, pAb = ps_next()
                pP, pPb = ps_next()
                pl, plb = ps_next()
                for hh in range(NH):
                    for hf, sl in HALVES:
                        mm(pA[sl, hc_(hh)], Lc[sl, hc_(hh)], Am[sl, hc_(hh)], True, True, (LcB[0], AB_[0]), (pAb,))
                for hh in range(NH):
                    for hf, sl in HALVES:
                        mm(pP[sl, hc_(hh)], Lc[sl, hc_(hh)], Pm[sl, hc_(hh)], True, True, (LcB[0], PB_[0]), (pPb,))
                for hh in range(NH):
                    for hf, sl in HALVES:
                        mm(pl[sl, hc_(hh)], Am[sl, hc_(hh)], Lc[sl, hc_(hh)], True, True, (LcB[0], AB_[0]), (plb,))
                tt("dve", Pm, pP[:], Pm, ALU.add, (pPb, PB_[0]), (PB_[0],))
                cp("dve", Am, pA[:], (pAb,), (AB_[0],))
                cp("dve", Lc, pl[:], (plb,), (LcB[0],))
                yield
            yield
            p5, p5b = ps_next()
            for hh in range(NH):
                for hf, sl in HALVES:
                    mm(p5[sl, hc_(hh)], Lc[sl, hc_(hh)], Pm[sl, hc_(hh)], True, True, (LcB[0], PB_[0]), (p5b,))
            tt("dve", MT, p5[:], Pm, ALU.add, (p5b, PB_[0]), (MTB[0],))
            yield
            h3 = lambda ap: ap.rearrange("p (h d) -> p h d", h=NH)
            tt("dve", h3(Vb), vtok[:, s, :, :], bc(b_s.unsqueeze(2), [P, NH, HD]), ALU.mult, (vtokB, betaB), (VbB[0],))
            tt("dve", bk[:, :], b_s, ek[:, 0:8], ALU.mult, (betaB, ekB[hb]), (bkB[0],))
            tt("dve", h3(Kbg), ktok[:, s, :, :], bc(bk[:, :].unsqueeze(2), [P, NH, HD]), ALU.mult, (ktokB, bkB[0]), (KbgB[0],))
            tt("dve", h3(Kd), ktok[:, s, :, :], bc(ek[:, 8:16].unsqueeze(2), [P, NH, HD]), ALU.mult, (ktokB, ekB[hb]), (KdB[hb],))
            yield
            pua, puab = ps_next()
            pub_, pubb = ps_next()
            pws = [ps_next(), ps_next()]
            for hh in range(NH):
                px, pxB_ = (pua, puab) if hh < 4 else (pub_, pubb)
                o0 = (hh % 4) * HD
                for hf, sl in HALVES:
                    mm(px[sl, o0:o0 + HD], MT[sl, hc_(hh)], Vb[sl, hh * HD:(hh + 1) * HD], True, True,
                       (MTB[0], VbB[0]), (pxB_,))
            for hh in range(NH):
                for hf, sl in HALVES:
                    mm(pws[hf][0][:, hc_(hh)], Kbg[sl, hh * HD:(hh + 1) * HD], MT[sl, hc_(hh)], True, True,
                       (KbgB[0], MTB[0]), (pws[hf][1],))
            cp("dve", Ut0, pua[:], (puab,), (UtB[hb],))
            cp("dve", Ut1, pub_[:], (pubb,), (UtB[hb],))
            for hf, sl in HALVES:
                cp("act", WTs[hf], pws[hf][0][:], (pws[hf][1],), (WTBh[hb][hf],))
        def seq(pr):
            s = pr
            hb = pr % 2
            ATi, Kd, ek, sdecp = ATi_h[hb], Kd_h[hb], ek_h[hb], sdecp_h[hb]
            Ut0, Ut1 = Ut_h[hb]
            WTs = WT_h[hb]
            for hf, sl in HALVES:
                c = 2 * pr + hf
                cs_ = slice(c * CH, (c + 1) * CH)
                WT = WTs[hf]
                pwa, pwab = ps_next()
                pwb2, pwbb2 = ps_next()
                for hh in range(NH):
                    px, pxB_ = (pwa, pwab) if hh < 4 else (pwb2, pwbb2)
                    o0 = (hh % 4) * HD
                    mm(px[sl, o0:o0 + HD], WT[:, hc_(hh)], Sb[:, hh, :], True, True, (WTBh[hb][hf], SbB), (pxB_,))
                tt("dve", vn[sl, 0:512], Ut0[sl, :], pwa[sl, :], ALU.subtract, (UtB[hb], pwab), (vnB[hf],))
                tt("dve", vn[sl, 512:1024], Ut1[sl, :], pwb2[sl, :], ALU.subtract, (UtB[hb], pwbb2), (vnB[hf],))
                yield
                pqa, pqab = ps_next()
                pqb2, pqbb2 = ps_next()
                for hh in range(NH):
                    px, pxB_ = (pqa, pqab) if hh < 4 else (pqb2, pqbb2)
                    o0 = (hh % 4) * HD
                    mm(px[sl, o0:o0 + HD], qTd[:, hh, cs_], Sb[:, hh, :], True, True, (qTdB[hh], SbB), (pxB_,))
                pva, pvab = ps_next()
                pvb2, pvbb2 = ps_next()
                for hh in range(NH):
                    px, pxB_ = (pva, pvab) if hh < 4 else (pvb2, pvbb2)
                    o0 = (hh % 4) * HD
                    mm(px[sl, o0:o0 + HD], ATi[sl, hc_(hh)], vn[sl, hh * HD:(hh + 1) * HD], True, True,
                       (ATiB[hb], vnB[hf]), (pxB_,))
                for half, (pqx, pqxb, pvx, pvxb) in enumerate(((pqa, pqab, pva, pvab), (pqb2, pqbb2, pvb2, pvbb2))):
                    cols = slice(half * 512, half * 512 + 512)
                    eg = bc(ek[sl, half * 4:half * 4 + 4].unsqueeze(2), [CH, 4, HD])
                    tt("dve", otmp[sl, cols].rearrange("p (h d) -> p h d", h=4),
                       pqx[sl, :].rearrange("p (h d) -> p h d", h=4), eg, ALU.mult, (pqxb, ekB[hb]), (otmpB[hf],))
                    tt("dve", osub[sl, cols], otmp[sl, cols], pvx[sl, :], ALU.add, (otmpB[hf], pvxb), (osubB[hf],))
                yield
                psa, psab = ps_next()
                psb2, psbb2 = ps_next()
                for hh in range(NH):
                    px, pxB_ = (psa, psab) if hh < 4 else (psb2, psbb2)
                    o0 = (hh % 4) * HD
                    mm(px[:, o0:o0 + HD], Kd[sl, hh * HD:(hh + 1) * HD], vn[sl, hh * HD:(hh + 1) * HD], True, True,
                       (KdB[hb], vnB[hf]), (pxB_,))
                tt("pool", Sf[:], Sf[:], bc(sdecp[:, 8 * hf:8 * hf + 8].unsqueeze(2), [P, NH, HD]), ALU.mult,
                   (SfB, sdecB[hb]), (SfB,))
                Sf2 = Sf[:].rearrange("p h d -> p (h d)")
                tt("dve", Sf2[:, 0:512], Sf2[:, 0:512], psa[:], ALU.add, (SfB, psab), (SfB,))
                tt("dve", Sf2[:, 512:1024], Sf2[:, 512:1024], psb2[:], ALU.add, (SfB, psbb2), (SfB,))
                cp("act", Sb[:], Sf[:], (SfB,), (SbB,))
                yield
            o3 = osub.rearrange("p (h d) -> p h d", h=NH)
            tt("dve", sqo, osub, osub, ALU.mult, (osubB[0], osubB[1]), (sqoB,))
            S.add("dve", lambda e: e.tensor_reduce(oss[:], sqo.rearrange("p (h d) -> p h d", h=NH), AX.X, ALU.add),
                  (sqoB,), (ossB,))
            ts("dve", oss[:], oss[:], 1.0 / HD, RMS_EPS, ALU.mult, ALU.add, (ossB,), (ossB,))
            act(oss[:], oss[:], AF.Sqrt, (ossB,), (ossB,))
            S.add("dve", lambda e: e.reciprocal(oss[:], oss[:]), (ossB,), (ossB,))
            tt("dve", o3, o3, bc(oss[:].unsqueeze(2), [P, NH, HD]), ALU.mult, (osubB[0], osubB[1], ossB), (osubB[0], osubB[1]))
            tt("dve", o3, o3, bc(small[:, SM_ONORM:SM_ONORM + HD].unsqueeze(1), [P, NH, HD]), ALU.mult,
               (osubB[0], osubB[1], smallB), (osubB[0], osubB[1]))
            tt("dve", ob, osub, zs[:, s, :], ALU.mult, (osubB[0], osubB[1], zsB), (obB,))
            for g4 in range(2):
                pt, pb = ps_next()
                pv = pt[:].bitcast(BF16)
                for j in range(4):
                    hh = g4 * 4 + j
                    tr(pv[:, j * P:(j + 1) * P], ob[:, hh * HD:(hh + 1) * HD], ident_bf[:], (obB, cbfB), (pb,))
                for j in range(4):
                    hh = g4 * 4 + j
                    cp("act", omixT[:, hh, s * P:(s + 1) * P], pv[:, j * P:(j + 1) * P], (pb,), (omixB[hh],))

        ringW, ringS = Ring([0, 1, 2, 3]), Ring([4, 5, 6, 7])
        NPR = TT // CH // 2
        cur_ring[0] = ringW
        for _ in wy(0):
            pass
        for pr in range(NPR):
            gens = [(seq(pr), ringS)]
            if pr + 1 < NPR:
                gens.append((wy(pr + 1), ringW))
            run_interleaved(gens)
        cur_ring[0] = ring_all

    def out_proj():
        wv = wout.rearrange("(k p) c -> p k c", p=P)
        for cb in range(D // 512):
            pss = proj_tok(wv, KD, cb * 512, lambda k, s: omixT[:, k, s * P:(s + 1) * P], lambda k: (omixB[k],))
            for s in range(NS):
                hv = h[s][:, cb * 512:(cb + 1) * 512]
                tt("dve", hv, pss[s][0][:], hv, ALU.add, (pss[s][1], hB[s]), (hB[s],))

    def ple(it):
        t0 = it * TT
        S.add("sp", lambda e: e.dma_start(out=ptok[:], in_=p_d[t0:t0 + TT, :].rearrange("(s p) c -> p s c", p=P)),
              (), ptokBs, dma="ptok")
        cp("dve", ptokb[:], ptok[:], ptokBs, ptokbBs)
        for s in range(NS):
            pt, pb = ps_next()
            pv = pt[:].bitcast(BF16)
            for k in range(2):
                tr(pv[:, k * P:(k + 1) * P], ptokb[:, s, k * P:(k + 1) * P], ident_bf[:], ptokbBs + (cbfB,), (pb,))
            cp("dve", pT[:, :, s * P:(s + 1) * P], pv[:, 0:2 * P].rearrange("p (k t) -> p k t", k=2), (pb,), pTBs)
        wgv = wpg.rearrange("(k p) c -> p k c", p=P)
        wpv = wpp.rearrange("(k p) c -> p k c", p=P)
        for cb in range(D // 512):
            psg = proj_tok(wgv, KD, cb * 512, lambda k, s: nT[:, k, s * P:(s + 1) * P], lambda k: (nTB,))
            psp = proj_tok(wpv, 2, cb * 512, lambda k, s: pT[:, k, s * P:(s + 1) * P], lambda k: pTBs)
            for s in range(NS):
                i = sgctr[0] % NSG
                sgctr[0] += 1
                act(sg[i][:], psg[s][0][:], AF.Sigmoid, (psg[s][1],), (sgB[i],))
                tt("dve", sg[i][:], sg[i][:], psp[s][0][:], ALU.mult, (sgB[i], psp[s][1]), (sgB[i],))
                hv = h[s][:, cb * 512:(cb + 1) * 512]
                tt("dve", hv, hv, sg[i][:], ALU.add, (hB[s], sgB[i]), (hB[s],))

    def final_norm_store(it):
        t0 = it * TT
        S.add("sp", lambda e: e.dma_start(out=finbc, in_=fin_d), (), finbcBs, dma="finbc")
        sum_squares()
        row_rstd(NS)
        for s in range(NS):
            stt(h[s][:], h[s][:], rstd[:, s:s + 1], finbc, ALU.mult, ALU.mult, (hB[s], rstdB) + finbcBs, (hB[s],))
            S.add("sp", lambda e, s=s: e.dma_start(out=out_d[t0 + s * P:t0 + (s + 1) * P, :], in_=h[s][:]),
                  (hB[s],), (), dma=f"o{s}")

    for it in range(NT):
        t0 = it * TT
        for s in range(NS):
            S.add("sp", lambda e, s=s, t0=t0: e.dma_start(out=h[s][:], in_=x_d[t0 + s * P:t0 + (s + 1) * P, :]),
                  (), (hB[s],), dma=f"x{s}")
        rmsnorm_to_nT(0)
        ffn(w1gu, w1d)
        if stages == "ffn1":
            for s in range(NS):
                S.add("sp", lambda e, s=s, t0=t0: e.dma_start(out=out_d[t0 + s * P:t0 + (s + 1) * P, :], in_=h[s][:]),
                      (hB[s],), (), dma=f"o{s}")
            continue
        barrier()
        rmsnorm_to_nT(1)
        if stages != "b1":
            sb_phase(it)
        barrier()
        if stages in ("b1", "b2", "b3", "b4"):
            for s in range(NS):
                S.add("sp", lambda e, s=s, t0=t0: e.dma_start(out=out_d[t0 + s * P:t0 + (s + 1) * P, :], in_=h[s][:]),
                      (hB[s],), (), dma=f"o{s}")
            continue
        if stages in ("nodn", "nodnmix"):
            for c in range(8):
                S.add("pool", lambda e, c=c: e.memset(omixT[:, c, :], 0.0), (), (omixB[c],))
        else:
            dn_inproj()
            barrier()
            if dlev > 0:
                dn_chunks()
            if dlev < 99:
                barrier()
                for c in range(8):
                    S.add("pool", lambda e, c=c: e.memset(omixT[:, c, :], 0.0), (), (omixB[c],))
        out_proj()
        barrier()
        if stages in ("mix", "nodnmix") or dlev < 99:
            for s in range(NS):
                S.add("sp", lambda e, s=s, t0=t0: e.dma_start(out=out_d[t0 + s * P:t0 + (s + 1) * P, :], in_=h[s][:]),
                      (hB[s],), (), dma=f"o{s}")
            continue
        rmsnorm_to_nT(2)
        ffn(w2gu, w2d)
        rmsnorm_to_nT(3)
        ple(it)
        final_norm_store(it)

    last = {}
    for o in S.q["sp"]:
        if o.dma and o.dma[0].startswith("o"):
            last[o.dma[0]] = o
    endop = Op("sp", lambda e: e.nop(), None)
    for o in last.values():
        endop.deps.append(o)
    S.q["sp"].append(endop)

    with nc.Block() as block:
        S.emit(nc, es, block)
    es.close()
    return nc


def make_small(inp):
    sm = np.zeros((P, SMALL_COLS), np.float32)
    for i, k in enumerate(("ffn1_norm", "mix_norm", "ffn2_norm", "ple_norm")):
        sm[:, SM_GAMMA + i * KD: SM_GAMMA + (i + 1) * KD] = np.asarray(inp[k], np.float32).reshape(KD, P).T
    cw = np.asarray(inp["dn_conv"], np.float32).reshape(4, 24, P)
    sm[:, SM_CONV:SM_CONV + 96] = cw.transpose(2, 1, 0).reshape(P, 96)
    sm[:, SM_ALOG:SM_ALOG + 8] = np.asarray(inp["dn_a_log"], np.float32).reshape(1, 8)
    sm[:, SM_DTB:SM_DTB + 8] = np.asarray(inp["dn_dt_bias"], np.float32).reshape(1, 8)
    sm[:, SM_ONORM:SM_ONORM + 128] = np.asarray(inp["dn_out_norm"], np.float32).reshape(1, 128)
    return sm


def make_consts():
    c = np.zeros((P, CONST_COLS), np.float32)
    c[:, C_IDENT:C_IDENT + P] = np.eye(P, dtype=np.float32)
    q = np.arange(P)[:, None]
    k = np.arange(P)[None, :]
    c[:, C_MASKS:C_MASKS + P] = (k < q)
    c[:, C_ONES:C_ONES + P] = 1.0
    pm = (np.arange(P) % CH)[:, None]
    j = np.arange(CH)[None, :]
    c[:, C_TRIKI:C_TRIKI + CH] = (pm <= j)
    c[:, C_UPS:C_UPS + CH] = (pm > j)
    c[:, C_NEGTRI:C_NEGTRI + CH] = -(pm <= j).astype(np.float32)
    c[:, C_NMLS:C_NMLS + CH] = np.where(pm > j, 0.0, NEG)
    c[:, C_PMUI:C_PMUI + CH] = np.where(j >= pm, 0.0, -NEG)
    c[:, C_I2:C_I2 + CH] = (pm == j)
    c[:, C_NEGM:C_NEGM + P] = np.where(k < q, 0.0, NEGB)
    return c


_NC_CACHE = {}


def run(inputs, T=2048, ncores=8, stages="all"):
    import os
    key = (T, stages)
    if key not in _NC_CACHE:
        _NC_CACHE[key] = build(T, stages)
    nc = _NC_CACHE[key]
    sm = make_small(inputs)
    cs = make_consts()
    fin_bc = np.ascontiguousarray(np.broadcast_to(np.asarray(inputs["final_norm"], np.float32).reshape(1, D), (P, D)))
    shared = {
        "ffn1_w_gu": np.ascontiguousarray(inputs["ffn1_w_gu"][0]),
        "ffn1_w_down": np.ascontiguousarray(inputs["ffn1_w_down"][0]),
        "w_in": np.ascontiguousarray(inputs["w_in"][0]),
        "w_out": np.ascontiguousarray(inputs["w_out"][0]),
        "ffn2_w_gu": np.ascontiguousarray(inputs["ffn2_w_gu"][0]),
        "ffn2_w_down": np.ascontiguousarray(inputs["ffn2_w_down"][0]),
        "ple_w_gate": np.ascontiguousarray(inputs["ple_w_gate"][0]),
        "ple_w_proj": np.ascontiguousarray(inputs["ple_w_proj"][0]),
        "small": sm, "consts": cs, "final_bc": fin_bc,
    }
    w_in0 = np.asarray(inputs["w_in"][0], np.float32)
    cols = np.concatenate([np.arange(0, 3072), np.arange(4112, 6160)])
    wf = w_in0[:, cols].reshape(KD, P, 20, 256).transpose(2, 1, 0, 3)
    shared["w_in_feat"] = np.ascontiguousarray(wf).reshape(20, P, KD * 256)
    wab = w_in0[:, 4096:4112].reshape(KD, P, 16).transpose(1, 0, 2)
    shared["w_in_ab"] = np.ascontiguousarray(wab).reshape(P, KD * 16)
    in_maps = []
    for c in range(ncores):
        m = dict(shared)
        m["x"] = np.ascontiguousarray(inputs["x"][c, :T])
        m["p"] = np.ascontiguousarray(inputs["p"][0, c, :T])
        in_maps.append(m)
    trace = bool(os.environ.get("K_TRACE"))
    res = run_bass_kernel_spmd(nc, in_maps, core_ids=list(range(ncores)), trace=trace)
    if trace:
        print("exec_time_ns", res.exec_time_ns)
    return np.stack([np.asarray(r["out"]) for r in res.results], axis=0)


def kernel(**inputs):
    inputs = {k: np.asarray(v) for k, v in inputs.items()}
    return run(inputs, T=2048, ncores=8).astype(np.float32)
```

```python
import numpy as np
from contextlib import ExitStack
import concourse.bass as bass
import concourse.mybir as mybir
from concourse.bass_utils import run_bass_kernel_spmd

F32 = mybir.dt.float32
BF16 = mybir.dt.bfloat16
AF = mybir.ActivationFunctionType
ALU = mybir.AluOpType
AX = mybir.AxisListType

P = 128
D = 2048
F = 5632
KD = D // P
KF = F // P
TT = 512
NS = TT // P
NH = 8
HD = 128
CH = 64
IN_COLS = 7184
PLE = 256
RMS_EPS = 1e-6
L2_EPS = 1e-6
NEG = -30000.0

ENGS = ("pe", "act", "dve", "pool", "sp")


class Buf:
    __slots__ = ("name", "w", "r")

    def __init__(self, name):
        self.name = name
        self.w = []
        self.r = []


class Op:
    __slots__ = ("eng", "fn", "dma", "deps", "needed", "val")

    def __init__(self, eng, fn, dma):
        self.eng = eng
        self.fn = fn
        self.dma = dma
        self.deps = []
        self.needed = False
        self.val = 0

    def key(self):
        return ("d", self.dma[0]) if self.dma else ("e", self.eng)


def _push(lst, o):
    k = o.key()
    lst[:] = [x for x in lst if x.key() != k]
    lst.append(o)


class Sched:
    def __init__(self):
        self.q = {e: [] for e in ENGS}
        self.dcnt = {}

    def add(self, eng, fn, reads=(), writes=(), dma=None):
        if dma is not None:
            c = self.dcnt.get(dma, 0) + 16
            self.dcnt[dma] = c
            o = Op(eng, fn, (dma, c))
        else:
            o = Op(eng, fn, None)
        deps = []
        for b in reads:
            for x in b.w:
                if x.dma or o.dma or x.eng != o.eng or o.eng != "pe":
                    deps.append(x)
        for b in writes:
            for x in b.r + b.w:
                if x.dma or o.dma or x.eng != o.eng:
                    deps.append(x)
        for b in reads:
            _push(b.r, o)
        for b in writes:
            if b.r and not (len(b.r) == 1 and b.r[0] is o):
                b.w = [o]
                b.r = [x for x in b.r if x is o]
            else:
                _push(b.w, o)
        for x in deps:
            if x is not o:
                x.needed = True
                o.deps.append(x)
        self.q[eng].append(o)
        return o

    def emit(self, nc, es, block):
        sems = {}
        for e in ENGS:
            sems[("e", e)] = es.enter_context(nc.semaphore("s_" + e))
            c = 0
            for o in self.q[e]:
                if o.dma is None and o.needed:
                    c += 1
                    o.val = c
        for d in self.dcnt:
            sems[("d", d)] = es.enter_context(nc.semaphore("d_" + d))

        def body(ename):
            def run(eng):
                seen = {}
                for o in self.q[ename]:
                    need = {}
                    for x in o.deps:
                        k = x.key()
                        v = x.dma[1] if x.dma else x.val
                        if seen.get(k, 0) >= v:
                            continue
                        if need.get(k, 0) < v:
                            need[k] = v
                    for k, v in need.items():
                        seen[k] = v
                        eng.wait_ge(sems[k], v)
                    ins = o.fn(eng)
                    if o.dma:
                        ins.then_inc(sems[o.key()], 16)
                    elif o.needed:
                        ins.then_inc(sems[o.key()], 1)
            return run

        block.tensor(body("pe"))
        block.scalar(body("act"))
        block.vector(body("dve"))
        block.gpsimd(body("pool"))
        block.sync(body("sp"))


SM_GAMMA = 0
SM_CONV = 64
SM_ALOG = 160
SM_DTB = 168
SM_ONORM = 176
SMALL_COLS = 304

C_IDENT = 0
C_MASKS = 128
C_ONES = 256
C_TRIKI = 384
C_UPS = 448
C_NEGTRI = 512
C_NMLS = 576
C_PMUI = 640
C_ZERO = 704
C_I2 = 705
C_NEGM = 769
CONST_COLS = 897
NEGB = -30000.0 * (128.0 ** 0.5)

KB = 1024
import os
DSUB = int(os.environ.get("DSUB", "0"))


def build(T, stages="all"):
    NT = T // TT
    nc = bass.Bass("TRN2", target_bir_lowering=False)

    def din(name, shape):
        return nc.dram_tensor(name, list(shape), F32, kind="ExternalInput").ap()

    x_d = din("x", [T, D])
    p_d = din("p", [T, PLE])
    w1gu = din("ffn1_w_gu", [D, 2 * F])
    w1d = din("ffn1_w_down", [F, D])
    win = din("w_in", [D, IN_COLS])
    winf = din("w_in_feat", [20, P, KD * 256])
    winab = din("w_in_ab", [P, KD * 16])
    wout = din("w_out", [D, D])
    w2gu = din("ffn2_w_gu", [D, 2 * F])
    w2d = din("ffn2_w_down", [F, D])
    wpg = din("ple_w_gate", [D, D])
    wpp = din("ple_w_proj", [PLE, D])
    small_d = din("small", [P, SMALL_COLS])
    const_d = din("consts", [P, CONST_COLS])
    fin_d = din("final_bc", [P, D])
    out_d = nc.dram_tensor("out", [T, D], F32, kind="ExternalOutput").ap()
    ksb_d = nc.dram_tensor("ksb_scr", [NH, P, T], BF16, kind="Internal").ap()
    vsb_d = nc.dram_tensor("vsb_scr", [T, NH * HD], BF16, kind="Internal").ap()
    ksbB = [Buf(f"ksbd{hh}") for hh in range(NH)]
    vsbB = Buf("vsbd")

    S = Sched()
    es = ExitStack()

    def sb(name, shape, dt=F32):
        return es.enter_context(nc.sbuf_tensor("sb_" + name, list(shape), dt))

    h = [sb(f"h{s}", [P, D]) for s in range(NS)]
    hB = [Buf(f"h{s}") for s in range(NS)]
    nT = sb("nT", [P, KD, TT], BF16)
    nTB = Buf("nT")
    small = sb("small", [P, SMALL_COLS])
    smallB = Buf("small")
    consts = sb("consts", [P, CONST_COLS])
    constsB = Buf("consts")
    ident_bf = sb("ident_bf", [P, P], BF16)
    maskS_bf = sb("maskS_bf", [P, P], BF16)
    ones_bf = sb("ones_bf", [P, P], BF16)
    negm_bf = sb("negm_bf", [P, P], BF16)
    cbfB = Buf("cbf")
    NW = 4
    WSZ = 4096
    wslots = [sb(f"w{i}", [P, WSZ], BF16) for i in range(NW)]
    wB = [Buf(f"w{i}") for i in range(NW)]
    wctr = [0]
    ssq = sb("ssq", [P, 8])
    ssqB = Buf("ssq")
    rstd = sb("rstd", [P, 8])
    rstdB = Buf("rstd")
    NSG = 4
    sg = [sb(f"sg{i}", [P, TT]) for i in range(NSG)]
    sgB = [Buf(f"sg{i}") for i in range(NSG)]
    sgctr = [0]
    Sf = sb("Sf", [P, NH, HD])
    SfB = Buf("Sf")
    Sb = sb("Sb", [P, NH, HD], BF16)
    SbB = Buf("Sb")
    halo = sb("halo", [P, 24, 3])
    haloB = Buf("halo")
    negA = sb("negA", [P, 8])
    negAB = Buf("negA")
    graw = sb("graw", [P, NS, 8])
    grawB = Buf("graw")
    gtmp = sb("gtmp", [P, NS, 8])
    gtmpB = Buf("gtmp")
    beta = sb("beta", [P, NS, 8])
    betaB = Buf("beta")
    ek = sb("ek", [P, 16])
    ekB = [Buf("ek0"), Buf("ek1")]
    bk = sb("bk", [P, 8])
    bkB = [Buf("bk0"), Buf("bk1")]
    sdecp = sb("sdecp", [P, 16])
    sdecB = [Buf("sdec0"), Buf("sdec1")]
    oss = sb("oss", [P, 8])
    ossB = Buf("oss")
    ntot = sb("ntot", [P, 4])
    ntotB = Buf("ntot")

    ARENA_KB = 100
    arena = sb("arena", [P, ARENA_KB * KB // 4])

    def av(off_b, nbytes, dt, shape=None):
        a = arena[:, off_b // 4:(off_b + nbytes) // 4]
        if dt is BF16:
            a = a.bitcast(BF16)
        return a

    hid = av(0, 44 * KB, BF16).rearrange("p (f t) -> p f t", f=KF)
    hidB = [Buf(f"hid{f}") for f in range(KF)]
    hs = [av((88 + 4 * i) * KB, 4 * KB, BF16) for i in range(2)]
    hsB = [Buf(f"hs{i}") for i in range(2)]
    hsctr = [0]
    junk = av(96 * KB, 4 * KB, BF16)
    junkB = Buf("junk")
    omixT = av(0, 16 * KB, BF16).rearrange("p (c t) -> p c t", c=16)
    omixB = [Buf(f"omix{c}") for c in range(16)]
    zs = av(16 * KB, 8 * KB, BF16).rearrange("p (s c) -> p s c", s=NS)
    zsB = Buf("zs")
    finbc = av(0, 8 * KB, F32)
    finbcBs = tuple(hidB[0:8])
    pT = av(8 * KB, 1 * KB * 2, BF16).rearrange("p (k t) -> p k t", k=2)
    pTBs = tuple(hidB[8:10])
    ptok = av(10 * KB, 4 * KB, F32).rearrange("p (s c) -> p s c", s=NS)
    ptokBs = tuple(hidB[10:14])
    ptokb = av(14 * KB, 2 * KB, BF16).rearrange("p (s c) -> p s c", s=NS)
    ptokbBs = tuple(hidB[14:16])
    qTs = av(24 * KB, 8 * KB, BF16).rearrange("p (h t) -> p h t", h=NH)
    qTsB = [Buf(f"qTs{i}") for i in range(NH)]
    kstage = [av((32 + i) * KB, 1 * KB, BF16) for i in range(2)]
    kstageB = [Buf(f"kst{i}") for i in range(2)]
    vstage = [av((34 + 2 * i) * KB, 2 * KB, BF16) for i in range(2)]
    vstageB = [Buf(f"vst{i}") for i in range(2)]
    NPAR = 4
    et = [av((38 + 2 * i) * KB, 2 * KB, F32) for i in range(NPAR)]
    etB = [Buf(f"et{i}") for i in range(NPAR)]
    spt = [av((46 + 2 * i) * KB, 2 * KB, F32) for i in range(NPAR)]
    sptB = [Buf(f"spt{i}") for i in range(NPAR)]
    Pb = [av((54 + 3 * i) * KB, 3 * KB, F32) for i in range(NPAR)]
    PbB = [Buf(f"Pb{i}") for i in range(NPAR)]
    lb = [av((66 + 2 * i) * KB, 2 * KB, F32) for i in range(NPAR)]
    lbB = [Buf(f"lb{i}") for i in range(NPAR)]
    aa = [av((74 + i) * KB, 1 * KB, BF16) for i in range(NPAR)]
    aaB = [Buf(f"aa{i}") for i in range(NPAR)]
    aT = [av((78 + i) * KB, 1 * KB, BF16).rearrange("p (k q) -> p k q", k=4) for i in range(NPAR)]
    aTB = [Buf(f"aT{i}") for i in range(NPAR)]
    ktl = [av((82 + 4 * i) * KB, 4 * KB, BF16) for i in range(2)]
    ktlB = [Buf(f"ktl{i}") for i in range(2)]
    vtl = [av((90 + 4 * i) * KB, 4 * KB, BF16).rearrange("p (k d) -> p k d", k=16) for i in range(2)]
    vtlB = [Buf(f"vtl{i}") for i in range(2)]
    ntotB2 = [Buf(f"ntot{i}") for i in range(NPAR)]
    qTd = av(24 * KB, 8 * KB, BF16).rearrange("p (h t) -> p h t", h=NH)
    qTdB = [Buf(f"qTd{i}") for i in range(NH)]
    kTd = av(32 * KB, 8 * KB, BF16).rearrange("p (h t) -> p h t", h=NH)
    kTdB = [Buf(f"kTd{i}") for i in range(NH)]
    ktok = av(40 * KB, 8 * KB, BF16).rearrange("p (s h d) -> p s h d", s=NS, h=NH)
    ktokB = Buf("ktok")
    vtok = av(48 * KB, 8 * KB, BF16).rearrange("p (s h d) -> p s h d", s=NS, h=NH)
    vtokB = Buf("vtok")
    NSET = 6
    SETB = 7 * KB
    raw = [av(56 * KB + SETB * i, 2560, F32) for i in range(NSET)]
    rawB = [Buf(f"raw{i}") for i in range(NSET)]
    cacc = [av(56 * KB + SETB * i + 2560, 2 * KB, F32) for i in range(NSET)]
    caccB = [Buf(f"cacc{i}") for i in range(NSET)]
    csil = cacc
    csilB = caccB
    sqb = [av(56 * KB + SETB * i + 2560 + 2 * KB, 1 * KB, BF16) for i in range(NSET)]
    sqbB = [Buf(f"sqb{i}") for i in range(NSET)]
    rinv = [raw[i][:, 0:TT] for i in range(NSET)]
    rinvB = rawB
    G1 = av(56 * KB, 2 * KB, F32)
    G1B = [Buf("G10"), Buf("G11")]
    G2 = av(58 * KB, 2 * KB, F32)
    G2B = [Buf("G20"), Buf("G21")]
    E1 = av(60 * KB, 2 * KB, F32)
    E1B = [Buf("E10"), Buf("E11")]
    E2 = av(62 * KB, 2 * KB, F32)
    E2B = [Buf("E20"), Buf("E21")]
    Lc = av(64 * KB, 2 * KB, F32)
    LcB = [Buf("Lc0"), Buf("Lc1")]
    Am = av(66 * KB, 2 * KB, F32)
    Pm = av(68 * KB, 2 * KB, F32)
    AB_ = [Buf("A0"), Buf("A1")]
    PB_ = [Buf("Pm0"), Buf("Pm1")]
    ATi = av(70 * KB, 1 * KB, BF16)
    ATiB = [Buf("ATi0"), Buf("ATi1")]
    MT = av(71 * KB, 1 * KB, BF16)
    MTB = [Buf("MT0"), Buf("MT1")]
    Vb = av(72 * KB, 2 * KB, BF16)
    VbB = [Buf("Vb0"), Buf("Vb1")]
    Kbg = av(74 * KB, 2 * KB, BF16)
    KbgB = [Buf("Kbg0"), Buf("Kbg1")]
    Kd = av(76 * KB, 2 * KB, BF16)
    KdB = [Buf("Kd0"), Buf("Kd1")]
    Ut = av(78 * KB, 4 * KB, F32)
    UtB = [Buf("U0"), Buf("U1")]
    WT2 = [av(82 * KB, 1 * KB, BF16), av(99 * KB, 1 * KB, BF16)]
    WTB = [Buf("WT0"), Buf("WT1")]
    vn = av(83 * KB, 2 * KB, BF16)
    vnB = [Buf("vn0"), Buf("vn1")]
    otmp = av(85 * KB, 4 * KB, F32)
    otmpB = [Buf("otmp0"), Buf("otmp1")]
    osub = av(89 * KB, 4 * KB, F32)
    osubB = [Buf("osub0"), Buf("osub1")]
    ob = av(93 * KB, 2 * KB, BF16)
    obB = Buf("ob")
    sqo = av(95 * KB, 4 * KB, F32)
    sqoB = Buf("sqo")

    ATi_b = sb("ATi_b", [P, 512], BF16)
    ek_b = sb("ek_b", [P, 16])
    sdecp_b = sb("sdecp_b", [P, 16])
    ATi_h = [ATi, ATi_b[:]]
    Kd_h = [Kd, sg[2][:].bitcast(BF16)]
    ek_h = [ek, ek_b]
    sdecp_h = [sdecp, sdecp_b]
    Ut_h = [(Ut[:, 0:512], Ut[:, 512:1024]), (sg[0][:], sg[1][:])]
    _sg3 = sg[3][:].bitcast(BF16)
    WT_h = [[WT2[0], WT2[1]], [_sg3[:, 0:512], _sg3[:, 512:1024]]]
    WTBh = [[Buf("WT00"), Buf("WT01")], [Buf("WT10"), Buf("WT11")]]
    psum = [es.enter_context(nc.psum_tensor(f"ps{i}", [P, 512], F32)) for i in range(8)]
    psB = [Buf(f"ps{i}") for i in range(8)]
    psctr = [0]

    class Ring:
        def __init__(self, banks):
            self.banks = list(banks)
            self.c = 0

        def next(self):
            i = self.banks[self.c % len(self.banks)]
            self.c += 1
            return psum[i], psB[i]

    ring_all = Ring(range(8))
    cur_ring = [ring_all]

    def ps_next():
        return cur_ring[0].next()

    def run_interleaved(gens):
        gens = list(gens)
        while gens:
            for g in list(gens):
                try:
                    cur_ring[0] = g[1]
                    next(g[0])
                except StopIteration:
                    gens.remove(g)
        cur_ring[0] = ring_all

    def mm(out, lhsT, rhs, start, stop, reads, writes):
        S.add("pe", lambda e: e.matmul(out, lhsT, rhs, start=start, stop=stop), reads, writes)

    def tr(out, in_, ident, reads, writes):
        S.add("pe", lambda e: e.transpose(out, in_, ident), reads, writes)

    def act(out, in_, func, reads, writes, bias=None, scale=None, accum_out=None):
        kw = {}
        if bias is not None:
            kw["bias"] = bias
        if scale is not None:
            kw["scale"] = scale
        if accum_out is not None:
            kw["accum_out"] = accum_out
        S.add("act", lambda e: e.activation(out, in_, func, **kw), reads, writes)

    def tt(eng, out, in0, in1, op, reads, writes):
        S.add(eng, lambda e: e.tensor_tensor(out, in0, in1, op), reads, writes)

    def ts(eng, out, in0, s1, s2, op0, op1, reads, writes):
        if op1 is None:
            S.add(eng, lambda e: e.tensor_scalar(out, in0, s1, None, op0), reads, writes)
        else:
            S.add(eng, lambda e: e.tensor_scalar(out, in0, s1, s2, op0, op1), reads, writes)

    def stt(out, in0, scalar, in1, op0, op1, reads, writes):
        S.add("dve", lambda e: e.scalar_tensor_tensor(out, in0, scalar, in1, op0, op1), reads, writes)

    def cp(eng, out, in_, reads, writes):
        if eng == "act":
            S.add("act", lambda e: e.copy(out, in_), reads, writes)
        else:
            S.add(eng, lambda e: e.tensor_copy(out, in_), reads, writes)

    def barrier():
        lasts = []
        for e in ENGS:
            for o in reversed(S.q[e]):
                if o.dma is None:
                    lasts.append(o)
                    break
        seen = set()
        for e in ENGS:
            for o in reversed(S.q[e]):
                if o.dma and o.dma[0] not in seen:
                    seen.add(o.dma[0])
                    lasts.append(o)
        for e in ENGS:
            b = Op(e, lambda eng: eng.nop(), None)
            for x in lasts:
                if x.dma or x.eng != e:
                    x.needed = True
                    b.deps.append(x)
            S.q[e].append(b)

    def wload(src, kc, ncols):
        i = wctr[0] % NW
        wctr[0] += 1
        t = wslots[i][:, 0:kc * ncols].rearrange("p (k c) -> p k c", k=kc)
        b = wB[i]
        step = 8
        for k0 in range(0, kc, step):
            k1 = min(kc, k0 + step)
            S.add("pool", lambda e, k0=k0, k1=k1: e.dma_start(out=t[:, k0:k1, :], in_=src[:, k0:k1, :]),
                  reads=(), writes=(b,), dma=f"w{i}")
        return t, b

    def bc(ap2, shape):
        return ap2.broadcast_to(list(shape))

    S.add("sp", lambda e: e.dma_start(out=small[:], in_=small_d), (), (smallB,), dma="small")
    S.add("sp", lambda e: e.dma_start(out=consts[:], in_=const_d), (), (constsB,), dma="consts")
    cp("dve", ident_bf[:], consts[:, C_IDENT:C_IDENT + P], (constsB,), (cbfB,))
    cp("dve", maskS_bf[:], consts[:, C_MASKS:C_MASKS + P], (constsB,), (cbfB,))
    cp("dve", ones_bf[:], consts[:, C_ONES:C_ONES + P], (constsB,), (cbfB,))
    cp("dve", negm_bf[:], consts[:, C_NEGM:C_NEGM + P], (constsB,), (cbfB,))
    S.add("pool", lambda e: e.memset(Sf[:], 0.0), (), (SfB,))
    S.add("pool", lambda e: e.memset(Sb[:], 0.0), (), (SbB,))
    S.add("pool", lambda e: e.memset(halo[:], 0.0), (), (haloB,))
    act(negA[:], small[:, SM_ALOG:SM_ALOG + 8], AF.Exp, (smallB,), (negAB,))
    ts("dve", negA[:], negA[:], -1.0, None, ALU.mult, None, (negAB,), (negAB,))
    ident_f = consts[:, C_IDENT:C_IDENT + P]
    ones_f = consts[:, C_ONES:C_ONES + P]
    maskS_f = consts[:, C_MASKS:C_MASKS + P]
    zcol = consts[:, C_ZERO:C_ZERO + 1]

    def gamma(idx):
        return small[:, SM_GAMMA + idx * KD: SM_GAMMA + (idx + 1) * KD]

    def row_rstd(n):
        ts("dve", rstd[:, 0:n], ssq[:, 0:n], 1.0 / D, RMS_EPS, ALU.mult, ALU.add, (ssqB,), (rstdB,))
        act(rstd[:, 0:n], rstd[:, 0:n], AF.Sqrt, (rstdB,), (rstdB,))
        S.add("dve", lambda e: e.reciprocal(rstd[:, 0:n], rstd[:, 0:n]), (rstdB,), (rstdB,))

    def sum_squares():
        for s in range(NS):
            if s % 2 == 0:
                act(junk, h[s][:], AF.Square, (hB[s],), (junkB, ssqB), accum_out=ssq[:, s:s + 1])
            else:
                S.add("dve", lambda e, s=s: e.scalar_tensor_tensor(hs[1], h[s][:], 1.0, h[s][:], ALU.mult, ALU.mult,
                                                                  accum_out=ssq[:, s:s + 1]),
                      (hB[s],), (hsB[1], ssqB))

    def rmsnorm_to_nT(gidx):
        g = gamma(gidx)
        sum_squares()
        row_rstd(NS)
        for s in range(NS):
            i = hsctr[0] % 2
            hsctr[0] += 1
            if s % 2 == 0:
                ts("dve", hs[i], h[s][:], rstd[:, s:s + 1], None, ALU.mult, None, (hB[s], rstdB), (hsB[i],))
            else:
                act(hs[i], h[s][:], AF.Copy, (hB[s], rstdB), (hsB[i],), scale=rstd[:, s:s + 1])
            for jg in range(KD // 4):
                pt, pb = ps_next()
                pv = pt[:].bitcast(BF16)
                for jj in range(4):
                    j = jg * 4 + jj
                    tr(pv[:, jj * P:(jj + 1) * P], hs[i][:, j * P:(j + 1) * P], ident_bf[:], (hsB[i], cbfB), (pb,))
                gb = bc(g[:, jg * 4:(jg + 1) * 4].unsqueeze(2), [P, 4, P])
                tt("dve", nT[:, jg * 4:(jg + 1) * 4, s * P:(s + 1) * P],
                   pv[:, 0:4 * P].rearrange("p (a b) -> p a b", a=4), gb, ALU.mult, (pb, smallB), (nTB,))

    def ffn(wgu, wd):
        wgu_v = wgu.rearrange("(k p) c -> p k c", p=P)
        wd_v = wd.rearrange("(k p) c -> p k c", p=P)
        for fg in range(F // 512):
            sgi = []
            for part, coff in ((0, 0), (1, F)):
                pss_ = [ps_next() for _ in range(4)]
                for kh in range(2):
                    wt, wtb = wload(wgu_v[:, kh * 8:(kh + 1) * 8, coff + fg * 512:coff + (fg + 1) * 512], 8, 512)
                    for kk in range(8):
                        k = kh * 8 + kk
                        for fl in range(4):
                            mm(pss_[fl][0][:], wt[:, kk, fl * P:(fl + 1) * P], nT[:, k, :], k == 0, k == KD - 1,
                               (wtb, nTB), (pss_[fl][1],))
                for fl in range(4):
                    if part == 0:
                        i = sgctr[0] % NSG
                        sgctr[0] += 1
                        sgi.append(i)
                        act(sg[i][:], pss_[fl][0][:], AF.Silu, (pss_[fl][1],), (sgB[i],))
                    else:
                        i = sgi[fl]
                        tt("dve", hid[:, fg * 4 + fl, :], sg[i][:], pss_[fl][0][:], ALU.mult,
                           (sgB[i], pss_[fl][1]), (hidB[fg * 4 + fl],))
        for cb in range(D // 512):
            pss = [ps_next() for _ in range(NS)]
            f0 = 0
            while f0 < KF:
                kq = min(8, KF - f0)
                wt, wtb = wload(wd_v[:, f0:f0 + kq, cb * 512:(cb + 1) * 512], kq, 512)
                for fl in range(kq):
                    f = f0 + fl
                    for s in range(NS):
                        mm(pss[s][0][:], hid[:, f, s * P:(s + 1) * P], wt[:, fl, :], f == 0, f == KF - 1,
                           (hidB[f], wtb), (pss[s][1],))
                f0 += kq
            for s in range(NS):
                hv = h[s][:, cb * 512:(cb + 1) * 512]
                stt(hv, pss[s][0][:], 0.5, hv, ALU.mult, ALU.add, (pss[s][1], hB[s]), (hB[s],))

    def proj_tok(wv, nk, c0, lhs_fn, lhs_bufs):
        pss = [ps_next() for _ in range(NS)]
        k0 = 0
        while k0 < nk:
            kq = min(8, nk - k0)
            wt, wtb = wload(wv[:, k0:k0 + kq, c0:c0 + 512], kq, 512)
            for kk in range(kq):
                k = k0 + kk
                for s in range(NS):
                    mm(pss[s][0][:], lhs_fn(k, s), wt[:, kk, :], k == 0, k == nk - 1,
                       tuple(lhs_bufs(k)) + (wtb,), (pss[s][1],))
            k0 += kq
        return pss

    win_v = win.rearrange("(k p) c -> p k c", p=P)

    def feat_blk(c0):
        if c0 < 3072:
            return c0 // 256
        return 12 + (c0 - 4112) // 256

    def proj_feat(c0, emit):
        wt, wtb = wload(winf[feat_blk(c0)].rearrange("p (k c) -> p k c", k=KD), KD, 256)
        for cl in range(2):
            pt, pb = ps_next()
            for k in range(KD):
                mm(pt[:], wt[:, k, cl * P:(cl + 1) * P], nT[:, k, :], k == 0, k == KD - 1, (wtb, nTB), (pb,))
            emit(cl, pt, pb)

    SB_SCALE = float(HD) ** -0.5
    Q_OFF, K_OFF, V_OFF, Z_OFF, A_OFF = 0, 1024, 2048, 3072, 4096
    QS_OFF, KS_OFF, VS_OFF = 4112, 5136, 6160

    def sb_phase(it):
        t0 = it * TT
        for blk in range(4):
            def emit_q(cl, pt, pb, blk=blk):
                hh = blk * 2 + cl
                cp("act", qTs[:, hh, :], pt[:], (pb,), (qTsB[hh],))
            proj_feat(QS_OFF + blk * 256, emit_q)
        for blk in range(4):
            def emit_k(cl, pt, pb, blk=blk):
                hh = blk * 2 + cl
                i = hh % 2
                cp("act", kstage[i], pt[:], (pb,), (kstageB[i],))
                S.add("sp", lambda e: e.dma_start(out=ksb_d[hh, :, t0:t0 + TT], in_=kstage[i]),
                      (kstageB[i],), (ksbB[hh],), dma=f"kst{i}")
            proj_feat(KS_OFF + blk * 256, emit_k)
        for cb in range(2):
            pss = proj_tok(win_v, KD, VS_OFF + cb * 512, lambda k, s: nT[:, k, s * P:(s + 1) * P], lambda k: (nTB,))
            for s in range(NS):
                i = s % 2
                cp("act", vstage[i][:, 0:512], pss[s][0][:], (pss[s][1],), (vstageB[i],))
                S.add("sp", lambda e, s=s, i=i, cb=cb: e.dma_start(
                    out=vsb_d[t0 + s * P:t0 + (s + 1) * P, cb * 512:(cb + 1) * 512], in_=vstage[i][:, 0:512]),
                    (vstageB[i],), (vsbB,), dma=f"vst{i}")
        nkb_tot = 4 * (it + 1)
        if stages == "b2":
            return
        for par in range(NPAR):
            S.add("pool", lambda e, par=par: e.memset(Pb[par][:, 0:1], 0.0), (), (PbB[par],))
        rings = [Ring([par]) for par in range(NPAR)]

        def sb_iter(hh, qs, par, kv):
            qb = 4 * it + qs
            nk = qb + 1
            nkeys = nk * P
            po, pob = psum[4 + par], psB[4 + par]
            tiles = []
            kt0 = 0
            while kt0 < nkeys:
                nkt = min(512, nkeys - kt0)
                tiles.append((kt0, nkt))
                kt0 += nkt
            ntl = len(tiles)
            for ti, (kt0, nkt) in enumerate(reversed(tiles)):
                diag = (kt0 + nkt == nkeys)
                pz, pzb = ps_next()
                mm(pz[:, 0:nkt], qTs[:, hh, qs * P:(qs + 1) * P], ktl[kv][:, kt0:kt0 + nkt], True, not diag,
                   (qTsB[hh], ktlB[kv]), (pzb,))
                if diag:
                    mm(pz[:, nkt - P:nkt], ident_bf[:], negm_bf[:], False, True, (cbfB,), (pzb,))
                act(et[par][:, 0:nkt], pz[:, 0:nkt], AF.Exp, (pzb,), (etB[par],), scale=SB_SCALE)
                act(spt[par][:, 0:nkt], et[par][:, 0:nkt], AF.Ln, (etB[par],), (sptB[par],), bias=1.0)
                yield
                S.add("dve", lambda e, par=par, nkt=nkt: e.tensor_tensor_scan(
                    Pb[par][:, 1:1 + nkt], spt[par][:, 0:nkt], bc(zcol, [P, nkt]),
                    Pb[par][:, 0:1], ALU.add, ALU.add), (sptB[par], PbB[par], constsB), (PbB[par],))
                if ti == 0:
                    ts("dve", ntot[:, par:par + 1], Pb[par][:, nkt:nkt + 1], -1.0, None, ALU.mult, None,
                       (PbB[par],), (ntotB2[par],))
                else:
                    stt(ntot[:, par:par + 1], Pb[par][:, nkt:nkt + 1], -1.0, ntot[:, par:par + 1], ALU.mult, ALU.add,
                        (PbB[par], ntotB2[par]), (ntotB2[par],))
                stt(lb[par][:, 0:nkt], pz[:, 0:nkt], SB_SCALE, Pb[par][:, 0:nkt], ALU.mult, ALU.add,
                    (pzb, PbB[par]), (lbB[par],))
                yield
                act(aa[par][:, 0:nkt], lb[par][:, 0:nkt], AF.Exp, (lbB[par], ntotB2[par]), (aaB[par],),
                    bias=ntot[:, par:par + 1])
                yield
                n = nkt // P
                pt, pb = ps_next()
                pv = pt[:].bitcast(BF16)
                for j in range(n):
                    tr(pv[:, j * P:(j + 1) * P], aa[par][:, j * P:(j + 1) * P], ident_bf[:], (aaB[par], cbfB), (pb,))
                cp("act", aT[par][:, 0:n, :], pv[:, 0:n * P].rearrange("p (a b) -> p a b", a=n), (pb,), (aTB[par],))
                yield
                for j in range(n):
                    kb = kt0 // P + j
                    mm(po[:, 0:P], vtl[kv][:, kb, :], aT[par][:, j, :], ti == 0 and j == 0,
                       ti == ntl - 1 and j == n - 1, (vtlB[kv], aTB[par]), (pob,))
                yield
            cp("act", omixT[:, 8 + hh, qs * P:(qs + 1) * P], po[:, 0:P], (pob,), (omixB[8 + hh],))

        def load_kv(hh):
            kv = hh % 2
            S.add("sp", lambda e: e.dma_start(out=ktl[kv][:, 0:nkb_tot * P], in_=ksb_d[hh, :, 0:nkb_tot * P]),
                  (ksbB[hh],), (ktlB[kv],), dma=f"ktl{kv}")
            S.add("sp", lambda e: e.dma_start(
                out=vtl[kv][:, 0:nkb_tot, :],
                in_=vsb_d[0:nkb_tot * P, hh * HD:(hh + 1) * HD].rearrange("(kb p) d -> p kb d", p=P)),
                (vsbB,), (vtlB[kv],), dma=f"vtl{kv}")

        if stages == "b3":
            for hh in range(NH):
                load_kv(hh)
            return
        todo = [(hh, qs) for hh in range(NH) for qs in range(NS)]
        active = {}
        free_slots = list(range(NPAR))
        loaded = set()
        remaining = {hh: NS for hh in range(NH)}
        while todo or active:
            while todo and free_slots:
                hh, qs = todo[0]
                if hh >= 2 and remaining[hh - 2] > 0:
                    break
                todo.pop(0)
                if hh not in loaded:
                    load_kv(hh)
                    loaded.add(hh)
                par = free_slots.pop(0)
                active[par] = (sb_iter(hh, qs, par, hh % 2), hh)
            for par in sorted(active):
                cur_ring[0] = rings[par]
                try:
                    next(active[par][0])
                except StopIteration:
                    remaining[active[par][1]] -= 1
                    del active[par]
                    free_slots.append(par)
        cur_ring[0] = ring_all

    def dn_inproj():
        wcache = {}
        wpending = [(grp, goff, blk) for grp, goff in ((0, Q_OFF), (1, K_OFF), (2, V_OFF)) for blk in range(4)]

        def wissue(n):
            for _ in range(n):
                if wpending:
                    grp, goff, blk = wpending.pop(0)
                    wcache[(grp, blk)] = wload(winf[feat_blk(goff + blk * 256)].rearrange("p (k c) -> p k c", k=KD), KD, 256)

        wissue(NW - 1)

        def qkv_chunk(grp, goff, hh, r):
            blk, cl = hh // 2, hh % 2
            if cl == 0:
                wissue(1)
            while (grp, blk) not in wcache:
                wissue(1)
            wt, wtb = wcache[(grp, blk)]
            cidx = grp * 8 + hh
            cw = small[:, SM_CONV + cidx * 4: SM_CONV + cidx * 4 + 4]
            pt, pb = ps_next()
            for k in range(KD):
                mm(pt[:], wt[:, k, cl * P:(cl + 1) * P], nT[:, k, :], k == 0, k == KD - 1, (wtb, nTB), (pb,))
            cp("dve", raw[r][:, 0:3], halo[:, cidx, :], (haloB,), (rawB[r],))
            cp("act", raw[r][:, 3:3 + TT], pt[:], (pb,), (rawB[r],))
            cp("dve", halo[:, cidx, :], raw[r][:, TT:TT + 3], (rawB[r],), (haloB,))
            yield
            ts("dve", cacc[r], raw[r][:, 0:TT], cw[:, 0:1], None, ALU.mult, None, (rawB[r], smallB), (caccB[r],))
            for k in range(1, 4):
                stt(cacc[r], raw[r][:, k:k + TT], cw[:, k:k + 1], cacc[r], ALU.mult, ALU.add,
                    (rawB[r], smallB, caccB[r]), (caccB[r],))
            yield
            if grp == 2:
                act(sqb[r], cacc[r], AF.Silu, (caccB[r],), (sqbB[r],))
                yield
                pt2, pb2 = ps_next()
                pv = pt2[:].bitcast(BF16)
                for s in range(NS):
                    tr(pv[:, s * P:(s + 1) * P], sqb[r][:, s * P:(s + 1) * P], ident_bf[:], (sqbB[r], cbfB), (pb2,))
                cp("dve", vtok[:, :, hh, :], pv[:, 0:TT].rearrange("p (s d) -> p s d", s=NS), (pb2,), (vtokB,))
                return
            act(csil[r], cacc[r], AF.Silu, (caccB[r],), (csilB[r],))
            act(sqb[r], csil[r], AF.Square, (csilB[r],), (sqbB[r],))
            yield
            pt2, pb2 = ps_next()
            mm(pt2[:], ones_bf[:], sqb[r], True, True, (cbfB, sqbB[r]), (pb2,))
            act(rinv[r], pt2[:], AF.Sqrt, (pb2,), (rinvB[r],), bias=L2_EPS)
            yield
            S.add("dve", lambda e: e.reciprocal(rinv[r], rinv[r]), (rinvB[r],), (rinvB[r],))
            if grp == 0:
                stt(qTd[:, hh, :], csil[r], float(HD) ** -0.5, rinv[r], ALU.mult, ALU.mult,
                    (csilB[r], rinvB[r]), (qTdB[hh],))
            else:
                tt("dve", kTd[:, hh, :], csil[r], rinv[r], ALU.mult, (csilB[r], rinvB[r]), (kTdB[hh],))
                yield
                pt3, pb3 = ps_next()
                pv = pt3[:].bitcast(BF16)
                for s in range(NS):
                    tr(pv[:, s * P:(s + 1) * P], kTd[:, hh, s * P:(s + 1) * P], ident_bf[:], (kTdB[hh], cbfB), (pb3,))
                cp("dve", ktok[:, :, hh, :], pv[:, 0:TT].rearrange("p (s d) -> p s d", s=NS), (pb3,), (ktokB,))

        todo = [(grp, goff, hh) for grp, goff in ((0, Q_OFF), (1, K_OFF), (2, V_OFF)) for hh in range(NH)]
        ringsQ = [Ring([r]) for r in range(NSET)]
        active = {}
        free_sets = list(range(NSET))
        while todo or active:
            while todo and free_sets:
                r = free_sets.pop(0)
                grp, goff, hh = todo.pop(0)
                active[r] = qkv_chunk(grp, goff, hh, r)
            for r in sorted(active):
                cur_ring[0] = ringsQ[r]
                try:
                    next(active[r])
                except StopIteration:
                    del active[r]
                    free_sets.append(r)
        cur_ring[0] = ring_all
        for cb in range(2):
            pss = proj_tok(win_v, KD, Z_OFF + cb * 512, lambda k, s: nT[:, k, s * P:(s + 1) * P], lambda k: (nTB,))
            for s in range(NS):
                act(zs[:, s, cb * 512:(cb + 1) * 512], pss[s][0][:], AF.Silu, (pss[s][1],), (zsB,))
        wt, wtb = wload(winab.rearrange("p (k c) -> p k c", k=KD), KD, 16)
        pab, pabb = ps_next()
        for s in range(NS):
            for k in range(KD):
                mm(pab[:, s * 16:(s + 1) * 16], nT[:, k, s * P:(s + 1) * P], wt[:, k, :], k == 0, k == KD - 1,
                   (nTB, wtb), (pabb,))
        pab3 = pab[:, 0:NS * 16].rearrange("p (s c) -> p s c", s=NS)
        tt("dve", gtmp[:], pab3[:, :, 0:8], bc(small[:, SM_DTB:SM_DTB + 8].unsqueeze(1), [P, NS, 8]), ALU.add,
           (pabb, smallB), (gtmpB,))
        act(gtmp[:], gtmp[:], AF.Exp, (gtmpB,), (gtmpB,))
        act(gtmp[:], gtmp[:], AF.Ln, (gtmpB,), (gtmpB,), bias=1.0)
        tt("dve", graw[:], gtmp[:], bc(negA[:].unsqueeze(1), [P, NS, 8]), ALU.mult, (gtmpB, negAB), (grawB,))
        act(beta[:], pab3[:, :, 8:16], AF.Sigmoid, (pabb,), (betaB,))

    dlev = int(stages[2:]) if stages.startswith("dc") else 99

    def dn_chunks():
        TRIKI = consts[:, C_TRIKI:C_TRIKI + CH]
        UPS = consts[:, C_UPS:C_UPS + CH]
        NEGTRI = consts[:, C_NEGTRI:C_NEGTRI + CH]
        NMLS = consts[:, C_NMLS:C_NMLS + CH]
        PMUI = consts[:, C_PMUI:C_PMUI + CH]
        I2 = consts[:, C_I2:C_I2 + CH]
        HALVES = ((0, slice(0, CH)), (1, slice(CH, P)))

        def f3(ap):
            return ap.rearrange("p (h j) -> p h j", h=NH)

        def hc_(hh):
            return slice(hh * CH, (hh + 1) * CH)

        def wy(pr):
            s = pr
            hb = pr % 2
            ATi, Kd, ek, sdecp = ATi_h[hb], Kd_h[hb], ek_h[hb], sdecp_h[hb]
            Ut0, Ut1 = Ut_h[hb]
            WTs = WT_h[hb]
            g_s = graw[:, s, :]
            b_s = beta[:, s, :]
            yield
            cp("dve", f3(G1), bc(g_s.unsqueeze(2), [P, NH, CH]), (grawB,), (G1B[0],))
            tt("dve", f3(G2), bc(g_s.unsqueeze(2), [P, NH, CH]), bc(NEGTRI.unsqueeze(1), [P, NH, CH]),
               ALU.mult, (grawB, constsB), (G2B[0],))
            pd, pdb = ps_next()
            pm, pmb = ps_next()
            for hf, sl in HALVES:
                mm(pd[sl, :], TRIKI[sl, :], G1[sl, :], True, False, (constsB, G1B[0]), (pdb,))
                mm(pd[sl, :], ones_f[sl, 0:CH], G2[sl, :], False, True, (constsB, G2B[0]), (pdb,))
            for hf, sl in HALVES:
                mm(pm[sl, 0:8], TRIKI[sl, :], graw[sl, s, :], True, True, (constsB, grawB), (pmb,))
                mm(pm[sl, 8:16], UPS[sl, :], graw[sl, s, :], True, True, (constsB, grawB), (pmb,))
                mm(pm[:, 16 + 8 * hf:24 + 8 * hf], ones_f[sl, :], graw[sl, s, :], True, True, (constsB, grawB), (pmb,))
            act(ek[:, :], pm[:, 0:16], AF.Exp, (pmb,), (ekB[hb],))
            act(sdecp[:, :], pm[:, 16:32], AF.Exp, (pmb,), (sdecB[hb],))
            stt(f3(E1), f3(pd[:]), 0.0, bc(NMLS.unsqueeze(1), [P, NH, CH]), ALU.min, ALU.add, (pdb, constsB), (E1B[0],))
            act(E1, E1, AF.Exp, (E1B[0],), (E1B[0],))
            stt(f3(E2), f3(pd[:]), 0.0, bc(PMUI.unsqueeze(1), [P, NH, CH]), ALU.max, ALU.add, (pdb, constsB), (E2B[0],))
            act(E2, E2, AF.Exp, (E2B[0],), (E2B[0],), scale=-1.0)
            yield
            pk, pkb = ps_next()
            pq, pqb = ps_next()
            for hh in range(NH):
                for hf, sl in HALVES:
                    cs_ = slice((2 * pr + hf) * CH, (2 * pr + hf + 1) * CH)
                    mm(pk[sl, hc_(hh)], kTd[:, hh, cs_], kTd[:, hh, cs_], True, True, (kTdB[hh],), (pkb,))
            for hh in range(NH):
                for hf, sl in HALVES:
                    cs_ = slice((2 * pr + hf) * CH, (2 * pr + hf + 1) * CH)
                    mm(pq[sl, hc_(hh)], kTd[:, hh, cs_], qTd[:, hh, cs_], True, True, (kTdB[hh], qTdB[hh]), (pqb,))
            tt("dve", Lc, pk[:], E1, ALU.mult, (pkb, E1B[0]), (LcB[0],))
            tt("dve", f3(Lc), f3(Lc), bc(b_s.unsqueeze(2), [P, NH, CH]), ALU.mult, (LcB[0], betaB), (LcB[0],))
            tt("dve", ATi, pq[:], E2, ALU.mult, (pqb, E2B[0]), (ATiB[hb],))
            yield
            pa, pab_ = ps_next()
            for hh in range(NH):
                for hf, sl in HALVES:
                    mm(pa[sl, hc_(hh)], Lc[sl, hc_(hh)], I2[sl, :], True, True, (LcB[0], constsB), (pab_,))
            cp("dve", Am, pa[:], (pab_,), (AB_[0],))
            stt(f3(Pm), f3(Am), -1.0, bc(I2.unsqueeze(1), [P, NH, CH]), ALU.mult, ALU.add, (AB_[0], constsB), (PB_[0],))
            yield
            p1, p1b = ps_next()
            p2, p2b = ps_next()
            for hh in range(NH):
                for hf, sl in HALVES:
                    mm(p1[sl, hc_(hh)], Lc[sl, hc_(hh)], Am[sl, hc_(hh)], True, True, (LcB[0], AB_[0]), (p1b,))
            for hh in range(NH):
                for hf, sl in HALVES:
                    mm(p2[sl, hc_(hh)], Am[sl, hc_(hh)], Lc[sl, hc_(hh)], True, True, (LcB[0], AB_[0]), (p2b,))
            cp("dve", Am, p1[:], (p1b,), (AB_[0],))
            cp("dve", Lc, p2[:], (p2b,), (LcB[0],))
            yield
            for lvl in range(1, 5):
                pA, pAb = ps_next()
                pP, pPb = ps_next()
                pl, plb = ps_next()
                for hh in range(NH):
                    for hf, sl in HALVES:
                        mm(pA[sl, hc_(hh)], Lc[sl, hc_(hh)], Am[sl, hc_(hh)], True, True, (LcB[0], AB_[0]), (pAb,))
                for hh in range(NH):
                    for hf, sl in HALVES:
                        mm(pP[sl, hc_(hh)], Lc[sl, hc_(hh)], Pm[sl, hc_(hh)], True, True, (LcB[0], PB_[0]), (pPb,))
                for hh in range(NH):
                    for hf, sl in HALVES:
                        mm(pl[sl, hc_(hh)], Am[sl, hc_(hh)], Lc[sl, hc_(hh)], True, True, (LcB[0], AB_[0]), (plb,))
                tt("dve", Pm, pP[:], Pm, ALU.add, (pPb, PB_[0]), (PB_[0],))
                cp("dve", Am, pA[:], (pAb,), (AB_[0],))
                cp("dve", Lc, pl[:], (plb,), (LcB[0],))
                yield
            yield
            p5, p5b = ps_next()
            for hh in range(NH):
                for hf, sl in HALVES:
                    mm(p5[sl, hc_(hh)], Lc[sl, hc_(hh)], Pm[sl, hc_(hh)], True, True, (LcB[0], PB_[0]), (p5b,))
            tt("dve", MT, p5[:], Pm, ALU.add, (p5b, PB_[0]), (MTB[0],))
            yield
            h3 = lambda ap: ap.rearrange("p (h d) -> p h d", h=NH)
            tt("dve", h3(Vb), vtok[:, s, :, :], bc(b_s.unsqueeze(2), [P, NH, HD]), ALU.mult, (vtokB, betaB), (VbB[0],))
            tt("dve", bk[:, :], b_s, ek[:, 0:8], ALU.mult, (betaB, ekB[hb]), (bkB[0],))
            tt("dve", h3(Kbg), ktok[:, s, :, :], bc(bk[:, :].unsqueeze(2), [P, NH, HD]), ALU.mult, (ktokB, bkB[0]), (KbgB[0],))
            tt("dve", h3(Kd), ktok[:, s, :, :], bc(ek[:, 8:16].unsqueeze(2), [P, NH, HD]), ALU.mult, (ktokB, ekB[hb]), (KdB[hb],))
            yield
            pua, puab = ps_next()
            pub_, pubb = ps_next()
            pws = [ps_next(), ps_next()]
            for hh in range(NH):
                px, pxB_ = (pua, puab) if hh < 4 else (pub_, pubb)
                o0 = (hh % 4) * HD
                for hf, sl in HALVES:
                    mm(px[sl, o0:o0 + HD], MT[sl, hc_(hh)], Vb[sl, hh * HD:(hh + 1) * HD], True, True,
                       (MTB[0], VbB[0]), (pxB_,))
            for hh in range(NH):
                for hf, sl in HALVES:
                    mm(pws[hf][0][:, hc_(hh)], Kbg[sl, hh * HD:(hh + 1) * HD], MT[sl, hc_(hh)], True, True,
                       (KbgB[0], MTB[0]), (pws[hf][1],))
            cp("dve", Ut0, pua[:], (puab,), (UtB[hb],))
            cp("dve", Ut1, pub_[:], (pubb,), (UtB[hb],))
            for hf, sl in HALVES:
                cp("act", WTs[hf], pws[hf][0][:], (pws[hf][1],), (WTBh[hb][hf],))
        def seq(pr):
            s = pr
            hb = pr % 2
            ATi, Kd, ek, sdecp = ATi_h[hb], Kd_h[hb], ek_h[hb], sdecp_h[hb]
            Ut0, Ut1 = Ut_h[hb]
            WTs = WT_h[hb]
            for hf, sl in HALVES:
                c = 2 * pr + hf
                cs_ = slice(c * CH, (c + 1) * CH)
                WT = WTs[hf]
                pwa, pwab = ps_next()
                pwb2, pwbb2 = ps_next()
                for hh in range(NH):
                    px, pxB_ = (pwa, pwab) if hh < 4 else (pwb2, pwbb2)
                    o0 = (hh % 4) * HD
                    mm(px[sl, o0:o0 + HD], WT[:, hc_(hh)], Sb[:, hh, :], True, True, (WTBh[hb][hf], SbB), (pxB_,))
                tt("dve", vn[sl, 0:512], Ut0[sl, :], pwa[sl, :], ALU.subtract, (UtB[hb], pwab), (vnB[hf],))
                tt("dve", vn[sl, 512:1024], Ut1[sl, :], pwb2[sl, :], ALU.subtract, (UtB[hb], pwbb2), (vnB[hf],))
                yield
                pqa, pqab = ps_next()
                pqb2, pqbb2 = ps_next()
                for hh in range(NH):
                    px, pxB_ = (pqa, pqab) if hh < 4 else (pqb2, pqbb2)
                    o0 = (hh % 4) * HD
                    mm(px[sl, o0:o0 + HD], qTd[:, hh, cs_], Sb[:, hh, :], True, True, (qTdB[hh], SbB), (pxB_,))
                pva, pvab = ps_next()
                pvb2, pvbb2 = ps_next()
                for hh in range(NH):
                    px, pxB_ = (pva, pvab) if hh < 4 else (pvb2, pvbb2)
                    o0 = (hh % 4) * HD
                    mm(px[sl, o0:o0 + HD], ATi[sl, hc_(hh)], vn[sl, hh * HD:(hh + 1) * HD], True, True,
                       (ATiB[hb], vnB[hf]), (pxB_,))
                for half, (pqx, pqxb, pvx, pvxb) in enumerate(((pqa, pqab, pva, pvab), (pqb2, pqbb2, pvb2, pvbb2))):
                    cols = slice(half * 512, half * 512 + 512)
                    eg = bc(ek[sl, half * 4:half * 4 + 4].unsqueeze(2), [CH, 4, HD])
                    tt("dve", otmp[sl, cols].rearrange("p (h d) -> p h d", h=4),
                       pqx[sl, :].rearrange("p (h d) -> p h d", h=4), eg, ALU.mult, (pqxb, ekB[hb]), (otmpB[hf],))
                    tt("dve", osub[sl, cols], otmp[sl, cols], pvx[sl, :], ALU.add, (otmpB[hf], pvxb), (osubB[hf],))
                yield
                psa, psab = ps_next()
                psb2, psbb2 = ps_next()
                for hh in range(NH):
                    px, pxB_ = (psa, psab) if hh < 4 else (psb2, psbb2)
                    o0 = (hh % 4) * HD
                    mm(px[:, o0:o0 + HD], Kd[sl, hh * HD:(hh + 1) * HD], vn[sl, hh * HD:(hh + 1) * HD], True, True,
                       (KdB[hb], vnB[hf]), (pxB_,))
                tt("pool", Sf[:], Sf[:], bc(sdecp[:, 8 * hf:8 * hf + 8].unsqueeze(2), [P, NH, HD]), ALU.mult,
                   (SfB, sdecB[hb]), (SfB,))
                Sf2 = Sf[:].rearrange("p h d -> p (h d)")
                tt("dve", Sf2[:, 0:512], Sf2[:, 0:512], psa[:], ALU.add, (SfB, psab), (SfB,))
                tt("dve", Sf2[:, 512:1024], Sf2[:, 512:1024], psb2[:], ALU.add, (SfB, psbb2), (SfB,))
                cp("act", Sb[:], Sf[:], (SfB,), (SbB,))
                yield
            o3 = osub.rearrange("p (h d) -> p h d", h=NH)
            tt("dve", sqo, osub, osub, ALU.mult, (osubB[0], osubB[1]), (sqoB,))
            S.add("dve", lambda e: e.tensor_reduce(oss[:], sqo.rearrange("p (h d) -> p h d", h=NH), AX.X, ALU.add),
                  (sqoB,), (ossB,))
            ts("dve", oss[:], oss[:], 1.0 / HD, RMS_EPS, ALU.mult, ALU.add, (ossB,), (ossB,))
            act(oss[:], oss[:], AF.Sqrt, (ossB,), (ossB,))
            S.add("dve", lambda e: e.reciprocal(oss[:], oss[:]), (ossB,), (ossB,))
            tt("dve", o3, o3, bc(oss[:].unsqueeze(2), [P, NH, HD]), ALU.mult, (osubB[0], osubB[1], ossB), (osubB[0], osubB[1]))
            tt("dve", o3, o3, bc(small[:, SM_ONORM:SM_ONORM + HD].unsqueeze(1), [P, NH, HD]), ALU.mult,
               (osubB[0], osubB[1], smallB), (osubB[0], osubB[1]))
            tt("dve", ob, osub, zs[:, s, :], ALU.mult, (osubB[0], osubB[1], zsB), (obB,))
            for g4 in range(2):
                pt, pb = ps_next()
                pv = pt[:].bitcast(BF16)
                for j in range(4):
                    hh = g4 * 4 + j
                    tr(pv[:, j * P:(j + 1) * P], ob[:, hh * HD:(hh + 1) * HD], ident_bf[:], (obB, cbfB), (pb,))
                for j in range(4):
                    hh = g4 * 4 + j
                    cp("act", omixT[:, hh, s * P:(s + 1) * P], pv[:, j * P:(j + 1) * P], (pb,), (omixB[hh],))

        ringW, ringS = Ring([0, 1, 2, 3]), Ring([4, 5, 6, 7])
        NPR = TT // CH // 2
        cur_ring[0] = ringW
        for _ in wy(0):
            pass
        for pr in range(NPR):
            gens = [(seq(pr), ringS)]
            if pr + 1 < NPR:
                gens.append((wy(pr + 1), ringW))
            run_interleaved(gens)
        cur_ring[0] = ring_all

    def out_proj():
        wv = wout.rearrange("(k p) c -> p k c", p=P)
        for cb in range(D // 512):
            pss = proj_tok(wv, KD, cb * 512, lambda k, s: omixT[:, k, s * P:(s + 1) * P], lambda k: (omixB[k],))
            for s in range(NS):
                hv = h[s][:, cb * 512:(cb + 1) * 512]
                tt("dve", hv, pss[s][0][:], hv, ALU.add, (pss[s][1], hB[s]), (hB[s],))

    def ple(it):
        t0 = it * TT
        S.add("sp", lambda e: e.dma_start(out=ptok[:], in_=p_d[t0:t0 + TT, :].rearrange("(s p) c -> p s c", p=P)),
              (), ptokBs, dma="ptok")
        cp("dve", ptokb[:], ptok[:], ptokBs, ptokbBs)
        for s in range(NS):
            pt, pb = ps_next()
            pv = pt[:].bitcast(BF16)
            for k in range(2):
                tr(pv[:, k * P:(k + 1) * P], ptokb[:, s, k * P:(k + 1) * P], ident_bf[:], ptokbBs + (cbfB,), (pb,))
            cp("dve", pT[:, :, s * P:(s + 1) * P], pv[:, 0:2 * P].rearrange("p (k t) -> p k t", k=2), (pb,), pTBs)
        wgv = wpg.rearrange("(k p) c -> p k c", p=P)
        wpv = wpp.rearrange("(k p) c -> p k c", p=P)
        for cb in range(D // 512):
            psg = proj_tok(wgv, KD, cb * 512, lambda k, s: nT[:, k, s * P:(s + 1) * P], lambda k: (nTB,))
            psp = proj_tok(wpv, 2, cb * 512, lambda k, s: pT[:, k, s * P:(s + 1) * P], lambda k: pTBs)
            for s in range(NS):
                i = sgctr[0] % NSG
                sgctr[0] += 1
                act(sg[i][:], psg[s][0][:], AF.Sigmoid, (psg[s][1],), (sgB[i],))
                tt("dve", sg[i][:], sg[i][:], psp[s][0][:], ALU.mult, (sgB[i], psp[s][1]), (sgB[i],))
                hv = h[s][:, cb * 512:(cb + 1) * 512]
                tt("dve", hv, hv, sg[i][:], ALU.add, (hB[s], sgB[i]), (hB[s],))

    def final_norm_store(it):
        t0 = it * TT
        S.add("sp", lambda e: e.dma_start(out=finbc, in_=fin_d), (), finbcBs, dma="finbc")
        sum_squares()
        row_rstd(NS)
        for s in range(NS):
            stt(h[s][:], h[s][:], rstd[:, s:s + 1], finbc, ALU.mult, ALU.mult, (hB[s], rstdB) + finbcBs, (hB[s],))
            S.add("sp", lambda e, s=s: e.dma_start(out=out_d[t0 + s * P:t0 + (s + 1) * P, :], in_=h[s][:]),
                  (hB[s],), (), dma=f"o{s}")

    for it in range(NT):
        t0 = it * TT
        for s in range(NS):
            S.add("sp", lambda e, s=s, t0=t0: e.dma_start(out=h[s][:], in_=x_d[t0 + s * P:t0 + (s + 1) * P, :]),
                  (), (hB[s],), dma=f"x{s}")
        rmsnorm_to_nT(0)
        ffn(w1gu, w1d)
        if stages == "ffn1":
            for s in range(NS):
                S.add("sp", lambda e, s=s, t0=t0: e.dma_start(out=out_d[t0 + s * P:t0 + (s + 1) * P, :], in_=h[s][:]),
                      (hB[s],), (), dma=f"o{s}")
            continue
        barrier()
        rmsnorm_to_nT(1)
        if stages != "b1":
            sb_phase(it)
        barrier()
        if stages in ("b1", "b2", "b3", "b4"):
            for s in range(NS):
                S.add("sp", lambda e, s=s, t0=t0: e.dma_start(out=out_d[t0 + s * P:t0 + (s + 1) * P, :], in_=h[s][:]),
                      (hB[s],), (), dma=f"o{s}")
            continue
        if stages in ("nodn", "nodnmix"):
            for c in range(8):
                S.add("pool", lambda e, c=c: e.memset(omixT[:, c, :], 0.0), (), (omixB[c],))
        else:
            dn_inproj()
            barrier()
            if dlev > 0:
                dn_chunks()
            if dlev < 99:
                barrier()
                for c in range(8):
                    S.add("pool", lambda e, c=c: e.memset(omixT[:, c, :], 0.0), (), (omixB[c],))
        out_proj()
        barrier()
        if stages in ("mix", "nodnmix") or dlev < 99:
            for s in range(NS):
                S.add("sp", lambda e, s=s, t0=t0: e.dma_start(out=out_d[t0 + s * P:t0 + (s + 1) * P, :], in_=h[s][:]),
                      (hB[s],), (), dma=f"o{s}")
            continue
        rmsnorm_to_nT(2)
        ffn(w2gu, w2d)
        rmsnorm_to_nT(3)
        ple(it)
        final_norm_store(it)

    last = {}
    for o in S.q["sp"]:
        if o.dma and o.dma[0].startswith("o"):
            last[o.dma[0]] = o
    endop = Op("sp", lambda e: e.nop(), None)
    for o in last.values():
        endop.deps.append(o)
    S.q["sp"].append(endop)

    with nc.Block() as block:
        S.emit(nc, es, block)
    es.close()
    return nc


def make_small(inp):
    sm = np.zeros((P, SMALL_COLS), np.float32)
    for i, k in enumerate(("ffn1_norm", "mix_norm", "ffn2_norm", "ple_norm")):
        sm[:, SM_GAMMA + i * KD: SM_GAMMA + (i + 1) * KD] = np.asarray(inp[k], np.float32).reshape(KD, P).T
    cw = np.asarray(inp["dn_conv"], np.float32).reshape(4, 24, P)
    sm[:, SM_CONV:SM_CONV + 96] = cw.transpose(2, 1, 0).reshape(P, 96)
    sm[:, SM_ALOG:SM_ALOG + 8] = np.asarray(inp["dn_a_log"], np.float32).reshape(1, 8)
    sm[:, SM_DTB:SM_DTB + 8] = np.asarray(inp["dn_dt_bias"], np.float32).reshape(1, 8)
    sm[:, SM_ONORM:SM_ONORM + 128] = np.asarray(inp["dn_out_norm"], np.float32).reshape(1, 128)
    return sm


def make_consts():
    c = np.zeros((P, CONST_COLS), np.float32)
    c[:, C_IDENT:C_IDENT + P] = np.eye(P, dtype=np.float32)
    q = np.arange(P)[:, None]
    k = np.arange(P)[None, :]
    c[:, C_MASKS:C_MASKS + P] = (k < q)
    c[:, C_ONES:C_ONES + P] = 1.0
    pm = (np.arange(P) % CH)[:, None]
    j = np.arange(CH)[None, :]
    c[:, C_TRIKI:C_TRIKI + CH] = (pm <= j)
    c[:, C_UPS:C_UPS + CH] = (pm > j)
    c[:, C_NEGTRI:C_NEGTRI + CH] = -(pm <= j).astype(np.float32)
    c[:, C_NMLS:C_NMLS + CH] = np.where(pm > j, 0.0, NEG)
    c[:, C_PMUI:C_PMUI + CH] = np.where(j >= pm, 0.0, -NEG)
    c[:, C_I2:C_I2 + CH] = (pm == j)
    c[:, C_NEGM:C_NEGM + P] = np.where(k < q, 0.0, NEGB)
    return c


_NC_CACHE = {}


def run(inputs, T=2048, ncores=8, stages="all"):
    import os
    key = (T, stages)
    if key not in _NC_CACHE:
        _NC_CACHE[key] = build(T, stages)
    nc = _NC_CACHE[key]
    sm = make_small(inputs)
    cs = make_consts()
    fin_bc = np.ascontiguousarray(np.broadcast_to(np.asarray(inputs["final_norm"], np.float32).reshape(1, D), (P, D)))
    shared = {
        "ffn1_w_gu": np.ascontiguousarray(inputs["ffn1_w_gu"][0]),
        "ffn1_w_down": np.ascontiguousarray(inputs["ffn1_w_down"][0]),
        "w_in": np.ascontiguousarray(inputs["w_in"][0]),
        "w_out": np.ascontiguousarray(inputs["w_out"][0]),
        "ffn2_w_gu": np.ascontiguousarray(inputs["ffn2_w_gu"][0]),
        "ffn2_w_down": np.ascontiguousarray(inputs["ffn2_w_down"][0]),
        "ple_w_gate": np.ascontiguousarray(inputs["ple_w_gate"][0]),
        "ple_w_proj": np.ascontiguousarray(inputs["ple_w_proj"][0]),
        "small": sm, "consts": cs, "final_bc": fin_bc,
    }
    w_in0 = np.asarray(inputs["w_in"][0], np.float32)
    cols = np.concatenate([np.arange(0, 3072), np.arange(4112, 6160)])
    wf = w_in0[:, cols].reshape(KD, P, 20, 256).transpose(2, 1, 0, 3)
    shared["w_in_feat"] = np.ascontiguousarray(wf).reshape(20, P, KD * 256)
    wab = w_in0[:, 4096:4112].reshape(KD, P, 16).transpose(1, 0, 2)
    shared["w_in_ab"] = np.ascontiguousarray(wab).reshape(P, KD * 16)
    in_maps = []
    for c in range(ncores):
        m = dict(shared)
        m["x"] = np.ascontiguousarray(inputs["x"][c, :T])
        m["p"] = np.ascontiguousarray(inputs["p"][0, c, :T])
        in_maps.append(m)
    trace = bool(os.environ.get("K_TRACE"))
    res = run_bass_kernel_spmd(nc, in_maps, core_ids=list(range(ncores)), trace=trace)
    if trace:
        print("exec_time_ns", res.exec_time_ns)
    return np.stack([np.asarray(r["out"]) for r in res.results], axis=0)


def kernel(**inputs):
    inputs = {k: np.asarray(v) for k, v in inputs.items()}
    return run(inputs, T=2048, ncores=8).astype(np.float32)
```

```python
import numpy as np
from contextlib import ExitStack
import concourse.bass as bass
import concourse.mybir as mybir
from concourse.bass_utils import run_bass_kernel_spmd

F32 = mybir.dt.float32
BF16 = mybir.dt.bfloat16
AF = mybir.ActivationFunctionType
ALU = mybir.AluOpType
AX = mybir.AxisListType

P = 128
D = 2048
F = 5632
KD = D // P
KF = F // P
TT = 512
NS = TT // P
NH = 8
HD = 128
CH = 64
IN_COLS = 7184
PLE = 256
RMS_EPS = 1e-6
L2_EPS = 1e-6
NEG = -30000.0

ENGS = ("pe", "act", "dve", "pool", "sp")


class Buf:
    __slots__ = ("name", "w", "r")

    def __init__(self, name):
        self.name = name
        self.w = []
        self.r = []


class Op:
    __slots__ = ("eng", "fn", "dma", "deps", "needed", "val")

    def __init__(self, eng, fn, dma):
        self.eng = eng
        self.fn = fn
        self.dma = dma
        self.deps = []
        self.needed = False
        self.val = 0

    def key(self):
        return ("d", self.dma[0]) if self.dma else ("e", self.eng)


def _push(lst, o):
    k = o.key()
    lst[:] = [x for x in lst if x.key() != k]
    lst.append(o)


class Sched:
    def __init__(self):
        self.q = {e: [] for e in ENGS}
        self.dcnt = {}

    def add(self, eng, fn, reads=(), writes=(), dma=None):
        if dma is not None:
            c = self.dcnt.get(dma, 0) + 16
            self.dcnt[dma] = c
            o = Op(eng, fn, (dma, c))
        else:
            o = Op(eng, fn, None)
        deps = []
        for b in reads:
            for x in b.w:
                if x.dma or o.dma or x.eng != o.eng or o.eng != "pe":
                    deps.append(x)
        for b in writes:
            for x in b.r + b.w:
                if x.dma or o.dma or x.eng != o.eng:
                    deps.append(x)
        for b in reads:
            _push(b.r, o)
        for b in writes:
            if b.r and not (len(b.r) == 1 and b.r[0] is o):
                b.w = [o]
                b.r = [x for x in b.r if x is o]
            else:
                _push(b.w, o)
        for x in deps:
            if x is not o:
                x.needed = True
                o.deps.append(x)
        self.q[eng].append(o)
        return o

    def emit(self, nc, es, block):
        sems = {}
        for e in ENGS:
            sems[("e", e)] = es.enter_context(nc.semaphore("s_" + e))
            c = 0
            for o in self.q[e]:
                if o.dma is None and o.needed:
                    c += 1
                    o.val = c
        for d in self.dcnt:
            sems[("d", d)] = es.enter_context(nc.semaphore("d_" + d))

        def body(ename):
            def run(eng):
                seen = {}
                for o in self.q[ename]:
                    need = {}
                    for x in o.deps:
                        k = x.key()
                        v = x.dma[1] if x.dma else x.val
                        if seen.get(k, 0) >= v:
                            continue
                        if need.get(k, 0) < v:
                            need[k] = v
                    for k, v in need.items():
                        seen[k] = v
                        eng.wait_ge(sems[k], v)
                    ins = o.fn(eng)
                    if o.dma:
                        ins.then_inc(sems[o.key()], 16)
                    elif o.needed:
                        ins.then_inc(sems[o.key()], 1)
            return run

        block.tensor(body("pe"))
        block.scalar(body("act"))
        block.vector(body("dve"))
        block.gpsimd(body("pool"))
        block.sync(body("sp"))


SM_GAMMA = 0
SM_CONV = 64
SM_ALOG = 160
SM_DTB = 168
SM_ONORM = 176
SMALL_COLS = 304

C_IDENT = 0
C_MASKS = 128
C_ONES = 256
C_TRIKI = 384
C_UPS = 448
C_NEGTRI = 512
C_NMLS = 576
C_PMUI = 640
C_ZERO = 704
C_I2 = 705
C_NEGM = 769
CONST_COLS = 897
NEGB = -30000.0 * (128.0 ** 0.5)

KB = 1024
import os
DSUB = int(os.environ.get("DSUB", "0"))


def build(T, stages="all"):
    NT = T // TT
    nc = bass.Bass("TRN2", target_bir_lowering=False)

    def din(name, shape):
        return nc.dram_tensor(name, list(shape), F32, kind="ExternalInput").ap()

    x_d = din("x", [T, D])
    p_d = din("p", [T, PLE])
    w1gu = din("ffn1_w_gu", [D, 2 * F])
    w1d = din("ffn1_w_down", [F, D])
    win = din("w_in", [D, IN_COLS])
    winf = din("w_in_feat", [20, P, KD * 256])
    winab = din("w_in_ab", [P, KD * 16])
    wout = din("w_out", [D, D])
    w2gu = din("ffn2_w_gu", [D, 2 * F])
    w2d = din("ffn2_w_down", [F, D])
    wpg = din("ple_w_gate", [D, D])
    wpp = din("ple_w_proj", [PLE, D])
    small_d = din("small", [P, SMALL_COLS])
    const_d = din("consts", [P, CONST_COLS])
    fin_d = din("final_bc", [P, D])
    out_d = nc.dram_tensor("out", [T, D], F32, kind="ExternalOutput").ap()
    ksb_d = nc.dram_tensor("ksb_scr", [NH, P, T], BF16, kind="Internal").ap()
    vsb_d = nc.dram_tensor("vsb_scr", [T, NH * HD], BF16, kind="Internal").ap()
    ksbB = [Buf(f"ksbd{hh}") for hh in range(NH)]
    vsbB = Buf("vsbd")

    S = Sched()
    es = ExitStack()

    def sb(name, shape, dt=F32):
        return es.enter_context(nc.sbuf_tensor("sb_" + name, list(shape), dt))

    h = [sb(f"h{s}", [P, D]) for s in range(NS)]
    hB = [Buf(f"h{s}") for s in range(NS)]
    nT = sb("nT", [P, KD, TT], BF16)
    nTB = Buf("nT")
    small = sb("small", [P, SMALL_COLS])
    smallB = Buf("small")
    consts = sb("consts", [P, CONST_COLS])
    constsB = Buf("consts")
    ident_bf = sb("ident_bf", [P, P], BF16)
    maskS_bf = sb("maskS_bf", [P, P], BF16)
    ones_bf = sb("ones_bf", [P, P], BF16)
    negm_bf = sb("negm_bf", [P, P], BF16)
    cbfB = Buf("cbf")
    NW = 4
    WSZ = 4096
    wslots = [sb(f"w{i}", [P, WSZ], BF16) for i in range(NW)]
    wB = [Buf(f"w{i}") for i in range(NW)]
    wctr = [0]
    ssq = sb("ssq", [P, 8])
    ssqB = Buf("ssq")
    rstd = sb("rstd", [P, 8])
    rstdB = Buf("rstd")
    NSG = 4
    sg = [sb(f"sg{i}", [P, TT]) for i in range(NSG)]
    sgB = [Buf(f"sg{i}") for i in range(NSG)]
    sgctr = [0]
    Sf = sb("Sf", [P, NH, HD])
    SfB = Buf("Sf")
    Sb = sb("Sb", [P, NH, HD], BF16)
    SbB = Buf("Sb")
    halo = sb("halo", [P, 24, 3])
    haloB = Buf("halo")
    negA = sb("negA", [P, 8])
    negAB = Buf("negA")
    graw = sb("graw", [P, NS, 8])
    grawB = Buf("graw")
    gtmp = sb("gtmp", [P, NS, 8])
    gtmpB = Buf("gtmp")
    beta = sb("beta", [P, NS, 8])
    betaB = Buf("beta")
    ek = sb("ek", [P, 16])
    ekB = [Buf("ek0"), Buf("ek1")]
    bk = sb("bk", [P, 8])
    bkB = [Buf("bk0"), Buf("bk1")]
    sdecp = sb("sdecp", [P, 16])
    sdecB = [Buf("sdec0"), Buf("sdec1")]
    oss = sb("oss", [P, 8])
    ossB = Buf("oss")
    ntot = sb("ntot", [P, 4])
    ntotB = Buf("ntot")

    ARENA_KB = 100
    arena = sb("arena", [P, ARENA_KB * KB // 4])

    def av(off_b, nbytes, dt, shape=None):
        a = arena[:, off_b // 4:(off_b + nbytes) // 4]
        if dt is BF16:
            a = a.bitcast(BF16)
        return a

    hid = av(0, 44 * KB, BF16).rearrange("p (f t) -> p f t", f=KF)
    hidB = [Buf(f"hid{f}") for f in range(KF)]
    hs = [av((88 + 4 * i) * KB, 4 * KB, BF16) for i in range(2)]
    hsB = [Buf(f"hs{i}") for i in range(2)]
    hsctr = [0]
    junk = av(96 * KB, 4 * KB, BF16)
    junkB = Buf("junk")
    omixT = av(0, 16 * KB, BF16).rearrange("p (c t) -> p c t", c=16)
    omixB = [Buf(f"omix{c}") for c in range(16)]
    zs = av(16 * KB, 8 * KB, BF16).rearrange("p (s c) -> p s c", s=NS)
    zsB = Buf("zs")
    finbc = av(0, 8 * KB, F32)
    finbcBs = tuple(hidB[0:8])
    pT = av(8 * KB, 1 * KB * 2, BF16).rearrange("p (k t) -> p k t", k=2)
    pTBs = tuple(hidB[8:10])
    ptok = av(10 * KB, 4 * KB, F32).rearrange("p (s c) -> p s c", s=NS)
    ptokBs = tuple(hidB[10:14])
    ptokb = av(14 * KB, 2 * KB, BF16).rearrange("p (s c) -> p s c", s=NS)
    ptokbBs = tuple(hidB[14:16])
    qTs = av(24 * KB, 8 * KB, BF16).rearrange("p (h t) -> p h t", h=NH)
    qTsB = [Buf(f"qTs{i}") for i in range(NH)]
    kstage = [av((32 + i) * KB, 1 * KB, BF16) for i in range(2)]
    kstageB = [Buf(f"kst{i}") for i in range(2)]
    vstage = [av((34 + 2 * i) * KB, 2 * KB, BF16) for i in range(2)]
    vstageB = [Buf(f"vst{i}") for i in range(2)]
    NPAR = 4
    et = [av((38 + 2 * i) * KB, 2 * KB, F32) for i in range(NPAR)]
    etB = [Buf(f"et{i}") for i in range(NPAR)]
    spt = [av((46 + 2 * i) * KB, 2 * KB, F32) for i in range(NPAR)]
    sptB = [Buf(f"spt{i}") for i in range(NPAR)]
    Pb = [av((54 + 3 * i) * KB, 3 * KB, F32) for i in range(NPAR)]
    PbB = [Buf(f"Pb{i}") for i in range(NPAR)]
    lb = [av((66 + 2 * i) * KB, 2 * KB, F32) for i in range(NPAR)]
    lbB = [Buf(f"lb{i}") for i in range(NPAR)]
    aa = [av((74 + i) * KB, 1 * KB, BF16) for i in range(NPAR)]
    aaB = [Buf(f"aa{i}") for i in range(NPAR)]
    aT = [av((78 + i) * KB, 1 * KB, BF16).rearrange("p (k q) -> p k q", k=4) for i in range(NPAR)]
    aTB = [Buf(f"aT{i}") for i in range(NPAR)]
    ktl = [av((82 + 4 * i) * KB, 4 * KB, BF16) for i in range(2)]
    ktlB = [Buf(f"ktl{i}") for i in range(2)]
    vtl = [av((90 + 4 * i) * KB, 4 * KB, BF16).rearrange("p (k d) -> p k d", k=16) for i in range(2)]
    vtlB = [Buf(f"vtl{i}") for i in range(2)]
    ntotB2 = [Buf(f"ntot{i}") for i in range(NPAR)]
    qTd = av(24 * KB, 8 * KB, BF16).rearrange("p (h t) -> p h t", h=NH)
    qTdB = [Buf(f"qTd{i}") for i in range(NH)]
    kTd = av(32 * KB, 8 * KB, BF16).rearrange("p (h t) -> p h t", h=NH)
    kTdB = [Buf(f"kTd{i}") for i in range(NH)]
    ktok = av(40 * KB, 8 * KB, BF16).rearrange("p (s h d) -> p s h d", s=NS, h=NH)
    ktokB = Buf("ktok")
    vtok = av(48 * KB, 8 * KB, BF16).rearrange("p (s h d) -> p s h d", s=NS, h=NH)
    vtokB = Buf("vtok")
    NSET = 6
    SETB = 7 * KB
    raw = [av(56 * KB + SETB * i, 2560, F32) for i in range(NSET)]
    rawB = [Buf(f"raw{i}") for i in range(NSET)]
    cacc = [av(56 * KB + SETB * i + 2560, 2 * KB, F32) for i in range(NSET)]
    caccB = [Buf(f"cacc{i}") for i in range(NSET)]
    csil = cacc
    csilB = caccB
    sqb = [av(56 * KB + SETB * i + 2560 + 2 * KB, 1 * KB, BF16) for i in range(NSET)]
    sqbB = [Buf(f"sqb{i}") for i in range(NSET)]
    rinv = [raw[i][:, 0:TT] for i in range(NSET)]
    rinvB = rawB
    G1 = av(56 * KB, 2 * KB, F32)
    G1B = [Buf("G10"), Buf("G11")]
    G2 = av(58 * KB, 2 * KB, F32)
    G2B = [Buf("G20"), Buf("G21")]
    E1 = av(60 * KB, 2 * KB, F32)
    E1B = [Buf("E10"), Buf("E11")]
    E2 = av(62 * KB, 2 * KB, F32)
    E2B = [Buf("E20"), Buf("E21")]
    Lc = av(64 * KB, 2 * KB, F32)
    LcB = [Buf("Lc0"), Buf("Lc1")]
    Am = av(66 * KB, 2 * KB, F32)
    Pm = av(68 * KB, 2 * KB, F32)
    AB_ = [Buf("A0"), Buf("A1")]
    PB_ = [Buf("Pm0"), Buf("Pm1")]
    ATi = av(70 * KB, 1 * KB, BF16)
    ATiB = [Buf("ATi0"), Buf("ATi1")]
    MT = av(71 * KB, 1 * KB, BF16)
    MTB = [Buf("MT0"), Buf("MT1")]
    Vb = av(72 * KB, 2 * KB, BF16)
    VbB = [Buf("Vb0"), Buf("Vb1")]
    Kbg = av(74 * KB, 2 * KB, BF16)
    KbgB = [Buf("Kbg0"), Buf("Kbg1")]
    Kd = av(76 * KB, 2 * KB, BF16)
    KdB = [Buf("Kd0"), Buf("Kd1")]
    Ut = av(78 * KB, 4 * KB, F32)
    UtB = [Buf("U0"), Buf("U1")]
    WT2 = [av(82 * KB, 1 * KB, BF16), av(99 * KB, 1 * KB, BF16)]
    WTB = [Buf("WT0"), Buf("WT1")]
    vn = av(83 * KB, 2 * KB, BF16)
    vnB = [Buf("vn0"), Buf("vn1")]
    otmp = av(85 * KB, 4 * KB, F32)
    otmpB = [Buf("otmp0"), Buf("otmp1")]
    osub = av(89 * KB, 4 * KB, F32)
    osubB = [Buf("osub0"), Buf("osub1")]
    ob = av(93 * KB, 2 * KB, BF16)
    obB = Buf("ob")
    sqo = av(95 * KB, 4 * KB, F32)
    sqoB = Buf("sqo")

    ATi_b = sb("ATi_b", [P, 512], BF16)
    ek_b = sb("ek_b", [P, 16])
    sdecp_b = sb("sdecp_b", [P, 16])
    ATi_h = [ATi, ATi_b[:]]
    Kd_h = [Kd, sg[2][:].bitcast(BF16)]
    ek_h = [ek, ek_b]
    sdecp_h = [sdecp, sdecp_b]
    Ut_h = [(Ut[:, 0:512], Ut[:, 512:1024]), (sg[0][:], sg[1][:])]
    _sg3 = sg[3][:].bitcast(BF16)
    WT_h = [[WT2[0], WT2[1]], [_sg3[:, 0:512], _sg3[:, 512:1024]]]
    WTBh = [[Buf("WT00"), Buf("WT01")], [Buf("WT10"), Buf("WT11")]]
    psum = [es.enter_context(nc.psum_tensor(f"ps{i}", [P, 512], F32)) for i in range(8)]
    psB = [Buf(f"ps{i}") for i in range(8)]
    psctr = [0]

    class Ring:
        def __init__(self, banks):
            self.banks = list(banks)
            self.c = 0

        def next(self):
            i = self.banks[self.c % len(self.banks)]
            self.c += 1
            return psum[i], psB[i]

    ring_all = Ring(range(8))
    cur_ring = [ring_all]

    def ps_next():
        return cur_ring[0].next()

    def run_interleaved(gens):
        gens = list(gens)
        while gens:
            for g in list(gens):
                try:
                    cur_ring[0] = g[1]
                    next(g[0])
                except StopIteration:
                    gens.remove(g)
        cur_ring[0] = ring_all

    def mm(out, lhsT, rhs, start, stop, reads, writes):
        S.add("pe", lambda e: e.matmul(out, lhsT, rhs, start=start, stop=stop), reads, writes)

    def tr(out, in_, ident, reads, writes):
        S.add("pe", lambda e: e.transpose(out, in_, ident), reads, writes)

    def act(out, in_, func, reads, writes, bias=None, scale=None, accum_out=None):
        kw = {}
        if bias is not None:
            kw["bias"] = bias
        if scale is not None:
            kw["scale"] = scale
        if accum_out is not None:
            kw["accum_out"] = accum_out
        S.add("act", lambda e: e.activation(out, in_, func, **kw), reads, writes)

    def tt(eng, out, in0, in1, op, reads, writes):
        S.add(eng, lambda e: e.tensor_tensor(out, in0, in1, op), reads, writes)

    def ts(eng, out, in0, s1, s2, op0, op1, reads, writes):
        if op1 is None:
            S.add(eng, lambda e: e.tensor_scalar(out, in0, s1, None, op0), reads, writes)
        else:
            S.add(eng, lambda e: e.tensor_scalar(out, in0, s1, s2, op0, op1), reads, writes)

    def stt(out, in0, scalar, in1, op0, op1, reads, writes):
        S.add("dve", lambda e: e.scalar_tensor_tensor(out, in0, scalar, in1, op0, op1), reads, writes)

    def cp(eng, out, in_, reads, writes):
        if eng == "act":
            S.add("act", lambda e: e.copy(out, in_), reads, writes)
        else:
            S.add(eng, lambda e: e.tensor_copy(out, in_), reads, writes)

    def barrier():
        lasts = []
        for e in ENGS:
            for o in reversed(S.q[e]):
                if o.dma is None:
                    lasts.append(o)
                    break
        seen = set()
        for e in ENGS:
            for o in reversed(S.q[e]):
                if o.dma and o.dma[0] not in seen:
                    seen.add(o.dma[0])
                    lasts.append(o)
        for e in ENGS:
            b = Op(e, lambda eng: eng.nop(), None)
            for x in lasts:
                if x.dma or x.eng != e:
                    x.needed = True
                    b.deps.append(x)
            S.q[e].append(b)

    wpre = {}

    def wprefetch(key, src, kc, ncols):
        wpre[key] = wload(src, kc, ncols)

    def wload(src, kc, ncols, key=None):
        if key is not None and key in wpre:
            return wpre.pop(key)
        i = wctr[0] % NW
        wctr[0] += 1
        t = wslots[i][:, 0:kc * ncols].rearrange("p (k c) -> p k c", k=kc)
        b = wB[i]
        step = 8
        for k0 in range(0, kc, step):
            k1 = min(kc, k0 + step)
            S.add("pool", lambda e, k0=k0, k1=k1: e.dma_start(out=t[:, k0:k1, :], in_=src[:, k0:k1, :]),
                  reads=(), writes=(b,), dma=f"w{i}")
        return t, b

    def bc(ap2, shape):
        return ap2.broadcast_to(list(shape))

    S.add("sp", lambda e: e.dma_start(out=small[:], in_=small_d), (), (smallB,), dma="small")
    S.add("sp", lambda e: e.dma_start(out=consts[:], in_=const_d), (), (constsB,), dma="consts")
    cp("dve", ident_bf[:], consts[:, C_IDENT:C_IDENT + P], (constsB,), (cbfB,))
    cp("dve", maskS_bf[:], consts[:, C_MASKS:C_MASKS + P], (constsB,), (cbfB,))
    cp("dve", ones_bf[:], consts[:, C_ONES:C_ONES + P], (constsB,), (cbfB,))
    cp("dve", negm_bf[:], consts[:, C_NEGM:C_NEGM + P], (constsB,), (cbfB,))
    S.add("pool", lambda e: e.memset(Sf[:], 0.0), (), (SfB,))
    S.add("pool", lambda e: e.memset(Sb[:], 0.0), (), (SbB,))
    S.add("pool", lambda e: e.memset(halo[:], 0.0), (), (haloB,))
    act(negA[:], small[:, SM_ALOG:SM_ALOG + 8], AF.Exp, (smallB,), (negAB,))
    ts("dve", negA[:], negA[:], -1.0, None, ALU.mult, None, (negAB,), (negAB,))
    ident_f = consts[:, C_IDENT:C_IDENT + P]
    ones_f = consts[:, C_ONES:C_ONES + P]
    maskS_f = consts[:, C_MASKS:C_MASKS + P]
    zcol = consts[:, C_ZERO:C_ZERO + 1]

    def gamma(idx):
        return small[:, SM_GAMMA + idx * KD: SM_GAMMA + (idx + 1) * KD]

    def row_rstd(n):
        ts("dve", rstd[:, 0:n], ssq[:, 0:n], 1.0 / D, RMS_EPS, ALU.mult, ALU.add, (ssqB,), (rstdB,))
        act(rstd[:, 0:n], rstd[:, 0:n], AF.Sqrt, (rstdB,), (rstdB,))
        S.add("dve", lambda e: e.reciprocal(rstd[:, 0:n], rstd[:, 0:n]), (rstdB,), (rstdB,))

    def sum_squares():
        for s in range(NS):
            if s % 2 == 0:
                act(junk, h[s][:], AF.Square, (hB[s],), (junkB, ssqB), accum_out=ssq[:, s:s + 1])
            else:
                S.add("dve", lambda e, s=s: e.scalar_tensor_tensor(hs[1], h[s][:], 1.0, h[s][:], ALU.mult, ALU.mult,
                                                                  accum_out=ssq[:, s:s + 1]),
                      (hB[s],), (hsB[1], ssqB))

    def rmsnorm_to_nT(gidx):
        g = gamma(gidx)
        sum_squares()
        row_rstd(NS)
        for s in range(NS):
            i = hsctr[0] % 2
            hsctr[0] += 1
            if s % 2 == 0:
                ts("dve", hs[i], h[s][:], rstd[:, s:s + 1], None, ALU.mult, None, (hB[s], rstdB), (hsB[i],))
            else:
                act(hs[i], h[s][:], AF.Copy, (hB[s], rstdB), (hsB[i],), scale=rstd[:, s:s + 1])
            for jg in range(KD // 4):
                pt, pb = ps_next()
                pv = pt[:].bitcast(BF16)
                for jj in range(4):
                    j = jg * 4 + jj
                    tr(pv[:, jj * P:(jj + 1) * P], hs[i][:, j * P:(j + 1) * P], ident_bf[:], (hsB[i], cbfB), (pb,))
                gb = bc(g[:, jg * 4:(jg + 1) * 4].unsqueeze(2), [P, 4, P])
                tt("dve", nT[:, jg * 4:(jg + 1) * 4, s * P:(s + 1) * P],
                   pv[:, 0:4 * P].rearrange("p (a b) -> p a b", a=4), gb, ALU.mult, (pb, smallB), (nTB,))

    def ffn_prefetch(wgu, ffn_id):
        wgu_v = wgu.rearrange("(k p) c -> p k c", p=P)
        for coff in (0, F):
            for kh in range(2):
                wprefetch(("gu", ffn_id, coff, 0, kh), wgu_v[:, kh * 8:(kh + 1) * 8, coff:coff + 512], 8, 512)

    def feat_prefetch(c0s):
        for c0 in c0s:
            wprefetch(("feat", c0), winf[feat_blk(c0)].rearrange("p (k c) -> p k c", k=KD), KD, 256)

    def ffn(wgu, wd, ffn_id=0):
        wgu_v = wgu.rearrange("(k p) c -> p k c", p=P)
        wd_v = wd.rearrange("(k p) c -> p k c", p=P)
        for fg in range(F // 512):
            sgi = []
            for part, coff in ((0, 0), (1, F)):
                pss_ = [ps_next() for _ in range(4)]
                for kh in range(2):
                    wt, wtb = wload(wgu_v[:, kh * 8:(kh + 1) * 8, coff + fg * 512:coff + (fg + 1) * 512], 8, 512,
                                    key=("gu", ffn_id, coff, fg, kh))
                    for kk in range(8):
                        k = kh * 8 + kk
                        for fl in range(4):
                            mm(pss_[fl][0][:], wt[:, kk, fl * P:(fl + 1) * P], nT[:, k, :], k == 0, k == KD - 1,
                               (wtb, nTB), (pss_[fl][1],))
                for fl in range(4):
                    if part == 0:
                        i = sgctr[0] % NSG
                        sgctr[0] += 1
                        sgi.append(i)
                        act(sg[i][:], pss_[fl][0][:], AF.Silu, (pss_[fl][1],), (sgB[i],))
                    else:
                        i = sgi[fl]
                        tt("dve", hid[:, fg * 4 + fl, :], sg[i][:], pss_[fl][0][:], ALU.mult,
                           (sgB[i], pss_[fl][1]), (hidB[fg * 4 + fl],))
        for cb in range(D // 512):
            pss = [ps_next() for _ in range(NS)]
            f0 = 0
            while f0 < KF:
                kq = min(8, KF - f0)
                wt, wtb = wload(wd_v[:, f0:f0 + kq, cb * 512:(cb + 1) * 512], kq, 512)
                for fl in range(kq):
                    f = f0 + fl
                    for s in range(NS):
                        mm(pss[s][0][:], hid[:, f, s * P:(s + 1) * P], wt[:, fl, :], f == 0, f == KF - 1,
                           (hidB[f], wtb), (pss[s][1],))
                f0 += kq
            for s in range(NS):
                hv = h[s][:, cb * 512:(cb + 1) * 512]
                stt(hv, pss[s][0][:], 0.5, hv, ALU.mult, ALU.add, (pss[s][1], hB[s]), (hB[s],))

    def proj_tok(wv, nk, c0, lhs_fn, lhs_bufs, wkey=None):
        pss = [ps_next() for _ in range(NS)]
        k0 = 0
        while k0 < nk:
            kq = min(8, nk - k0)
            wt, wtb = wload(wv[:, k0:k0 + kq, c0:c0 + 512], kq, 512, key=(wkey, c0, k0) if wkey else None)
            for kk in range(kq):
                k = k0 + kk
                for s in range(NS):
                    mm(pss[s][0][:], lhs_fn(k, s), wt[:, kk, :], k == 0, k == nk - 1,
                       tuple(lhs_bufs(k)) + (wtb,), (pss[s][1],))
            k0 += kq
        return pss

    win_v = win.rearrange("(k p) c -> p k c", p=P)

    def feat_blk(c0):
        if c0 < 3072:
            return c0 // 256
        return 12 + (c0 - 4112) // 256

    def proj_feat(c0, emit):
        wt, wtb = wload(winf[feat_blk(c0)].rearrange("p (k c) -> p k c", k=KD), KD, 256, key=("feat", c0))
        for cl in range(2):
            pt, pb = ps_next()
            for k in range(KD):
                mm(pt[:], wt[:, k, cl * P:(cl + 1) * P], nT[:, k, :], k == 0, k == KD - 1, (wtb, nTB), (pb,))
            emit(cl, pt, pb)

    SB_SCALE = float(HD) ** -0.5
    Q_OFF, K_OFF, V_OFF, Z_OFF, A_OFF = 0, 1024, 2048, 3072, 4096
    QS_OFF, KS_OFF, VS_OFF = 4112, 5136, 6160

    def sb_phase(it):
        t0 = it * TT
        for blk in range(4):
            def emit_q(cl, pt, pb, blk=blk):
                hh = blk * 2 + cl
                cp("act", qTs[:, hh, :], pt[:], (pb,), (qTsB[hh],))
            proj_feat(QS_OFF + blk * 256, emit_q)
        for blk in range(4):
            def emit_k(cl, pt, pb, blk=blk):
                hh = blk * 2 + cl
                i = hh % 2
                cp("act", kstage[i], pt[:], (pb,), (kstageB[i],))
                S.add("sp", lambda e: e.dma_start(out=ksb_d[hh, :, t0:t0 + TT], in_=kstage[i]),
                      (kstageB[i],), (ksbB[hh],), dma=f"kst{i}")
            proj_feat(KS_OFF + blk * 256, emit_k)
        for cb in range(2):
            pss = proj_tok(win_v, KD, VS_OFF + cb * 512, lambda k, s: nT[:, k, s * P:(s + 1) * P], lambda k: (nTB,))
            for s in range(NS):
                i = s % 2
                cp("act", vstage[i][:, 0:512], pss[s][0][:], (pss[s][1],), (vstageB[i],))
                S.add("sp", lambda e, s=s, i=i, cb=cb: e.dma_start(
                    out=vsb_d[t0 + s * P:t0 + (s + 1) * P, cb * 512:(cb + 1) * 512], in_=vstage[i][:, 0:512]),
                    (vstageB[i],), (vsbB,), dma=f"vst{i}")
        nkb_tot = 4 * (it + 1)
        if stages == "b2":
            return
        for par in range(NPAR):
            S.add("pool", lambda e, par=par: e.memset(Pb[par][:, 0:1], 0.0), (), (PbB[par],))
        rings = [Ring([par]) for par in range(NPAR)]

        def sb_iter(hh, qs, par, kv):
            qb = 4 * it + qs
            nk = qb + 1
            nkeys = nk * P
            po, pob = psum[4 + par], psB[4 + par]
            tiles = []
            kt0 = 0
            while kt0 < nkeys:
                nkt = min(512, nkeys - kt0)
                tiles.append((kt0, nkt))
                kt0 += nkt
            ntl = len(tiles)
            for ti, (kt0, nkt) in enumerate(reversed(tiles)):
                diag = (kt0 + nkt == nkeys)
                pz, pzb = ps_next()
                mm(pz[:, 0:nkt], qTs[:, hh, qs * P:(qs + 1) * P], ktl[kv][:, kt0:kt0 + nkt], True, not diag,
                   (qTsB[hh], ktlB[kv]), (pzb,))
                if diag:
                    mm(pz[:, nkt - P:nkt], ident_bf[:], negm_bf[:], False, True, (cbfB,), (pzb,))
                act(et[par][:, 0:nkt], pz[:, 0:nkt], AF.Exp, (pzb,), (etB[par],), scale=SB_SCALE)
                act(spt[par][:, 0:nkt], et[par][:, 0:nkt], AF.Ln, (etB[par],), (sptB[par],), bias=1.0)
                yield
                S.add("dve", lambda e, par=par, nkt=nkt: e.tensor_tensor_scan(
                    Pb[par][:, 1:1 + nkt], spt[par][:, 0:nkt], bc(zcol, [P, nkt]),
                    Pb[par][:, 0:1], ALU.add, ALU.add), (sptB[par], PbB[par], constsB), (PbB[par],))
                if ti == 0:
                    ts("dve", ntot[:, par:par + 1], Pb[par][:, nkt:nkt + 1], -1.0, None, ALU.mult, None,
                       (PbB[par],), (ntotB2[par],))
                else:
                    stt(ntot[:, par:par + 1], Pb[par][:, nkt:nkt + 1], -1.0, ntot[:, par:par + 1], ALU.mult, ALU.add,
                        (PbB[par], ntotB2[par]), (ntotB2[par],))
                stt(lb[par][:, 0:nkt], pz[:, 0:nkt], SB_SCALE, Pb[par][:, 0:nkt], ALU.mult, ALU.add,
                    (pzb, PbB[par]), (lbB[par],))
                yield
                act(aa[par][:, 0:nkt], lb[par][:, 0:nkt], AF.Exp, (lbB[par], ntotB2[par]), (aaB[par],),
                    bias=ntot[:, par:par + 1])
                yield
                n = nkt // P
                pt, pb = ps_next()
                pv = pt[:].bitcast(BF16)
                for j in range(n):
                    tr(pv[:, j * P:(j + 1) * P], aa[par][:, j * P:(j + 1) * P], ident_bf[:], (aaB[par], cbfB), (pb,))
                cp("act" if ti % 2 == 0 else "dve", aT[par][:, 0:n, :],
                   pv[:, 0:n * P].rearrange("p (a b) -> p a b", a=n), (pb,), (aTB[par],))
                yield
                for j in range(n):
                    kb = kt0 // P + j
                    mm(po[:, 0:P], vtl[kv][:, kb, :], aT[par][:, j, :], ti == 0 and j == 0,
                       ti == ntl - 1 and j == n - 1, (vtlB[kv], aTB[par]), (pob,))
                yield
            cp("act", omixT[:, 8 + hh, qs * P:(qs + 1) * P], po[:, 0:P], (pob,), (omixB[8 + hh],))

        def load_kv(hh):
            kv = hh % 2
            S.add("sp", lambda e: e.dma_start(out=ktl[kv][:, 0:nkb_tot * P], in_=ksb_d[hh, :, 0:nkb_tot * P]),
                  (ksbB[hh],), (ktlB[kv],), dma=f"ktl{kv}")
            S.add("sp", lambda e: e.dma_start(
                out=vtl[kv][:, 0:nkb_tot, :],
                in_=vsb_d[0:nkb_tot * P, hh * HD:(hh + 1) * HD].rearrange("(kb p) d -> p kb d", p=P)),
                (vsbB,), (vtlB[kv],), dma=f"vtl{kv}")

        if stages == "b3":
            for hh in range(NH):
                load_kv(hh)
            return
        todo = [(hh, qs) for hh in range(NH) for qs in range(NS)]
        active = {}
        free_slots = list(range(NPAR))
        loaded = set()
        remaining = {hh: NS for hh in range(NH)}
        while todo or active:
            while todo and free_slots:
                hh, qs = todo[0]
                if hh >= 2 and remaining[hh - 2] > 0:
                    break
                todo.pop(0)
                if hh not in loaded:
                    load_kv(hh)
                    loaded.add(hh)
                par = free_slots.pop(0)
                active[par] = (sb_iter(hh, qs, par, hh % 2), hh)
            for par in sorted(active):
                cur_ring[0] = rings[par]
                try:
                    next(active[par][0])
                except StopIteration:
                    remaining[active[par][1]] -= 1
                    del active[par]
                    free_slots.append(par)
        cur_ring[0] = ring_all

    def dn_inproj():
        wcache = {}
        wpending = [(grp, goff, blk) for grp, goff in ((0, Q_OFF), (1, K_OFF), (2, V_OFF)) for blk in range(4)]

        def wissue(n):
            for _ in range(n):
                if wpending:
                    grp, goff, blk = wpending.pop(0)
                    wcache[(grp, blk)] = wload(winf[feat_blk(goff + blk * 256)].rearrange("p (k c) -> p k c", k=KD), KD, 256,
                                               key=("feat", goff + blk * 256))

        wissue(NW - 1)

        def qkv_chunk(grp, goff, hh, r):
            blk, cl = hh // 2, hh % 2
            if cl == 0:
                wissue(1)
            while (grp, blk) not in wcache:
                wissue(1)
            wt, wtb = wcache[(grp, blk)]
            cidx = grp * 8 + hh
            cw = small[:, SM_CONV + cidx * 4: SM_CONV + cidx * 4 + 4]
            pt, pb = ps_next()
            for k in range(KD):
                mm(pt[:], wt[:, k, cl * P:(cl + 1) * P], nT[:, k, :], k == 0, k == KD - 1, (wtb, nTB), (pb,))
            cp("dve", raw[r][:, 0:3], halo[:, cidx, :], (haloB,), (rawB[r],))
            cp("act", raw[r][:, 3:3 + TT], pt[:], (pb,), (rawB[r],))
            cp("dve", halo[:, cidx, :], raw[r][:, TT:TT + 3], (rawB[r],), (haloB,))
            yield
            ts("dve", cacc[r], raw[r][:, 0:TT], cw[:, 0:1], None, ALU.mult, None, (rawB[r], smallB), (caccB[r],))
            for k in range(1, 4):
                stt(cacc[r], raw[r][:, k:k + TT], cw[:, k:k + 1], cacc[r], ALU.mult, ALU.add,
                    (rawB[r], smallB, caccB[r]), (caccB[r],))
            yield
            if grp == 2:
                act(sqb[r], cacc[r], AF.Silu, (caccB[r],), (sqbB[r],))
                yield
                pt2, pb2 = ps_next()
                pv = pt2[:].bitcast(BF16)
                for s in range(NS):
                    tr(pv[:, s * P:(s + 1) * P], sqb[r][:, s * P:(s + 1) * P], ident_bf[:], (sqbB[r], cbfB), (pb2,))
                cp("dve", vtok[:, :, hh, :], pv[:, 0:TT].rearrange("p (s d) -> p s d", s=NS), (pb2,), (vtokB,))
                return
            act(csil[r], cacc[r], AF.Silu, (caccB[r],), (csilB[r],))
            act(sqb[r], csil[r], AF.Square, (csilB[r],), (sqbB[r],))
            yield
            pt2, pb2 = ps_next()
            mm(pt2[:], ones_bf[:], sqb[r], True, True, (cbfB, sqbB[r]), (pb2,))
            act(rinv[r], pt2[:], AF.Sqrt, (pb2,), (rinvB[r],), bias=L2_EPS)
            yield
            S.add("dve", lambda e: e.reciprocal(rinv[r], rinv[r]), (rinvB[r],), (rinvB[r],))
            if grp == 0:
                stt(qTd[:, hh, :], csil[r], float(HD) ** -0.5, rinv[r], ALU.mult, ALU.mult,
                    (csilB[r], rinvB[r]), (qTdB[hh],))
            else:
                tt("dve", kTd[:, hh, :], csil[r], rinv[r], ALU.mult, (csilB[r], rinvB[r]), (kTdB[hh],))
                yield
                pt3, pb3 = ps_next()
                pv = pt3[:].bitcast(BF16)
                for s in range(NS):
                    tr(pv[:, s * P:(s + 1) * P], kTd[:, hh, s * P:(s + 1) * P], ident_bf[:], (kTdB[hh], cbfB), (pb3,))
                cp("dve", ktok[:, :, hh, :], pv[:, 0:TT].rearrange("p (s d) -> p s d", s=NS), (pb3,), (ktokB,))

        todo = [(grp, goff, hh) for grp, goff in ((0, Q_OFF), (1, K_OFF), (2, V_OFF)) for hh in range(NH)]
        ringsQ = [Ring([r]) for r in range(NSET)]
        active = {}
        free_sets = list(range(NSET))
        while todo or active:
            while todo and free_sets:
                r = free_sets.pop(0)
                grp, goff, hh = todo.pop(0)
                active[r] = qkv_chunk(grp, goff, hh, r)
            for r in sorted(active):
                cur_ring[0] = ringsQ[r]
                try:
                    next(active[r])
                except StopIteration:
                    del active[r]
                    free_sets.append(r)
        cur_ring[0] = ring_all
        for cb in range(2):
            pss = proj_tok(win_v, KD, Z_OFF + cb * 512, lambda k, s: nT[:, k, s * P:(s + 1) * P], lambda k: (nTB,))
            for s in range(NS):
                act(zs[:, s, cb * 512:(cb + 1) * 512], pss[s][0][:], AF.Silu, (pss[s][1],), (zsB,))
        wt, wtb = wload(winab.rearrange("p (k c) -> p k c", k=KD), KD, 16)
        pab, pabb = ps_next()
        for s in range(NS):
            for k in range(KD):
                mm(pab[:, s * 16:(s + 1) * 16], nT[:, k, s * P:(s + 1) * P], wt[:, k, :], k == 0, k == KD - 1,
                   (nTB, wtb), (pabb,))
        pab3 = pab[:, 0:NS * 16].rearrange("p (s c) -> p s c", s=NS)
        tt("dve", gtmp[:], pab3[:, :, 0:8], bc(small[:, SM_DTB:SM_DTB + 8].unsqueeze(1), [P, NS, 8]), ALU.add,
           (pabb, smallB), (gtmpB,))
        act(gtmp[:], gtmp[:], AF.Exp, (gtmpB,), (gtmpB,))
        act(gtmp[:], gtmp[:], AF.Ln, (gtmpB,), (gtmpB,), bias=1.0)
        tt("dve", graw[:], gtmp[:], bc(negA[:].unsqueeze(1), [P, NS, 8]), ALU.mult, (gtmpB, negAB), (grawB,))
        act(beta[:], pab3[:, :, 8:16], AF.Sigmoid, (pabb,), (betaB,))

    dlev = int(stages[2:]) if stages.startswith("dc") else 99

    def dn_chunks():
        TRIKI = consts[:, C_TRIKI:C_TRIKI + CH]
        UPS = consts[:, C_UPS:C_UPS + CH]
        NEGTRI = consts[:, C_NEGTRI:C_NEGTRI + CH]
        NMLS = consts[:, C_NMLS:C_NMLS + CH]
        PMUI = consts[:, C_PMUI:C_PMUI + CH]
        I2 = consts[:, C_I2:C_I2 + CH]
        HALVES = ((0, slice(0, CH)), (1, slice(CH, P)))

        def f3(ap):
            return ap.rearrange("p (h j) -> p h j", h=NH)

        def hc_(hh):
            return slice(hh * CH, (hh + 1) * CH)

        def wy(pr):
            s = pr
            hb = pr % 2
            ATi, Kd, ek, sdecp = ATi_h[hb], Kd_h[hb], ek_h[hb], sdecp_h[hb]
            Ut0, Ut1 = Ut_h[hb]
            WTs = WT_h[hb]
            g_s = graw[:, s, :]
            b_s = beta[:, s, :]
            yield
            cp("dve", f3(G1), bc(g_s.unsqueeze(2), [P, NH, CH]), (grawB,), (G1B[0],))
            tt("dve", f3(G2), bc(g_s.unsqueeze(2), [P, NH, CH]), bc(NEGTRI.unsqueeze(1), [P, NH, CH]),
               ALU.mult, (grawB, constsB), (G2B[0],))
            pd, pdb = ps_next()
            pm, pmb = ps_next()
            for hf, sl in HALVES:
                mm(pd[sl, :], TRIKI[sl, :], G1[sl, :], True, False, (constsB, G1B[0]), (pdb,))
                mm(pd[sl, :], ones_f[sl, 0:CH], G2[sl, :], False, True, (constsB, G2B[0]), (pdb,))
            for hf, sl in HALVES:
                mm(pm[sl, 0:8], TRIKI[sl, :], graw[sl, s, :], True, True, (constsB, grawB), (pmb,))
                mm(pm[sl, 8:16], UPS[sl, :], graw[sl, s, :], True, True, (constsB, grawB), (pmb,))
                mm(pm[:, 16 + 8 * hf:24 + 8 * hf], ones_f[sl, :], graw[sl, s, :], True, True, (constsB, grawB), (pmb,))
            act(ek[:, :], pm[:, 0:16], AF.Exp, (pmb,), (ekB[hb],))
            act(sdecp[:, :], pm[:, 16:32], AF.Exp, (pmb,), (sdecB[hb],))
            stt(f3(E1), f3(pd[:]), 0.0, bc(NMLS.unsqueeze(1), [P, NH, CH]), ALU.min, ALU.add, (pdb, constsB), (E1B[0],))
            act(E1, E1, AF.Exp, (E1B[0],), (E1B[0],))
            stt(f3(E2), f3(pd[:]), 0.0, bc(PMUI.unsqueeze(1), [P, NH, CH]), ALU.max, ALU.add, (pdb, constsB), (E2B[0],))
            act(E2, E2, AF.Exp, (E2B[0],), (E2B[0],), scale=-1.0)
            yield
            pk, pkb = ps_next()
            pq, pqb = ps_next()
            for hh in range(NH):
                for hf, sl in HALVES:
                    cs_ = slice((2 * pr + hf) * CH, (2 * pr + hf + 1) * CH)
                    mm(pk[sl, hc_(hh)], kTd[:, hh, cs_], kTd[:, hh, cs_], True, True, (kTdB[hh],), (pkb,))
            for hh in range(NH):
                for hf, sl in HALVES:
                    cs_ = slice((2 * pr + hf) * CH, (2 * pr + hf + 1) * CH)
                    mm(pq[sl, hc_(hh)], kTd[:, hh, cs_], qTd[:, hh, cs_], True, True, (kTdB[hh], qTdB[hh]), (pqb,))
            tt("dve", Lc, pk[:], E1, ALU.mult, (pkb, E1B[0]), (LcB[0],))
            tt("dve", f3(Lc), f3(Lc), bc(b_s.unsqueeze(2), [P, NH, CH]), ALU.mult, (LcB[0], betaB), (LcB[0],))
            tt("dve", ATi, pq[:], E2, ALU.mult, (pqb, E2B[0]), (ATiB[hb],))
            yield
            pa, pab_ = ps_next()
            for hh in range(NH):
                for hf, sl in HALVES:
                    mm(pa[sl, hc_(hh)], Lc[sl, hc_(hh)], I2[sl, :], True, True, (LcB[0], constsB), (pab_,))
            cp("dve", Am, pa[:], (pab_,), (AB_[0],))
            stt(f3(Pm), f3(Am), -1.0, bc(I2.unsqueeze(1), [P, NH, CH]), ALU.mult, ALU.add, (AB_[0], constsB), (PB_[0],))
            yield
            p1, p1b = ps_next()
            p2, p2b = ps_next()
            for hh in range(NH):
                for hf, sl in HALVES:
                    mm(p1[sl, hc_(hh)], Lc[sl, hc_(hh)], Am[sl, hc_(hh)], True, True, (LcB[0], AB_[0]), (p1b,))
            for hh in range(NH):
                for hf, sl in HALVES:
                    mm(p2[sl, hc_(hh)], Am[sl, hc_(hh)], Lc[sl, hc_(hh)], True, True, (LcB[0], AB_[0]), (p2b,))
            cp("dve", Am, p1[:], (p1b,), (AB_[0],))
            cp("dve", Lc, p2[:], (p2b,), (LcB[0],))
            yield
            for lvl in range(1, 5):
                pA, pAb = ps_next()
                pP, pPb = ps_next()
                pl, plb = ps_next()
                for hh in range(NH):
                    for hf, sl in HALVES:
                        mm(pA[sl, hc_(hh)], Lc[sl, hc_(hh)], Am[sl, hc_(hh)], True, True, (LcB[0], AB_[0]), (pAb,))
                for hh in range(NH):
                    for hf, sl in HALVES:
                        mm(pP[sl, hc_(hh)], Lc[sl, hc_(hh)], Pm[sl, hc_(hh)], True, True, (LcB[0], PB_[0]), (pPb,))
                for hh in range(NH):
                    for hf, sl in HALVES:
                        mm(pl[sl, hc_(hh)], Am[sl, hc_(hh)], Lc[sl, hc_(hh)], True, True, (LcB[0], AB_[0]), (plb,))
                tt("dve", Pm, pP[:], Pm, ALU.add, (pPb, PB_[0]), (PB_[0],))
                cp("dve", Am, pA[:], (pAb,), (AB_[0],))
                cp("dve", Lc, pl[:], (plb,), (LcB[0],))
                yield
            yield
            p5, p5b = ps_next()
            for hh in range(NH):
                for hf, sl in HALVES:
                    mm(p5[sl, hc_(hh)], Lc[sl, hc_(hh)], Pm[sl, hc_(hh)], True, True, (LcB[0], PB_[0]), (p5b,))
            tt("dve", MT, p5[:], Pm, ALU.add, (p5b, PB_[0]), (MTB[0],))
            yield
            h3 = lambda ap: ap.rearrange("p (h d) -> p h d", h=NH)
            tt("dve", h3(Vb), vtok[:, s, :, :], bc(b_s.unsqueeze(2), [P, NH, HD]), ALU.mult, (vtokB, betaB), (VbB[0],))
            tt("dve", bk[:, :], b_s, ek[:, 0:8], ALU.mult, (betaB, ekB[hb]), (bkB[0],))
            tt("dve", h3(Kbg), ktok[:, s, :, :], bc(bk[:, :].unsqueeze(2), [P, NH, HD]), ALU.mult, (ktokB, bkB[0]), (KbgB[0],))
            tt("dve", h3(Kd), ktok[:, s, :, :], bc(ek[:, 8:16].unsqueeze(2), [P, NH, HD]), ALU.mult, (ktokB, ekB[hb]), (KdB[hb],))
            yield
            pua, puab = ps_next()
            pub_, pubb = ps_next()
            pws = [ps_next(), ps_next()]
            for hh in range(NH):
                px, pxB_ = (pua, puab) if hh < 4 else (pub_, pubb)
                o0 = (hh % 4) * HD
                for hf, sl in HALVES:
                    mm(px[sl, o0:o0 + HD], MT[sl, hc_(hh)], Vb[sl, hh * HD:(hh + 1) * HD], True, True,
                       (MTB[0], VbB[0]), (pxB_,))
            for hh in range(NH):
                for hf, sl in HALVES:
                    mm(pws[hf][0][:, hc_(hh)], Kbg[sl, hh * HD:(hh + 1) * HD], MT[sl, hc_(hh)], True, True,
                       (KbgB[0], MTB[0]), (pws[hf][1],))
            cp("dve", Ut0, pua[:], (puab,), (UtB[hb],))
            cp("dve", Ut1, pub_[:], (pubb,), (UtB[hb],))
            for hf, sl in HALVES:
                cp("act", WTs[hf], pws[hf][0][:], (pws[hf][1],), (WTBh[hb][hf],))
        def seq(pr):
            s = pr
            hb = pr % 2
            ATi, Kd, ek, sdecp = ATi_h[hb], Kd_h[hb], ek_h[hb], sdecp_h[hb]
            Ut0, Ut1 = Ut_h[hb]
            WTs = WT_h[hb]
            for hf, sl in HALVES:
                c = 2 * pr + hf
                cs_ = slice(c * CH, (c + 1) * CH)
                WT = WTs[hf]
                pwa, pwab = ps_next()
                pwb2, pwbb2 = ps_next()
                for hh in range(NH):
                    px, pxB_ = (pwa, pwab) if hh < 4 else (pwb2, pwbb2)
                    o0 = (hh % 4) * HD
                    mm(px[sl, o0:o0 + HD], WT[:, hc_(hh)], Sb[:, hh, :], True, True, (WTBh[hb][hf], SbB), (pxB_,))
                tt("dve", vn[sl, 0:512], Ut0[sl, :], pwa[sl, :], ALU.subtract, (UtB[hb], pwab), (vnB[hf],))
                tt("dve", vn[sl, 512:1024], Ut1[sl, :], pwb2[sl, :], ALU.subtract, (UtB[hb], pwbb2), (vnB[hf],))
                yield
                pqa, pqab = ps_next()
                pqb2, pqbb2 = ps_next()
                for hh in range(NH):
                    px, pxB_ = (pqa, pqab) if hh < 4 else (pqb2, pqbb2)
                    o0 = (hh % 4) * HD
                    mm(px[sl, o0:o0 + HD], qTd[:, hh, cs_], Sb[:, hh, :], True, True, (qTdB[hh], SbB), (pxB_,))
                pva, pvab = ps_next()
                pvb2, pvbb2 = ps_next()
                for hh in range(NH):
                    px, pxB_ = (pva, pvab) if hh < 4 else (pvb2, pvbb2)
                    o0 = (hh % 4) * HD
                    mm(px[sl, o0:o0 + HD], ATi[sl, hc_(hh)], vn[sl, hh * HD:(hh + 1) * HD], True, True,
                       (ATiB[hb], vnB[hf]), (pxB_,))
                for half, (pqx, pqxb, pvx, pvxb) in enumerate(((pqa, pqab, pva, pvab), (pqb2, pqbb2, pvb2, pvbb2))):
                    cols = slice(half * 512, half * 512 + 512)
                    eg = bc(ek[sl, half * 4:half * 4 + 4].unsqueeze(2), [CH, 4, HD])
                    tt("dve", otmp[sl, cols].rearrange("p (h d) -> p h d", h=4),
                       pqx[sl, :].rearrange("p (h d) -> p h d", h=4), eg, ALU.mult, (pqxb, ekB[hb]), (otmpB[hf],))
                    tt("dve", osub[sl, cols], otmp[sl, cols], pvx[sl, :], ALU.add, (otmpB[hf], pvxb), (osubB[hf],))
                yield
                psa, psab = ps_next()
                psb2, psbb2 = ps_next()
                for hh in range(NH):
                    px, pxB_ = (psa, psab) if hh < 4 else (psb2, psbb2)
                    o0 = (hh % 4) * HD
                    mm(px[:, o0:o0 + HD], Kd[sl, hh * HD:(hh + 1) * HD], vn[sl, hh * HD:(hh + 1) * HD], True, True,
                       (KdB[hb], vnB[hf]), (pxB_,))
                tt("pool", Sf[:], Sf[:], bc(sdecp[:, 8 * hf:8 * hf + 8].unsqueeze(2), [P, NH, HD]), ALU.mult,
                   (SfB, sdecB[hb]), (SfB,))
                Sf2 = Sf[:].rearrange("p h d -> p (h d)")
                tt("dve", Sf2[:, 0:512], Sf2[:, 0:512], psa[:], ALU.add, (SfB, psab), (SfB,))
                tt("dve", Sf2[:, 512:1024], Sf2[:, 512:1024], psb2[:], ALU.add, (SfB, psbb2), (SfB,))
                cp("act", Sb[:], Sf[:], (SfB,), (SbB,))
                yield
            o3 = osub.rearrange("p (h d) -> p h d", h=NH)
            tt("dve", sqo, osub, osub, ALU.mult, (osubB[0], osubB[1]), (sqoB,))
            S.add("dve", lambda e: e.tensor_reduce(oss[:], sqo.rearrange("p (h d) -> p h d", h=NH), AX.X, ALU.add),
                  (sqoB,), (ossB,))
            ts("dve", oss[:], oss[:], 1.0 / HD, RMS_EPS, ALU.mult, ALU.add, (ossB,), (ossB,))
            act(oss[:], oss[:], AF.Sqrt, (ossB,), (ossB,))
            S.add("dve", lambda e: e.reciprocal(oss[:], oss[:]), (ossB,), (ossB,))
            tt("dve", o3, o3, bc(oss[:].unsqueeze(2), [P, NH, HD]), ALU.mult, (osubB[0], osubB[1], ossB), (osubB[0], osubB[1]))
            tt("dve", o3, o3, bc(small[:, SM_ONORM:SM_ONORM + HD].unsqueeze(1), [P, NH, HD]), ALU.mult,
               (osubB[0], osubB[1], smallB), (osubB[0], osubB[1]))
            tt("dve", ob, osub, zs[:, s, :], ALU.mult, (osubB[0], osubB[1], zsB), (obB,))
            for g4 in range(2):
                pt, pb = ps_next()
                pv = pt[:].bitcast(BF16)
                for j in range(4):
                    hh = g4 * 4 + j
                    tr(pv[:, j * P:(j + 1) * P], ob[:, hh * HD:(hh + 1) * HD], ident_bf[:], (obB, cbfB), (pb,))
                for j in range(4):
                    hh = g4 * 4 + j
                    cp("act", omixT[:, hh, s * P:(s + 1) * P], pv[:, j * P:(j + 1) * P], (pb,), (omixB[hh],))

        ringW, ringS = Ring([0, 1, 2, 3]), Ring([4, 5, 6, 7])
        NPR = TT // CH // 2
        cur_ring[0] = ringW
        for _ in wy(0):
            pass
        for pr in range(NPR):
            gens = [(seq(pr), ringS)]
            if pr + 1 < NPR:
                gens.append((wy(pr + 1), ringW))
            run_interleaved(gens)
        cur_ring[0] = ring_all

    def out_proj_prefetch():
        wv = wout.rearrange("(k p) c -> p k c", p=P)
        for cb in range(2):
            for k0 in (0, 8):
                wprefetch(("wout", cb * 512, k0), wv[:, k0:k0 + 8, cb * 512:cb * 512 + 512], 8, 512)

    def out_proj():
        wv = wout.rearrange("(k p) c -> p k c", p=P)
        for cb in range(D // 512):
            pss = proj_tok(wv, KD, cb * 512, lambda k, s: omixT[:, k, s * P:(s + 1) * P], lambda k: (omixB[k],),
                           wkey="wout")
            for s in range(NS):
                hv = h[s][:, cb * 512:(cb + 1) * 512]
                tt("dve", hv, pss[s][0][:], hv, ALU.add, (pss[s][1], hB[s]), (hB[s],))

    def ple(it):
        t0 = it * TT
        S.add("sp", lambda e: e.dma_start(out=ptok[:], in_=p_d[t0:t0 + TT, :].rearrange("(s p) c -> p s c", p=P)),
              (), ptokBs, dma="ptok")
        cp("dve", ptokb[:], ptok[:], ptokBs, ptokbBs)
        for s in range(NS):
            pt, pb = ps_next()
            pv = pt[:].bitcast(BF16)
            for k in range(2):
                tr(pv[:, k * P:(k + 1) * P], ptokb[:, s, k * P:(k + 1) * P], ident_bf[:], ptokbBs + (cbfB,), (pb,))
            cp("dve", pT[:, :, s * P:(s + 1) * P], pv[:, 0:2 * P].rearrange("p (k t) -> p k t", k=2), (pb,), pTBs)
        wgv = wpg.rearrange("(k p) c -> p k c", p=P)
        wpv = wpp.rearrange("(k p) c -> p k c", p=P)
        for cb in range(D // 512):
            psg = proj_tok(wgv, KD, cb * 512, lambda k, s: nT[:, k, s * P:(s + 1) * P], lambda k: (nTB,))
            psp = proj_tok(wpv, 2, cb * 512, lambda k, s: pT[:, k, s * P:(s + 1) * P], lambda k: pTBs)
            for s in range(NS):
                i = sgctr[0] % NSG
                sgctr[0] += 1
                act(sg[i][:], psg[s][0][:], AF.Sigmoid, (psg[s][1],), (sgB[i],))
                tt("dve", sg[i][:], sg[i][:], psp[s][0][:], ALU.mult, (sgB[i], psp[s][1]), (sgB[i],))
                hv = h[s][:, cb * 512:(cb + 1) * 512]
                tt("dve", hv, hv, sg[i][:], ALU.add, (hB[s], sgB[i]), (hB[s],))

    def final_norm_store(it):
        t0 = it * TT
        S.add("sp", lambda e: e.dma_start(out=finbc, in_=fin_d), (), finbcBs, dma="finbc")
        sum_squares()
        row_rstd(NS)
        for s in range(NS):
            stt(h[s][:], h[s][:], rstd[:, s:s + 1], finbc, ALU.mult, ALU.mult, (hB[s], rstdB) + finbcBs, (hB[s],))
            S.add("sp", lambda e, s=s: e.dma_start(out=out_d[t0 + s * P:t0 + (s + 1) * P, :], in_=h[s][:]),
                  (hB[s],), (), dma=f"o{s}")

    for it in range(NT):
        t0 = it * TT
        for s in range(NS):
            S.add("sp", lambda e, s=s, t0=t0: e.dma_start(out=h[s][:], in_=x_d[t0 + s * P:t0 + (s + 1) * P, :]),
                  (), (hB[s],), dma=f"x{s}")
        rmsnorm_to_nT(0)
        ffn(w1gu, w1d)
        if stages == "ffn1":
            for s in range(NS):
                S.add("sp", lambda e, s=s, t0=t0: e.dma_start(out=out_d[t0 + s * P:t0 + (s + 1) * P, :], in_=h[s][:]),
                      (hB[s],), (), dma=f"o{s}")
            continue
        if stages == "all":
            feat_prefetch([QS_OFF + blk * 256 for blk in range(4)])
        barrier()
        rmsnorm_to_nT(1)
        if stages != "b1":
            sb_phase(it)
        if stages == "all":
            feat_prefetch([Q_OFF + blk * 256 for blk in range(3)])
        barrier()
        if stages in ("b1", "b2", "b3", "b4"):
            for s in range(NS):
                S.add("sp", lambda e, s=s, t0=t0: e.dma_start(out=out_d[t0 + s * P:t0 + (s + 1) * P, :], in_=h[s][:]),
                      (hB[s],), (), dma=f"o{s}")
            continue
        if stages in ("nodn", "nodnmix"):
            for c in range(8):
                S.add("pool", lambda e, c=c: e.memset(omixT[:, c, :], 0.0), (), (omixB[c],))
        else:
            dn_inproj()
            barrier()
            if stages == "all":
                out_proj_prefetch()
            if dlev > 0:
                dn_chunks()
            if dlev < 99:
                barrier()
                for c in range(8):
                    S.add("pool", lambda e, c=c: e.memset(omixT[:, c, :], 0.0), (), (omixB[c],))
        out_proj()
        if stages == "all":
            ffn_prefetch(w2gu, 2)
        barrier()
        if stages in ("mix", "nodnmix") or dlev < 99:
            for s in range(NS):
                S.add("sp", lambda e, s=s, t0=t0: e.dma_start(out=out_d[t0 + s * P:t0 + (s + 1) * P, :], in_=h[s][:]),
                      (hB[s],), (), dma=f"o{s}")
            continue
        rmsnorm_to_nT(2)
        ffn(w2gu, w2d, 2)
        rmsnorm_to_nT(3)
        ple(it)
        final_norm_store(it)

    last = {}
    for o in S.q["sp"]:
        if o.dma and o.dma[0].startswith("o"):
            last[o.dma[0]] = o
    endop = Op("sp", lambda e: e.nop(), None)
    for o in last.values():
        endop.deps.append(o)
    S.q["sp"].append(endop)

    with nc.Block() as block:
        S.emit(nc, es, block)
    es.close()
    return nc


def make_small(inp):
    sm = np.zeros((P, SMALL_COLS), np.float32)
    for i, k in enumerate(("ffn1_norm", "mix_norm", "ffn2_norm", "ple_norm")):
        sm[:, SM_GAMMA + i * KD: SM_GAMMA + (i + 1) * KD] = np.asarray(inp[k], np.float32).reshape(KD, P).T
    cw = np.asarray(inp["dn_conv"], np.float32).reshape(4, 24, P)
    sm[:, SM_CONV:SM_CONV + 96] = cw.transpose(2, 1, 0).reshape(P, 96)
    sm[:, SM_ALOG:SM_ALOG + 8] = np.asarray(inp["dn_a_log"], np.float32).reshape(1, 8)
    sm[:, SM_DTB:SM_DTB + 8] = np.asarray(inp["dn_dt_bias"], np.float32).reshape(1, 8)
    sm[:, SM_ONORM:SM_ONORM + 128] = np.asarray(inp["dn_out_norm"], np.float32).reshape(1, 128)
    return sm


def make_consts():
    c = np.zeros((P, CONST_COLS), np.float32)
    c[:, C_IDENT:C_IDENT + P] = np.eye(P, dtype=np.float32)
    q = np.arange(P)[:, None]
    k = np.arange(P)[None, :]
    c[:, C_MASKS:C_MASKS + P] = (k < q)
    c[:, C_ONES:C_ONES + P] = 1.0
    pm = (np.arange(P) % CH)[:, None]
    j = np.arange(CH)[None, :]
    c[:, C_TRIKI:C_TRIKI + CH] = (pm <= j)
    c[:, C_UPS:C_UPS + CH] = (pm > j)
    c[:, C_NEGTRI:C_NEGTRI + CH] = -(pm <= j).astype(np.float32)
    c[:, C_NMLS:C_NMLS + CH] = np.where(pm > j, 0.0, NEG)
    c[:, C_PMUI:C_PMUI + CH] = np.where(j >= pm, 0.0, -NEG)
    c[:, C_I2:C_I2 + CH] = (pm == j)
    c[:, C_NEGM:C_NEGM + P] = np.where(k < q, 0.0, NEGB)
    return c


_NC_CACHE = {}


def run(inputs, T=2048, ncores=8, stages="all"):
    import os
    key = (T, stages)
    if key not in _NC_CACHE:
        _NC_CACHE[key] = build(T, stages)
    nc = _NC_CACHE[key]
    sm = make_small(inputs)
    cs = make_consts()
    fin_bc = np.ascontiguousarray(np.broadcast_to(np.asarray(inputs["final_norm"], np.float32).reshape(1, D), (P, D)))
    shared = {
        "ffn1_w_gu": np.ascontiguousarray(inputs["ffn1_w_gu"][0]),
        "ffn1_w_down": np.ascontiguousarray(inputs["ffn1_w_down"][0]),
        "w_in": np.ascontiguousarray(inputs["w_in"][0]),
        "w_out": np.ascontiguousarray(inputs["w_out"][0]),
        "ffn2_w_gu": np.ascontiguousarray(inputs["ffn2_w_gu"][0]),
        "ffn2_w_down": np.ascontiguousarray(inputs["ffn2_w_down"][0]),
        "ple_w_gate": np.ascontiguousarray(inputs["ple_w_gate"][0]),
        "ple_w_proj": np.ascontiguousarray(inputs["ple_w_proj"][0]),
        "small": sm, "consts": cs, "final_bc": fin_bc,
    }
    w_in0 = np.asarray(inputs["w_in"][0], np.float32)
    cols = np.concatenate([np.arange(0, 3072), np.arange(4112, 6160)])
    wf = w_in0[:, cols].reshape(KD, P, 20, 256).transpose(2, 1, 0, 3)
    shared["w_in_feat"] = np.ascontiguousarray(wf).reshape(20, P, KD * 256)
    wab = w_in0[:, 4096:4112].reshape(KD, P, 16).transpose(1, 0, 2)
    shared["w_in_ab"] = np.ascontiguousarray(wab).reshape(P, KD * 16)
    in_maps = []
    for c in range(ncores):
        m = dict(shared)
        m["x"] = np.ascontiguousarray(inputs["x"][c, :T])
        m["p"] = np.ascontiguousarray(inputs["p"][0, c, :T])
        in_maps.append(m)
    trace = bool(os.environ.get("K_TRACE"))
    res = run_bass_kernel_spmd(nc, in_maps, core_ids=list(range(ncores)), trace=trace)
    if trace:
        print("exec_time_ns", res.exec_time_ns)
    return np.stack([np.asarray(r["out"]) for r in res.results], axis=0)


def kernel(**inputs):
    inputs = {k: np.asarray(v) for k, v in inputs.items()}
    return run(inputs, T=2048, ncores=8).astype(np.float32)
```
